# Optimizing a Trainium2 kernel written in Bass

```python
import jax, jax.numpy as jnp
from jax import lax
import numpy as np

D_MODEL = 2048
BATCH = 8
SEQ = 2048
DEPTH = 1
DEC_BATCH = 128
DEC_SEQ = 4
PAST_LEN = 2048
PAGE_SIZE = 128

H_A = 8
HD_A = 128
W_A = H_A * HD_A
H_B = 4
DK_B = 128
DV_B = 256
WK_B = H_B * DK_B
WV_B = H_B * DV_B
GK_RANK = 16
GK_NORM = 16.0
GLA_CHUNK = 64
MIX = W_A + WV_B
IN_COLS = 3 * W_A + H_A + 2 * WK_B + 2 * WV_B + GK_RANK
Q_BLOCK = 128
D_FF = 4 * D_MODEL
PLE_DIM = 256
EPS = 1e-6
NEG = -1e30
POOL_NUM = 5
POOL_DEN = 4

kernel_name = "hymba_fox_gla_decode_step"


def rmsnorm(x, g):
    xf = x.astype(jnp.float32)
    y = xf * lax.rsqrt(jnp.mean(xf * xf, axis=-1, keepdims=True) + EPS)
    return (y * g.astype(jnp.float32)).astype(x.dtype)


def split_proj(h, w_in, b_f, w_gk2, b_gk):
    B, T, _ = h.shape
    z = h @ w_in
    sizes = (W_A, W_A, W_A, H_A, WK_B, WK_B, WV_B, WV_B, GK_RANK)
    idx = []
    acc = 0
    for s in sizes[:-1]:
        acc += s
        idx.append(acc)
    q_a, k_a, v_a, f_a, q_b, k_b, v_b, r_b, g_lr = jnp.split(z, idx, axis=-1)
    q_a = q_a.reshape(B, T, H_A, HD_A)
    k_a = k_a.reshape(B, T, H_A, HD_A)
    v_a = v_a.reshape(B, T, H_A, HD_A)
    logf = jax.nn.log_sigmoid((f_a + b_f).astype(jnp.float32))
    q_b = q_b.reshape(B, T, H_B, DK_B) * (DK_B ** -0.5)
    k_b = k_b.reshape(B, T, H_B, DK_B)
    v_b = v_b.reshape(B, T, H_B, DV_B)
    r_b = r_b.reshape(B, T, H_B, DV_B)
    log_a = jax.nn.log_sigmoid((g_lr @ w_gk2 + b_gk).astype(jnp.float32)) / GK_NORM
    log_a = log_a.reshape(B, T, H_B, DK_B)
    return q_a, k_a, v_a, logf, q_b, k_b, v_b, r_b, log_a


def fox_block(q, c_q, q_pos, k, v, c_k, k_pos):
    s = jnp.einsum('bqhd,bkhd->bhqk', q, k).astype(jnp.float32) * (HD_A ** -0.5)
    s = s + jnp.transpose(c_q, (0, 2, 1))[:, :, :, None] - jnp.transpose(c_k, (0, 2, 1))[:, :, None, :]
    mask = k_pos[None, :] <= q_pos[:, None]
    s = jnp.where(mask[None, None], s, NEG)
    p = jax.nn.softmax(s, axis=-1).astype(v.dtype)
    return jnp.einsum('bhqk,bkhd->bqhd', p, v)


def fox_prompt(q, k, v, logf):
    B, T, H, D = q.shape
    c = jnp.cumsum(logf, axis=1)
    nb = T // Q_BLOCK
    qb = jnp.moveaxis(q.reshape(B, nb, Q_BLOCK, H, D), 1, 0)
    cb = jnp.moveaxis(c.reshape(B, nb, Q_BLOCK, H), 1, 0)
    k_pos = jnp.arange(T)

    def one(args):
        qi, ci, i = args
        q_pos = i * Q_BLOCK + jnp.arange(Q_BLOCK)
        return fox_block(qi, ci, q_pos, k, v, c, k_pos)

    o = lax.map(one, (qb, cb, jnp.arange(nb)))
    return jnp.moveaxis(o, 0, 1).reshape(B, T, H, D)


def gla_chunked(q, k, v, log_a, S0):
    B, T, H, DK = q.shape
    DV = v.shape[-1]
    chunk = GLA_CHUNK if T % GLA_CHUNK == 0 else T
    n = T // chunk

    def to_chunks(a):
        return jnp.moveaxis(a.astype(jnp.float32).reshape((B, n, chunk) + a.shape[2:]), 1, 0)

    causal = jnp.tril(jnp.ones((chunk, chunk), dtype=bool))

    def step(S, inp):
        qc, kc, vc, ac = inp
        b = jnp.cumsum(ac, axis=1)
        b_last = b[:, -1]
        qe = qc * jnp.exp(b)
        ke = kc * jnp.exp(-b)
        kd = kc * jnp.exp(b_last[:, None] - b)
        o_inter = jnp.einsum('bchk,bhkv->bchv', qe, S)
        A = jnp.einsum('bchk,bshk->bhcs', qe, ke)
        A = jnp.where(causal[None, None], A, 0.0)
        o_intra = jnp.einsum('bhcs,bshv->bchv', A, vc)
        S_new = S * jnp.exp(b_last)[..., None] + jnp.einsum('bchk,bchv->bhkv', kd, vc)
        return S_new, o_inter + o_intra

    S_T, o = lax.scan(step, S0.astype(jnp.float32), (to_chunks(q), to_chunks(k), to_chunks(v), to_chunks(log_a)))
    o = jnp.moveaxis(o, 0, 1).reshape(B, T, H, DV).astype(v.dtype)
    return S_T, o


def mix_out(o_a, o_b, r_b, g_gla_out, w_out):
    B, T = o_a.shape[:2]
    o_b = rmsnorm(o_b, g_gla_out) * jax.nn.silu(r_b)
    o = jnp.concatenate([o_a.reshape(B, T, W_A), o_b.reshape(B, T, WV_B)], axis=-1)
    return o @ w_out


def channel_and_ple(x, p, g_mlp, w_up, w_down, w_ple, g_ple, g_ple_gate, w_ple_gate):
    h = rmsnorm(x, g_mlp)
    x = x + jnp.square(jax.nn.relu(h @ w_up)) @ w_down
    e = rmsnorm(p @ w_ple, g_ple)
    gate = jax.nn.sigmoid(rmsnorm(x, g_ple_gate) @ w_ple_gate)
    return x + gate * e


def setup_inputs(seed: int = 0) -> dict:
    key = jax.random.key(seed)
    ks = jax.random.split(key, 32)
    n_pages = PAST_LEN // PAGE_SIZE
    n_pool = (DEC_BATCH * n_pages * POOL_NUM) // POOL_DEN

    def nrm(k, shape, scale=1.0):
        return jax.random.normal(k, shape, jnp.float32) * scale

    page_table = jax.random.permutation(ks[6], n_pool)[: DEC_BATCH * n_pages]
    page_table = page_table.reshape(DEC_BATCH, n_pages).astype(jnp.int32)
    return {
        "x_prompt": nrm(ks[0], (BATCH, SEQ, D_MODEL)),
        "x_sample": nrm(ks[1], (DEC_BATCH, DEC_SEQ, D_MODEL)),
        "cache_k": nrm(ks[2], (DEPTH, n_pool, PAGE_SIZE, H_A, HD_A)),
        "cache_v": nrm(ks[3], (DEPTH, n_pool, PAGE_SIZE, H_A, HD_A)),
        "cache_logf": jax.nn.log_sigmoid(2.0 + nrm(ks[4], (DEPTH, n_pool, PAGE_SIZE, H_A))),
        "state_gla": nrm(ks[5], (DEPTH, DEC_BATCH, H_B, DK_B, DV_B), 0.5),
        "page_table": page_table,
        "p_prompt": nrm(ks[7], (DEPTH, BATCH, SEQ, PLE_DIM)),
        "p_sample": nrm(ks[8], (DEPTH, DEC_BATCH, DEC_SEQ, PLE_DIM)),
        "g_mix": 1.0 + nrm(ks[9], (DEPTH, D_MODEL), 0.02),
        "w_in": nrm(ks[10], (DEPTH, D_MODEL, IN_COLS), D_MODEL ** -0.5),
        "b_f": 2.0 + nrm(ks[11], (DEPTH, H_A), 0.1),
        "w_gk2": nrm(ks[12], (DEPTH, GK_RANK, WK_B), GK_RANK ** -0.5),
        "b_gk": nrm(ks[13], (DEPTH, WK_B), 0.1),
        "g_gla_out": 1.0 + nrm(ks[14], (DEPTH, DV_B), 0.02),
        "w_out": nrm(ks[15], (DEPTH, MIX, D_MODEL), MIX ** -0.5),
        "g_mlp": 1.0 + nrm(ks[16], (DEPTH, D_MODEL), 0.02),
        "w_up": nrm(ks[17], (DEPTH, D_MODEL, D_FF), D_MODEL ** -0.5),
        "w_down": nrm(ks[18], (DEPTH, D_FF, D_MODEL), D_FF ** -0.5),
        "w_ple": nrm(ks[19], (DEPTH, PLE_DIM, D_MODEL), PLE_DIM ** -0.5),
        "g_ple": 1.0 + nrm(ks[20], (DEPTH, D_MODEL), 0.02),
        "g_ple_gate": 1.0 + nrm(ks[21], (DEPTH, D_MODEL), 0.02),
        "w_ple_gate": nrm(ks[22], (DEPTH, D_MODEL, D_MODEL), D_MODEL ** -0.5),
        "g_final": 1.0 + nrm(ks[23], (D_MODEL,), 0.02),
    }


def reference(x_prompt, x_sample, cache_k, cache_v, cache_logf, state_gla, page_table,
              p_prompt, p_sample, g_mix, w_in, b_f, w_gk2, b_gk, g_gla_out, w_out,
              g_mlp, w_up, w_down, w_ple, g_ple, g_ple_gate, w_ple_gate, g_final):
    n_pages = page_table.shape[1]
    past = n_pages * PAGE_SIZE
    db, t_s = x_sample.shape[:2]
    xp = x_prompt
    xs = x_sample
    kp_l, vp_l, fp_l, sp_l = [], [], [], []
    ks_l, vs_l, fs_l, ss_l = [], [], [], []
    for l in range(DEPTH):
        h = rmsnorm(xp, g_mix[l])
        q_a, k_a, v_a, logf, q_b, k_b, v_b, r_b, log_a = split_proj(h, w_in[l], b_f[l], w_gk2[l], b_gk[l])
        o_a = fox_prompt(q_a, k_a, v_a, logf)
        S0 = jnp.zeros((xp.shape[0], H_B, DK_B, DV_B), jnp.float32)
        S_p, o_b = gla_chunked(q_b, k_b, v_b, log_a, S0)
        xp = xp + mix_out(o_a, o_b, r_b, g_gla_out[l], w_out[l])
        xp = channel_and_ple(xp, p_prompt[l], g_mlp[l], w_up[l], w_down[l], w_ple[l], g_ple[l], g_ple_gate[l], w_ple_gate[l])
        kp_l.append(k_a)
        vp_l.append(v_a)
        fp_l.append(logf)
        sp_l.append(S_p)

        h = rmsnorm(xs, g_mix[l])
        q_a, k_a, v_a, logf, q_b, k_b, v_b, r_b, log_a = split_proj(h, w_in[l], b_f[l], w_gk2[l], b_gk[l])
        k_past = cache_k[l][page_table].reshape(db, past, H_A, HD_A)
        v_past = cache_v[l][page_table].reshape(db, past, H_A, HD_A)
        f_past = cache_logf[l][page_table].reshape(db, past, H_A).astype(jnp.float32)
        k_all = jnp.concatenate([k_past, k_a.astype(k_past.dtype)], axis=1)
        v_all = jnp.concatenate([v_past, v_a.astype(v_past.dtype)], axis=1)
        c_all = jnp.cumsum(jnp.concatenate([f_past, logf], axis=1), axis=1)
        q_pos = past + jnp.arange(t_s)
        k_pos = jnp.arange(past + t_s)
        o_a = fox_block(q_a, c_all[:, past:], q_pos, k_all, v_all, c_all, k_pos).astype(xs.dtype)
        S_s, o_b = gla_chunked(q_b, k_b, v_b, log_a, state_gla[l])
        xs = xs + mix_out(o_a, o_b, r_b, g_gla_out[l], w_out[l])
        xs = channel_and_ple(xs, p_sample[l], g_mlp[l], w_up[l], w_down[l], w_ple[l], g_ple[l], g_ple_gate[l], w_ple_gate[l])
        ks_l.append(k_a)
        vs_l.append(v_a)
        fs_l.append(logf)
        ss_l.append(S_s)

    y_prompt = rmsnorm(xp, g_final)
    y_sample = rmsnorm(xs, g_final)
    return (y_prompt, y_sample,
            jnp.stack(kp_l), jnp.stack(vp_l), jnp.stack(fp_l), jnp.stack(sp_l),
            jnp.stack(ks_l), jnp.stack(vs_l), jnp.stack(fs_l), jnp.stack(ss_l))
```

```python
import contextlib
import numpy as np
import concourse.bass as bass
import concourse.mybir as mybir
from concourse.bass_utils import run_bass_kernel_spmd

F32 = mybir.dt.float32
BF16 = mybir.dt.bfloat16
I32 = mybir.dt.int32
AF = mybir.ActivationFunctionType
ALU = mybir.AluOpType

RAW, WAW, WAR = 0, 1, 2
COMPUTE = ("pe", "act", "dve", "pool")
NDMASEM = 8


class Op:
    __slots__ = ("id", "eng", "fn", "deps", "is_dma", "sig", "need_sig", "dsem", "dtarget", "prev_dma")

    def __init__(self, id, eng, fn, is_dma):
        self.id = id
        self.eng = eng
        self.fn = fn
        self.deps = {}
        self.is_dma = is_dma
        self.sig = None
        self.need_sig = False
        self.dsem = None
        self.dtarget = None
        self.prev_dma = None


class Prog:
    def __init__(self, nc):
        self.nc = nc
        self.ops = []
        self.last_w = {}
        self.readers = {}
        self.dma_count = {}
        self.dma_hist = {}

    def add(self, eng, fn, reads=(), writes=(), dma=False):
        op = Op(len(self.ops), eng, fn, dma)
        deps = op.deps
        for k in reads:
            w = self.last_w.get(k)
            if w is not None:
                deps[w] = RAW
        for k in writes:
            w = self.last_w.get(k)
            if w is not None and w not in deps:
                deps[w] = WAW
            for r in self.readers.get(k, ()):
                if r not in deps:
                    deps[r] = WAR
        for k in reads:
            self.readers.setdefault(k, []).append(op.id)
        for k in writes:
            self.last_w[k] = op.id
            self.readers[k] = []
        if dma:
            n = self.dma_count.get(eng, 0)
            self.dma_count[eng] = n + 1
            op.dsem = n % NDMASEM
            op.dtarget = 16 * (n // NDMASEM + 1)
            hist = self.dma_hist.setdefault(eng, [])
            if n >= NDMASEM:
                op.prev_dma = hist[n - NDMASEM]
            hist.append(op.id)
        self.ops.append(op)
        return op.id

    def emit(self):
        nc = self.nc
        ops = self.ops
        for y in ops:
            nd = {}
            for xid, kind in y.deps.items():
                x = ops[xid]
                if x.is_dma:
                    nd[xid] = kind
                    continue
                if x.eng == y.eng and not y.is_dma and (kind == WAR or (kind == WAW and x.eng == "pe")):
                    continue
                nd[xid] = kind
                x.need_sig = True
            y.deps = nd
        cnt = {e: 0 for e in COMPUTE}
        for x in ops:
            if x.need_sig:
                cnt[x.eng] += 1
                x.sig = cnt[x.eng]
        engs = ["pe", "act", "dve", "pool", "sp"]
        per = {e: [o for o in ops if o.eng == e] for e in engs}
        with contextlib.ExitStack() as ctx:
            esem = {e: ctx.enter_context(nc.semaphore("s_" + e)) for e in COMPUTE}
            dsem = {}
            for e in self.dma_count:
                dsem[e] = [ctx.enter_context(nc.semaphore("d_%s%d" % (e, i))) for i in range(NDMASEM)]
            block = ctx.enter_context(nc.Block())

            def run(e, eng):
                waited = {}

                def wait_tok(key, sem, val):
                    if waited.get(key, 0) >= val:
                        return
                    waited[key] = val
                    eng.wait_ge(sem, val)

                for y in per[e]:
                    for xid in sorted(y.deps):
                        x = ops[xid]
                        if x.is_dma:
                            wait_tok(("d", x.eng, x.dsem), dsem[x.eng][x.dsem], x.dtarget)
                        else:
                            wait_tok(("e", x.eng), esem[x.eng], x.sig)
                    if y.is_dma:
                        if y.prev_dma is not None:
                            p = ops[y.prev_dma]
                            wait_tok(("d", p.eng, p.dsem), dsem[p.eng][p.dsem], p.dtarget)
                        y.fn(eng).then_inc(dsem[e][y.dsem], 16)
                    else:
                        ins = y.fn(eng)
                        if y.need_sig:
                            ins.then_inc(esem[e], 1)
                if e == "sp":
                    for q, n in self.dma_count.items():
                        for i in range(NDMASEM):
                            k = (n - i + NDMASEM - 1) // NDMASEM
                            if k > 0:
                                eng.wait_ge(dsem[q][i], 16 * k)

            @block.tensor
            def _(eng):
                run("pe", eng)

            @block.scalar
            def _(eng):
                run("act", eng)

            @block.vector
            def _(eng):
                run("dve", eng)

            @block.gpsimd
            def _(eng):
                run("pool", eng)

            @block.sync
            def _(eng):
                run("sp", eng)


D = 2048
NKC = 16
H_A = 8
H_B = 4
DFF = 8192
EPS = 1e-6
SC_A = 128 ** -0.5
N_POOL_ROWS = 2560 * 128
C_QA, C_KA, C_VA, C_FA, C_QB, C_KB, C_VB, C_RB, C_GL = 0, 1024, 2048, 3072, 3080, 3592, 4104, 5128, 6152
IN_COLS = 6168
STOP = [99]


def bc_last(ap, n):
    return bass.AP(ap.tensor, ap.offset, [list(x) for x in ap.ap] + [[0, n]])


def bc_mid(ap, n):
    a = [list(x) for x in ap.ap]
    return bass.AP(ap.tensor, ap.offset, [a[0], [0, n]] + a[1:])


def build(nc, with_sample=True, n_ptiles=16):
    dt = nc.dram_tensor
    xp = dt("xp", [2048, D], F32, kind="ExternalInput").ap()
    xs = dt("xs", [64, D], F32, kind="ExternalInput").ap()
    ppi = dt("pp", [2048, 256], F32, kind="ExternalInput").ap()
    psi = dt("ps", [64, 256], F32, kind="ExternalInput").ap()
    ck = dt("ck", [N_POOL_ROWS, 1024], F32, kind="ExternalInput").ap()
    cv = dt("cv", [N_POOL_ROWS, 1024], F32, kind="ExternalInput").ap()
    clf = dt("clf", [N_POOL_ROWS, 8], F32, kind="ExternalInput").ap()
    sgl = dt("sgl", [16, 4, 128, 256], F32, kind="ExternalInput").ap()
    ptab = dt("ptab", [1, 256], I32, kind="ExternalInput").ap()
    g_mix = dt("g_mix", [D], F32, kind="ExternalInput").ap()
    w_in = dt("w_in", [D, IN_COLS], F32, kind="ExternalInput").ap()
    b_f = dt("b_f", [1, 8], F32, kind="ExternalInput").ap()
    w_gk2 = dt("w_gk2", [16, 512], F32, kind="ExternalInput").ap()
    b_gk = dt("b_gk", [512], F32, kind="ExternalInput").ap()
    g_gla = dt("g_gla", [256], F32, kind="ExternalInput").ap()
    w_out = dt("w_out", [D, D], F32, kind="ExternalInput").ap()
    g_mlp = dt("g_mlp", [D], F32, kind="ExternalInput").ap()
    w_up = dt("w_up", [D, DFF], F32, kind="ExternalInput").ap()
    w_down = dt("w_down", [DFF, D], F32, kind="ExternalInput").ap()
    w_ple = dt("w_ple", [256, D], F32, kind="ExternalInput").ap()
    g_ple = dt("g_ple", [1, D], F32, kind="ExternalInput").ap()
    g_pg = dt("g_pg", [D], F32, kind="ExternalInput").ap()
    w_pg = dt("w_pg", [D, D], F32, kind="ExternalInput").ap()
    g_fin = dt("g_fin", [1, D], F32, kind="ExternalInput").ap()

    yp = dt("yp", [2048, D], F32, kind="ExternalOutput").ap()
    ys = dt("ys", [64, D], F32, kind="ExternalOutput").ap()
    kpo = dt("kpo", [2048, 1024], F32, kind="ExternalOutput").ap()
    vpo = dt("vpo", [2048, 1024], F32, kind="ExternalOutput").ap()
    fpo = dt("fpo", [2048, 8], F32, kind="ExternalOutput").ap()
    spo = dt("spo", [4, 128, 256], F32, kind="ExternalOutput").ap()
    kso = dt("kso", [64, 1024], F32, kind="ExternalOutput").ap()
    vso = dt("vso", [64, 1024], F32, kind="ExternalOutput").ap()
    fso = dt("fso", [64, 8], F32, kind="ExternalOutput").ap()
    sso = dt("sso", [16, 4, 128, 256], F32, kind="ExternalOutput").ap()

    ctx = contextlib.ExitStack()
    with ctx:
        def sb(name, shape, dtype):
            return ctx.enter_context(nc.sbuf_tensor(name, shape, dtype))

        P = Prog(nc)
        A = P.add

        KT = sb("KT", [128, 8, 2048], BF16)
        Vt = sb("Vt", [128, 16, 1024], BF16)
        x_sb = sb("x_sb", [128, D], F32)
        hT = sb("hT", [128, NKC, 128], BF16)
        mixT = sb("mixT", [128, NKC, 128], BF16)
        hidT = sb("hidT", [128, 64, 128], BF16)
        wsl = [sb("wsl%d" % i, [128, NKC, 512], BF16) for i in range(2)]
        gple_bc = sb("gple_bc", [128, D], F32)
        gfin_bc = sb("gfin_bc", [128, D], F32)
        e_sb = sb("e_sb", [128, D], F32)
        hn = sb("hn", [128, D], BF16)
        tmpfall = sb("tmpfall", [128, 4, 512], F32)
        tmpf = [tmpfall[:, i, :] for i in range(4)]
        PTb = [sb("PT%d" % i, [128, 512], BF16) for i in range(2)]
        Sst = sb("Sst", [128, 4, 256], F32)
        Sbf = sb("Sbf", [128, 4, 256], BF16)
        cK = sb("cK", [128, 16, 8], F32)
        ctot = sb("ctot", [128, 8], F32)
        biasT = sb("biasT", [128, 16, 8], F32)
        lf = sb("lf", [128, 8], F32)
        lfx = sb("lfx", [128, 8], F32)
        smallf = sb("smallf", [128, 8], F32)
        ssq = sb("ssq", [128, 8], F32)
        rstd = sb("rstd", [128, 8], F32)
        ident = sb("ident", [128, 128], BF16)
        ones_bf = sb("ones_bf", [128, 128], BF16)
        zeros_bf = sb("zeros_bf", [128, 512], BF16)
        caus_bf = sb("caus_bf", [128, 128], BF16)
        caus_f = sb("caus_f", [128, 128], F32)
        ones_f = sb("ones_f", [128, 128], F32)
        lstr_f = sb("lstr_f", [128, 128], F32)
        m128 = sb("m128", [128, 128], F32)
        m4 = sb("m4", [128, 16, 4], F32)
        gains = sb("gains", [128, 3, 16], F32)
        ggla = sb("ggla", [128, 2], F32)
        nbgk = sb("nbgk", [128, 4], F32)
        bfb = sb("bfb", [128, 8], F32)
        wsm = sb("wsm", [128, NKC, 24], BF16)
        wgk = sb("wgk", [16, 512], BF16)
        qT = sb("qT", [128, 8, 128], BF16)
        qeT = sb("qeT", [128, 4, 128], BF16)
        keT = sb("keT", [128, 4, 128], BF16)
        kdT = sb("kdT", [128, 4, 128], BF16)
        kdTok = sb("kdTok", [128, 4, 128], BF16)
        vb = sb("vb", [128, 1024], BF16)
        srT = sb("srT", [128, 8, 128], BF16)
        glT = sb("glT", [16, 128], BF16)
        pT = sb("pT", [128, 2, 128], BF16)
        p_tm = sb("p_tm", [128, 256], BF16)
        eb = sb("eb", [128, 4, 128], F32)
        enb = sb("enb", [128, 4, 128], F32)
        ATm = [sb("ATm%d" % i, [128, 128], BF16) for i in range(2)]
        idx = sb("idx", [128, 256], I32)
        idxf = sb("idxf", [128, 256], F32)
        iot = sb("iot", [128, 1], F32)
        Emat = sb("Emat", [16, 64], BF16)
        blk4_bf = sb("blk4_bf", [64, 64], BF16)
        blk4_f = sb("blk4_f", [64, 64], F32)
        selT = sb("selT", [64, 16], F32)
        LFp = [sb("LFp%d" % i, [128, 16, 8], F32) for i in range(2)]
        bias_s = sb("bias_s", [128, 16, 8], F32)
        Vflat = Vt[:, :, :].rearrange("p a b -> p (a b)")
        KTflat = KT[:, :, :].rearrange("p a b -> p (a b)")

        S0f = [e_sb[:, 0:1024].rearrange("p (h v) -> p h v", h=4), e_sb[:, 1024:2048].rearrange("p (h v) -> p h v", h=4)]
        Sout = [tmpfall[:, 0:2, :].rearrange("p a (b v) -> p (a b) v", b=2), tmpfall[:, 2:4, :].rearrange("p a (b v) -> p (a b) v", b=2)]
        S0b = [Vflat[:, 8192:9216].rearrange("p (h v) -> p h v", h=4), Vflat[:, 9216:10240].rearrange("p (h v) -> p h v", h=4)]
        VMj = [Vflat[0:64, 10240:11264], Vflat[0:64, 11264:12288]]
        Kp = [Vflat[:, 12288:13312], Vflat[:, 13312:14336]]
        Vp = [Vflat[:, 14336:15360], Vflat[:, 15360:16384]]
        KpT = [KTflat[:, 0:1024].rearrange("p (a b) -> p a b", a=8), KTflat[:, 1024:2048].rearrange("p (a b) -> p a b", a=8)]
        cn = sb("cn", [64, 8], F32)

        bank = [ctx.enter_context(nc.psum_tensor("bank%d" % i, [128, 512], F32)) for i in range(8)]

        def bbf(i):
            return bank[i][:, :].bitcast(BF16)

        rr = [0]

        def mmbank():
            b = rr[0] % 4
            rr[0] += 1
            return b

        def BK(b):
            return ("bank", b)

        wq = []
        wst = {"issued": 0, "used": 0}

        NGRP = 53
        wscr = nc.dram_tensor("wscr", [NGRP, 128, 8192], BF16).ap()

        def w_issue_upto(n):
            while wst["issued"] < min(n, len(wq)):
                i = wst["issued"]
                s = i % 2
                gi = i % NGRP
                kind = wq[i][1]
                if kind == "ple":
                    A("pool", lambda e, s=s, gi=gi: e.dma_start(out=wsl[s][:, 0:8, :].rearrange("p a b -> p (a b)"), in_=wscr[gi, :, 0:4096]),
                      reads=[("wscr", gi)], writes=[("w", s, 0), ("w", s, 1)], dma=True)
                else:
                    A("pool", lambda e, s=s, gi=gi: e.dma_start(out=wsl[s][:, :, :].rearrange("p a b -> p (a b)"), in_=wscr[gi, :, :]),
                      reads=[("wscr", gi)], writes=[("w", s, 0), ("w", s, 1)], dma=True)
                wst["issued"] += 1

        def w_prologue():
            for gi in range(NGRP):
                s = gi % 2
                src, kind = wq[gi]
                if kind == "ple":
                    A("pool", lambda e, s=s, src=src: e.dma_start(
                        out=wsl[s][:, 0:8, :].rearrange("p (k a) c -> p k (a c)", k=2), in_=src),
                      writes=[("w", s, 0), ("w", s, 1)], dma=True)
                    A("sp", lambda e, s=s, gi=gi: e.dma_start(out=wscr[gi, :, 0:4096], in_=wsl[s][:, 0:8, :].rearrange("p a b -> p (a b)")),
                      reads=[("w", s, 0), ("w", s, 1)], writes=[("wscr", gi)], dma=True)
                else:
                    for g in range(2):
                        A("pool", lambda e, s=s, src=src, g=g: e.dma_start(out=wsl[s][:, 8 * g:8 * g + 8, :], in_=src[:, 8 * g:8 * g + 8, :]),
                          writes=[("w", s, g)], dma=True)
                    A("sp", lambda e, s=s, gi=gi: e.dma_start(out=wscr[gi, :, :], in_=wsl[s][:, :, :].rearrange("p a b -> p (a b)")),
                      reads=[("w", s, 0), ("w", s, 1)], writes=[("wscr", gi)], dma=True)

        def w_next():
            i = wst["used"]
            wst["used"] += 1
            w_issue_upto(i + 2)
            s = i % 2
            return wsl[s], [("w", s, 0), ("w", s, 1)]

        def wcols(w, c0):
            return w[:, c0:c0 + 512].rearrange("(kc p) c -> p kc c", p=128)

        tiles = [("p", i) for i in range(n_ptiles)] + ([("s", 0)] if with_sample else [])
        for _ in tiles:
            for c0 in [C_QB, C_KB, C_VB, C_VB + 512, C_RB, C_RB + 512, C_QA, C_QA + 512, C_KA, C_KA + 512, C_VA, C_VA + 512]:
                wq.append((wcols(w_in, c0), "std"))
            for j in range(4):
                wq.append((wcols(w_out, 512 * j), "std"))
            for j in range(16):
                wq.append((wcols(w_up, 512 * j), "std"))
            for cg in range(4):
                for fg in range(4):
                    wq.append((w_down[fg * 2048:(fg + 1) * 2048, cg * 512:(cg + 1) * 512].rearrange("(kc p) c -> p kc c", p=128), "std"))
            wq.append((w_ple.rearrange("(k p) c -> p k c", p=128), "ple"))
            for j in range(4):
                wq.append((wcols(w_pg, 512 * j), "std"))

        A("pool", lambda e: e.memset(ones_bf[:, :], 1.0), writes=["ones_bf"])
        A("pool", lambda e: e.memset(zeros_bf[:, :], 0.0), writes=["zeros_bf"])
        A("pool", lambda e: e.memset(ones_f[:, :], 1.0), writes=["ones_f"])
        A("pool", lambda e: e.affine_select(out=ident[:, :], in_=ones_bf[:, :], pattern=[[1, 128]], compare_op=ALU.is_equal,
                                            fill=0.0, base=0, channel_multiplier=-1), reads=["ones_bf"], writes=["ident"])
        A("pool", lambda e: e.affine_select(out=caus_bf[:, :], in_=ones_bf[:, :], pattern=[[1, 128]], compare_op=ALU.is_ge,
                                            fill=0.0, base=0, channel_multiplier=-1), reads=["ones_bf"], writes=["caus_bf"])
        A("pool", lambda e: e.affine_select(out=caus_f[:, :], in_=ones_f[:, :], pattern=[[1, 128]], compare_op=ALU.is_ge,
                                            fill=0.0, base=0, channel_multiplier=-1), reads=["ones_f"], writes=["caus_f"])
        A("pool", lambda e: e.affine_select(out=lstr_f[:, :], in_=ones_f[:, :], pattern=[[-1, 128]], compare_op=ALU.is_gt,
                                            fill=0.0, base=0, channel_multiplier=1), reads=["ones_f"], writes=["lstr_f"])
        A("pool", lambda e: e.memset(m128[:, :], 1.0), writes=["m128"])
        A("pool", lambda e: e.memset(m128[:, 0:1], 0.0), writes=["m128"])
        A("pool", lambda e: e.memset(m4[:, :, :], 1.0), writes=["m4"])
        A("pool", lambda e: e.memset(m4[:, :, 0:1], 0.0), writes=["m4"])
        A("pool", lambda e: e.memset(ctot[:, :], 0.0), writes=["ctot"])
        for gi, g in enumerate((g_mix, g_mlp, g_pg)):
            A("sp", lambda e, gi=gi, g=g: e.dma_start(out=gains[:, gi, :], in_=g.rearrange("(kc p) -> p kc", p=128),
                                                      allow_slow_non_contiguous=True), writes=["gains"], dma=True)
        A("sp", lambda e: e.dma_start(out=ggla[:, :], in_=g_gla.rearrange("(c p) -> p c", p=128), allow_slow_non_contiguous=True),
          writes=["ggla"], dma=True)
        A("sp", lambda e: e.dma_start(out=nbgk[:, :], in_=b_gk.rearrange("(h p) -> p h", p=128), allow_slow_non_contiguous=True),
          writes=["nbgk"], dma=True)
        A("dve", lambda e: e.tensor_scalar(out=nbgk[:, :], in0=nbgk[:, :], scalar1=-1.0, scalar2=None, op0=ALU.mult),
          reads=["nbgk"], writes=["nbgk"])
        A("sp", lambda e: e.dma_start(out=bfb[:, :], in_=b_f.partition_broadcast(128)), writes=["bfb"], dma=True)
        A("sp", lambda e: e.dma_start(out=gple_bc[:, :], in_=g_ple.partition_broadcast(128)), writes=["gple_bc"], dma=True)
        A("sp", lambda e: e.dma_start(out=gfin_bc[:, :], in_=g_fin.partition_broadcast(128)), writes=["gfin_bc"], dma=True)
        A("pool", lambda e: e.dma_start(out=wsm[:, :, 0:8], in_=w_in[:, C_FA:C_FA + 8].rearrange("(kc p) c -> p kc c", p=128)),
          writes=["wsm"], dma=True)
        A("pool", lambda e: e.dma_start(out=wsm[:, :, 8:24], in_=w_in[:, C_GL:C_GL + 16].rearrange("(kc p) c -> p kc c", p=128)),
          writes=["wsm"], dma=True)
        A("pool", lambda e: e.dma_start(out=wgk[:, :], in_=w_gk2[:, :]), writes=["wgk"], dma=True)

        def rms_rstd(R, src_key, col):
            A("act", lambda e: e.activation(out=smallf[0:R, col:col + 1], in_=ssq[0:R, col:col + 1], func=AF.Ln, bias=EPS, scale=1.0 / D),
              reads=[("ssq", col)], writes=[("smallf", col)])
            A("act", lambda e: e.activation(out=rstd[0:R, col:col + 1], in_=smallf[0:R, col:col + 1], func=AF.Exp, scale=-0.5),
              reads=[("smallf", col)], writes=[("rstd", col)])

        def norm_to_hT(R, gi):
            A("dve", lambda e: e.memset(ssq[0:R, 0:1], 0.0), writes=[("ssq", 0)])
            A("act", lambda e: e.activation(out=hn[0:R, :], in_=x_sb[0:R, :], func=AF.Square, accum_out=ssq[0:R, 0:1]),
              reads=["x", ("ssq", 0)], writes=["hn", ("ssq", 0)])
            rms_rstd(R, "x", 0)
            A("dve", lambda e: e.tensor_scalar(out=hn[0:R, :], in0=x_sb[0:R, :], scalar1=rstd[0:R, 0:1], scalar2=None, op0=ALU.mult),
              reads=["x", ("rstd", 0)], writes=["hn"])
            for g in range(2):
                b = mmbank()

                def tr(e, g=g, b=b):
                    ins = None
                    for k in range(8):
                        kc = 8 * g + k
                        ins = e.transpose(out=bbf(b)[:, k * 128:k * 128 + R], in_=hn[0:R, kc * 128:(kc + 1) * 128], identity=ident[0:R, 0:R])
                    return ins
                A("pe", tr, reads=["hn", "ident"], writes=[BK(b)])
                A("dve", lambda e, g=g, b=b: e.tensor_tensor(
                    out=hT[:, 8 * g:8 * g + 8, 0:R], in0=bbf(b).rearrange("p (k t) -> p k t", k=8)[:, :, 0:R],
                    in1=bc_last(gains[:, gi, 8 * g:8 * g + 8], R), op=ALU.mult),
                  reads=[BK(b), "gains"], writes=["hT"])

        def fm_group(wt, wk, R, src, srck, ncc=4, kcs=NKC):
            b = mmbank()

            def f(e):
                ins = None
                for cc in range(ncc):
                    for kc in range(kcs):
                        ins = e.matmul(bank[b][:, cc * 128:cc * 128 + R], lhsT=wt[:, kc, cc * 128:(cc + 1) * 128], rhs=src[:, kc, 0:R],
                                       start=(kc == 0), stop=(kc == kcs - 1))
                return ins
            A("pe", f, reads=wk + [srck], writes=[BK(b)])
            return b

        def tm_group(wt, wk, R, src, srck, kcs=NKC, ncols=512):
            b = mmbank()

            def f(e):
                ins = None
                for kc in range(kcs):
                    ins = e.matmul(bank[b][0:R, 0:ncols], lhsT=src[:, kc, 0:R], rhs=wt[:, kc, 0:ncols], start=(kc == 0), stop=(kc == kcs - 1))
                return ins
            A("pe", f, reads=wk + [srck], writes=[BK(b)])
            return b

        def fmview(b, R, n=4):
            return bank[b][:, :].rearrange("p (c t) -> p c t", c=4)[:, 0:n, 0:R]

        def log_sigmoid_neg(R, src_ap, dst_ap, bias_ap, rk, wk_):
            pass

        def tile_body(kind, ti):
            R = 128 if kind == "p" else 64
            xsrc = xp[ti * 128:(ti + 1) * 128, :] if kind == "p" else xs[:, :]
            A("sp", lambda e: e.dma_start(out=x_sb[0:R, :], in_=xsrc), writes=["x"], dma=True)
            psrc = ppi[ti * 128:(ti + 1) * 128, :] if kind == "p" else psi[:, :]
            A("pool", lambda e: e.dma_start(out=p_tm[0:R, :], in_=psrc), writes=["p_tm"], dma=True)
            norm_to_hT(R, 0)

            if STOP[0] <= 1:
                return
            b = mmbank()

            def f_fa(e, b=b):
                ins = None
                for kc in range(NKC):
                    ins = e.matmul(bank[b][0:R, 0:8], lhsT=hT[:, kc, 0:R], rhs=wsm[:, kc, 0:8], start=(kc == 0), stop=(kc == NKC - 1))
                return ins
            A("pe", f_fa, reads=["hT", "wsm"], writes=[BK(b)])
            A("dve", lambda e, b=b: e.tensor_tensor(out=lfx[0:R, :], in0=bank[b][0:R, 0:8], in1=bfb[0:R, :], op=ALU.add),
              reads=[BK(b), "bfb"], writes=["lfx"])
            A("act", lambda e: e.activation(out=lfx[0:R, :], in_=lfx[0:R, :], func=AF.Exp, scale=-1.0), reads=["lfx"], writes=["lfx"])
            A("act", lambda e: e.activation(out=lfx[0:R, :], in_=lfx[0:R, :], func=AF.Ln, bias=1.0), reads=["lfx"], writes=["lfx"])
            A("dve", lambda e: e.tensor_scalar(out=lf[0:R, :], in0=lfx[0:R, :], scalar1=-1.0, scalar2=None, op0=ALU.mult),
              reads=["lfx"], writes=["lf"])
            fdst = fpo[ti * 128:(ti + 1) * 128, :] if kind == "p" else fso[:, :]
            A("sp", lambda e: e.dma_start(out=fdst, in_=lf[0:R, :]), reads=["lf"], writes=["fout"], dma=True)

            b2 = mmbank()

            def f_gl(e):
                ins = None
                for kc in range(NKC):
                    ins = e.matmul(bank[b2][0:16, 0:R], lhsT=wsm[:, kc, 8:24], rhs=hT[:, kc, 0:R], start=(kc == 0), stop=(kc == NKC - 1))
                return ins
            A("pe", f_gl, reads=["hT", "wsm"], writes=[BK(b2)])
            A("act", lambda e: e.activation(out=glT[:, 0:R], in_=bank[b2][0:16, 0:R], func=AF.Copy), reads=[BK(b2)], writes=["glT"])
            b3 = mmbank()

            def f_la(e):
                ins = None
                for h in range(4):
                    ins = e.matmul(bank[b3][:, h * 128:h * 128 + R], lhsT=wgk[:, h * 128:(h + 1) * 128], rhs=glT[:, 0:R], start=True, stop=True)
                return ins
            A("pe", f_la, reads=["glT", "wgk"], writes=[BK(b3)])
            la = tmpf[0][:, :].rearrange("p (h t) -> p h t", h=4)
            for h in range(4):
                A("act", lambda e, h=h: e.activation(out=la[:, h, 0:R], in_=bank[b3][:, h * 128:h * 128 + R], func=AF.Exp,
                                                     bias=nbgk[:, h:h + 1], scale=-1.0),
                  reads=[BK(b3), "nbgk"], writes=[("tmpf", 0)])
                A("act", lambda e, h=h: e.activation(out=la[:, h, 0:R], in_=la[:, h, 0:R], func=AF.Ln, bias=1.0),
                  reads=[("tmpf", 0)], writes=[("tmpf", 0)])
                msk = m128[:, 0:R] if kind == "p" else m4[:, :, :].rearrange("p a b -> p (a b)")
                A("dve", lambda e, h=h, msk=msk: e.tensor_tensor_scan(out=eb[:, h, 0:R], data0=msk, data1=la[:, h, 0:R], initial=0.0,
                                                                      op0=ALU.mult, op1=ALU.add),
                  reads=[("tmpf", 0), "m128", "m4"], writes=[("eb", h)])
                A("act", lambda e, h=h: e.activation(out=enb[:, h, 0:R], in_=eb[:, h, 0:R], func=AF.Exp, scale=1.0 / 16.0),
                  reads=[("eb", h)], writes=[("enb", h)])
                A("act", lambda e, h=h: e.activation(out=eb[:, h, 0:R], in_=eb[:, h, 0:R], func=AF.Exp, scale=-1.0 / 16.0),
                  reads=[("eb", h)], writes=[("eb", h)])

            if STOP[0] <= 2:
                return
            wt, wk = w_next()
            b = fm_group(wt, wk, R, hT, "hT")
            A("dve", lambda e, b=b: e.scalar_tensor_tensor(out=qeT[:, :, 0:R], in0=fmview(b, R), scalar=SC_A, in1=eb[:, :, 0:R],
                                                           op0=ALU.mult, op1=ALU.mult),
              reads=[BK(b)] + [("eb", h) for h in range(4)], writes=["qeT"])
            if STOP[0] == 21:
                return
            wt, wk = w_next()
            b = fm_group(wt, wk, R, hT, "hT")
            A("dve", lambda e, b=b: e.tensor_tensor(out=keT[:, :, 0:R], in0=fmview(b, R), in1=enb[:, :, 0:R], op=ALU.mult),
              reads=[BK(b)] + [("enb", h) for h in range(4)], writes=["keT"])
            if kind == "p":
                A("dve", lambda e: e.tensor_tensor(out=kdT[:, :, 0:R], in0=keT[:, :, 0:R], in1=bc_last(eb[:, :, R - 1], R), op=ALU.mult),
                  reads=["keT"] + [("eb", h) for h in range(4)], writes=["kdT"])
            else:
                for h in range(4):
                    A("dve", lambda e, h=h: e.tensor_tensor(
                        out=kdT[:, h, 0:64].rearrange("p (j i) -> p j i", i=4), in0=keT[:, h, 0:64].rearrange("p (j i) -> p j i", i=4),
                        in1=bc_last(eb[:, h, 0:64].rearrange("p (j i) -> p j i", i=4)[:, :, 3], 4), op=ALU.mult),
                      reads=["keT", ("eb", h)], writes=["kdT"])
            if STOP[0] == 22:
                return
            for j in range(2):
                wt, wk = w_next()
                b = tm_group(wt, wk, R, hT, "hT")
                A("act", lambda e, b=b, j=j: e.activation(out=vb[0:R, j * 512:(j + 1) * 512], in_=bank[b][0:R, :], func=AF.Copy),
                  reads=[BK(b)], writes=["vb"])
            if STOP[0] == 23:
                return
            for j in range(2):
                wt, wk = w_next()
                b = fm_group(wt, wk, R, hT, "hT")
                t1 = tmpf[1][:, :].rearrange("p (c t) -> p c t", c=4)[:, :, 0:R]
                A("act", lambda e, b=b, t1=t1: e.activation(out=t1, in_=fmview(b, R), func=AF.Exp, scale=-1.0), reads=[BK(b)], writes=[("tmpf", 1)])
                A("dve", lambda e, t1=t1: e.tensor_scalar(out=t1, in0=t1, scalar1=1.0, scalar2=None, op0=ALU.add), reads=[("tmpf", 1)], writes=[("tmpf", 1)])
                A("dve", lambda e, t1=t1: e.reciprocal(out=t1, in_=t1), reads=[("tmpf", 1)], writes=[("tmpf", 1)])
                A("dve", lambda e, b=b, j=j, t1=t1: e.tensor_tensor(out=srT[:, 4 * j:4 * j + 4, 0:R], in0=fmview(b, R), in1=t1, op=ALU.mult),
                  reads=[BK(b), ("tmpf", 1)], writes=["srT"])
            if STOP[0] == 24:
                return
            for j in range(2):
                wt, wk = w_next()
                b = fm_group(wt, wk, R, hT, "hT")
                A("act", lambda e, b=b, j=j: e.activation(out=qT[:, 4 * j:4 * j + 4, 0:R], in_=fmview(b, R), func=AF.Copy),
                  reads=[BK(b)], writes=["qT"])
            if STOP[0] == 25:
                return
            knew = sb_knew
            for j in range(2):
                wt, wk = w_next()
                b = fm_group(wt, wk, R, hT, "hT")
                if kind == "p":
                    A("act", lambda e, b=b, j=j: e.activation(out=KT[:, 4 * j:4 * j + 4, ti * 128:(ti + 1) * 128], in_=fmview(b, R), func=AF.Copy),
                      reads=[BK(b)], writes=[("KT", ti)])
                else:
                    A("act", lambda e, b=b, j=j: e.activation(out=knew[:, 4 * j:4 * j + 4, 0:R], in_=fmview(b, R), func=AF.Copy),
                      reads=[BK(b)], writes=["knew"])
                b = tm_group(wt, wk, R, hT, "hT")
                A("dve", lambda e, b=b, j=j: e.tensor_copy(out=e_sb[0:R, j * 512:(j + 1) * 512], in_=bank[b][0:R, :]),
                  reads=[BK(b)], writes=["e_sb"])
            kdst = kpo[ti * 128:(ti + 1) * 128, :] if kind == "p" else kso[:, :]
            A("sp", lambda e: e.dma_start(out=kdst, in_=e_sb[0:R, 0:1024]), reads=["e_sb"], writes=["kout"], dma=True)
            if STOP[0] == 26:
                return
            if STOP[0] == 27:
                w_issue_upto(14)
                return
            for j in range(2):
                wt, wk = w_next()
                b = tm_group(wt, wk, R, hT, "hT")
                A("dve", lambda e, b=b, j=j: e.tensor_copy(out=e_sb[0:R, 1024 + j * 512:1024 + (j + 1) * 512], in_=bank[b][0:R, :]),
                  reads=[BK(b)], writes=["e_sb"])
                vdst_sb = Vt[0:R, ti, j * 512:(j + 1) * 512] if kind == "p" else vnew[0:R, j * 512:(j + 1) * 512]
                if STOP[0] != 28:
                    A("act", lambda e, j=j, v=vdst_sb: e.activation(out=v, in_=e_sb[0:R, 1024 + j * 512:1024 + (j + 1) * 512], func=AF.Copy),
                      reads=["e_sb"], writes=[("Vt", ti) if kind == "p" else "vnew"])
            vdst = vpo[ti * 128:(ti + 1) * 128, :] if kind == "p" else vso[:, :]
            if STOP[0] != 29:
                A("sp", lambda e: e.dma_start(out=vdst, in_=e_sb[0:R, 1024:2048]), reads=["e_sb"], writes=["vout"], dma=True)
            if STOP[0] in (28, 29):
                return

            if STOP[0] <= 3:
                return
            if kind == "p":
                gla_prompt(ti)
            else:
                gla_sample()
            if STOP[0] <= 4:
                return
            if kind == "p":
                fox_prompt(ti)
            else:
                fox_sample()

            if STOP[0] <= 5:
                return
            for j in range(4):
                wt, wk = w_next()
                b = tm_group(wt, wk, R, mixT, "mixT")
                A("dve", lambda e, b=b, j=j: e.tensor_tensor(out=x_sb[0:R, j * 512:(j + 1) * 512], in0=x_sb[0:R, j * 512:(j + 1) * 512],
                                                             in1=bank[b][0:R, :], op=ALU.add),
                  reads=[BK(b), "x"], writes=["x"])
            if STOP[0] <= 6:
                return
            norm_to_hT(R, 1)
            for j in range(16):
                wt, wk = w_next()
                b = fm_group(wt, wk, R, hT, "hT")
                tr_ = tmpf[j % 2][:, :].rearrange("p (c t) -> p c t", c=4)[:, :, 0:R]
                A("act", lambda e, b=b, tr_=tr_: e.activation(out=tr_, in_=fmview(b, R), func=AF.Relu), reads=[BK(b)], writes=[("tmpf", j % 2)])
                A("dve", lambda e, j=j, tr_=tr_: e.tensor_tensor(out=hidT[:, 4 * j:4 * j + 4, 0:R], in0=tr_, in1=tr_, op=ALU.mult),
                  reads=[("tmpf", j % 2)], writes=["hidT"])
            for cg in range(4):
                b = 4 + (cg % 2)
                for fg in range(4):
                    wt, wk = w_next()

                    def f(e, wt=wt, fg=fg, b=b):
                        ins = None
                        for kc in range(NKC):
                            ins = e.matmul(bank[b][0:R, :], lhsT=hidT[:, fg * 16 + kc, 0:R], rhs=wt[:, kc, :],
                                           start=(fg == 0 and kc == 0), stop=(fg == 3 and kc == NKC - 1))
                        return ins
                    A("pe", f, reads=wk + ["hidT"], writes=[BK(b)])
                A("dve", lambda e, b=b, cg=cg: e.tensor_tensor(out=x_sb[0:R, cg * 512:(cg + 1) * 512], in0=x_sb[0:R, cg * 512:(cg + 1) * 512],
                                                               in1=bank[b][0:R, :], op=ALU.add),
                  reads=[BK(b), "x"], writes=["x"])
            if STOP[0] <= 7:
                return
            b = mmbank()

            def trp(e, b=b):
                ins = None
                for k in range(2):
                    ins = e.transpose(out=bbf(b)[:, k * 128:k * 128 + R], in_=p_tm[0:R, k * 128:(k + 1) * 128], identity=ident[0:R, 0:R])
                return ins
            A("pe", trp, reads=["p_tm", "ident"], writes=[BK(b)])
            A("act", lambda e, b=b: e.activation(out=pT[:, :, 0:R], in_=bbf(b).rearrange("p (k t) -> p k t", k=8)[:, 0:2, 0:R], func=AF.Copy),
              reads=[BK(b)], writes=["pT"])
            wt, wk = w_next()
            wple = wt[:, 0:8, :].rearrange("p (k a) c -> p k (a c)", k=2)
            for cg in range(4):
                b = mmbank()

                def f(e, b=b, cg=cg, wple=wple):
                    ins = None
                    for k in range(2):
                        ins = e.matmul(bank[b][0:R, :], lhsT=pT[:, k, 0:R], rhs=wple[:, k, cg * 512:(cg + 1) * 512], start=(k == 0), stop=(k == 1))
                    return ins
                A("pe", f, reads=wk + ["pT"], writes=[BK(b)])
                A("dve", lambda e, b=b, cg=cg: e.tensor_copy(out=e_sb[0:R, cg * 512:(cg + 1) * 512], in_=bank[b][0:R, :]),
                  reads=[BK(b)], writes=["e_sb"])
            A("dve", lambda e: e.memset(ssq[0:R, 1:2], 0.0), writes=[("ssq", 1)])
            A("act", lambda e: e.activation(out=hn[0:R, :], in_=e_sb[0:R, :], func=AF.Square, accum_out=ssq[0:R, 1:2]),
              reads=["e_sb", ("ssq", 1)], writes=["hn", ("ssq", 1)])
            rms_rstd(R, "e", 1)
            A("dve", lambda e: e.scalar_tensor_tensor(out=e_sb[0:R, :], in0=e_sb[0:R, :], scalar=rstd[0:R, 1:2], in1=gple_bc[0:R, :],
                                                      op0=ALU.mult, op1=ALU.mult),
              reads=["e_sb", ("rstd", 1), "gple_bc"], writes=["e_sb"])
            norm_to_hT(R, 2)
            for j in range(4):
                wt, wk = w_next()
                b = tm_group(wt, wk, R, hT, "hT")
                t2 = tmpf[2][0:R, :]
                A("act", lambda e, b=b, t2=t2: e.activation(out=t2, in_=bank[b][0:R, :], func=AF.Exp, scale=-1.0), reads=[BK(b)], writes=[("tmpf", 2)])
                A("dve", lambda e, t2=t2: e.tensor_scalar(out=t2, in0=t2, scalar1=1.0, scalar2=None, op0=ALU.add), reads=[("tmpf", 2)], writes=[("tmpf", 2)])
                A("dve", lambda e, t2=t2: e.reciprocal(out=t2, in_=t2), reads=[("tmpf", 2)], writes=[("tmpf", 2)])
                A("dve", lambda e, t2=t2, j=j: e.tensor_tensor(out=t2, in0=t2, in1=e_sb[0:R, j * 512:(j + 1) * 512], op=ALU.mult),
                  reads=[("tmpf", 2), "e_sb"], writes=[("tmpf", 2)])
                A("dve", lambda e, t2=t2, j=j: e.tensor_tensor(out=x_sb[0:R, j * 512:(j + 1) * 512], in0=x_sb[0:R, j * 512:(j + 1) * 512], in1=t2,
                                                               op=ALU.add),
                  reads=[("tmpf", 2), "x"], writes=["x"])
            if STOP[0] <= 8:
                return
            A("dve", lambda e: e.memset(ssq[0:R, 2:3], 0.0), writes=[("ssq", 2)])
            A("act", lambda e: e.activation(out=hn[0:R, :], in_=x_sb[0:R, :], func=AF.Square, accum_out=ssq[0:R, 2:3]),
              reads=["x", ("ssq", 2)], writes=["hn", ("ssq", 2)])
            rms_rstd(R, "x", 2)
            A("dve", lambda e: e.scalar_tensor_tensor(out=x_sb[0:R, :], in0=x_sb[0:R, :], scalar=rstd[0:R, 2:3], in1=gfin_bc[0:R, :],
                                                      op0=ALU.mult, op1=ALU.mult),
              reads=["x", ("rstd", 2), "gfin_bc"], writes=["x"])
            ydst = yp[ti * 128:(ti + 1) * 128, :] if kind == "p" else ys[:, :]
            A("sp", lambda e: e.dma_start(out=ydst, in_=x_sb[0:R, :]), reads=["x"], writes=["yout"], dma=True)

        def gla_out_norm(R):
            pass

        def gla_finish(R, ob):
            for c in range(2):
                A("act", lambda e, c=c: e.activation(out=hn[:, c * 512:(c + 1) * 512].rearrange("p (h t) -> p h t", h=4)[:, :, 0:R],
                                                     in_=fmview(ob[c], R), func=AF.Square),
                  reads=[BK(ob[c])], writes=["hn"])
            b = mmbank()

            def f(e, b=b):
                ins = None
                for h in range(4):
                    for c in range(2):
                        ins = e.matmul(bank[b][:, h * 128:h * 128 + R], lhsT=ones_bf[:, :], rhs=hn[:, c * 512 + h * 128:c * 512 + h * 128 + R],
                                       start=(c == 0), stop=(c == 1))
                return ins
            A("pe", f, reads=["hn", "ones_bf"], writes=[BK(b)])
            t3 = tmpf[3][:, :].rearrange("p (h t) -> p h t", h=4)[:, :, 0:R]
            A("act", lambda e, b=b: e.activation(out=t3, in_=fmview(b, R), func=AF.Ln, bias=EPS, scale=1.0 / 256.0), reads=[BK(b)], writes=[("tmpf", 3)])
            A("act", lambda e: e.activation(out=t3, in_=t3, func=AF.Exp, scale=-0.5), reads=[("tmpf", 3)], writes=[("tmpf", 3)])
            for c in range(2):
                t1 = tmpf[1][:, :].rearrange("p (h t) -> p h t", h=4)[:, :, 0:R]
                A("dve", lambda e, c=c, t1=t1: e.scalar_tensor_tensor(out=t1, in0=fmview(ob[c], R), scalar=ggla[:, c:c + 1], in1=t3,
                                                                      op0=ALU.mult, op1=ALU.mult),
                  reads=[BK(ob[c]), ("tmpf", 3), "ggla"], writes=[("tmpf", 1)])
                A("dve", lambda e, c=c, t1=t1: e.tensor_tensor(
                    out=mixT[:, 8:16, 0:R].rearrange("p (h c) t -> p h c t", c=2)[:, :, c, :], in0=t1,
                    in1=srT[:, :, 0:R].rearrange("p (h c) t -> p h c t", c=2)[:, :, c, :], op=ALU.mult),
                  reads=[("tmpf", 1), "srT"], writes=["mixT"])

        def kd_transpose(R):
            b = mmbank()

            def f(e, b=b):
                ins = None
                for h in range(4):
                    ins = e.transpose(out=bbf(b)[0:R, h * 128:(h + 1) * 128], in_=kdT[:, h, 0:R], identity=ident[:, :])
                return ins
            A("pe", f, reads=["kdT", "ident"], writes=[BK(b)])
            A("act", lambda e, b=b: e.activation(out=kdTok[0:R, :, :], in_=bbf(b)[0:R, 0:512].rearrange("p (h k) -> p h k", h=4), func=AF.Copy),
              reads=[BK(b)], writes=["kdTok"])

        def gla_prompt(ti):
            R = 128
            kd_transpose(R)
            ob = [6, 7]
            for h in range(4):
                ab = 4 + (h % 2)
                A("pe", lambda e, h=h, ab=ab: e.matmul(bank[ab][:, 0:128], lhsT=keT[:, h, :], rhs=qeT[:, h, :], start=True, stop=True),
                  reads=["keT", "qeT"], writes=[BK(ab)])
                A("dve", lambda e, h=h, ab=ab: e.tensor_tensor(out=ATm[h % 2][:, :], in0=bank[ab][:, 0:128], in1=caus_bf[:, :], op=ALU.mult),
                  reads=[BK(ab), "caus_bf"], writes=[("ATm", h % 2)])

                def fo(e, h=h):
                    ins = None
                    for c in range(2):
                        o = bank[ob[c]][:, h * 128:(h + 1) * 128]
                        if ti > 0:
                            e.matmul(o, lhsT=Sbf[:, h, c * 128:(c + 1) * 128], rhs=qeT[:, h, :], start=True, stop=False)
                        ins = e.matmul(o, lhsT=vb[:, h * 256 + c * 128:h * 256 + (c + 1) * 128], rhs=ATm[h % 2][:, :], start=(ti == 0), stop=True)
                    return ins
                A("pe", fo, reads=[("Sbf", h), "qeT", "vb", ("ATm", h % 2)], writes=[BK(6), BK(7)])
                b = mmbank()
                A("pe", lambda e, h=h, b=b: e.matmul(bank[b][:, 0:256], lhsT=kdTok[:, h, :], rhs=vb[:, h * 256:(h + 1) * 256], start=True, stop=True),
                  reads=["kdTok", "vb"], writes=[BK(b)])
                if ti == 0:
                    A("dve", lambda e, h=h, b=b: e.tensor_copy(out=Sst[:, h, :], in_=bank[b][:, 0:256]), reads=[BK(b)], writes=[("Sst", h)])
                else:
                    A("dve", lambda e, h=h, b=b: e.scalar_tensor_tensor(out=Sst[:, h, :], in0=Sst[:, h, :], scalar=eb[:, h, 127:128],
                                                                        in1=bank[b][:, 0:256], op0=ALU.mult, op1=ALU.add),
                      reads=[BK(b), ("Sst", h), ("eb", h)], writes=[("Sst", h)])
                A("act", lambda e, h=h: e.activation(out=Sbf[:, h, :], in_=Sst[:, h, :], func=AF.Copy), reads=[("Sst", h)], writes=[("Sbf", h)])
            gla_finish(R, ob)
            if ti == n_ptiles - 1:
                A("sp", lambda e: e.dma_start(out=spo.rearrange("h k v -> k h v"), in_=Sst[:, :, :]),
                  reads=[("Sst", h) for h in range(4)], writes=["spo"], dma=True)

        def fox_prompt(ti):
            R = 128
            nblk = ti + 1
            b = mmbank()
            A("pe", lambda e, b=b: e.matmul(bank[b][:, 0:8], lhsT=caus_f[:, :], rhs=lf[:, :], start=True, stop=True),
              reads=["caus_f", "lf"], writes=[BK(b)])
            A("dve", lambda e, b=b: e.tensor_tensor(out=cK[:, ti, :], in0=bank[b][:, 0:8], in1=ctot[:, :], op=ALU.add),
              reads=[BK(b), "ctot"], writes=["cK"])
            A("dve", lambda e: e.tensor_tensor(out=biasT[:, 0:nblk, :], in0=bc_mid(ctot[:, :], nblk), in1=cK[:, 0:nblk, :], op=ALU.subtract),
              reads=["cK", "ctot"], writes=["biasT"])
            b2 = mmbank()
            A("pe", lambda e, b2=b2: e.matmul(bank[b2][:, 0:8], lhsT=ones_f[:, :], rhs=lf[:, :], start=True, stop=True),
              reads=["ones_f", "lf"], writes=[BK(b2)])
            A("dve", lambda e, b2=b2: e.tensor_tensor(out=ctot[:, :], in0=ctot[:, :], in1=bank[b2][:, 0:8], op=ALU.add),
              reads=[BK(b2), "ctot"], writes=["ctot"])
            n = 0
            for h in range(8):
                for kb in range(nblk):
                    sbk = 4 + (n % 2)
                    pt = PTb[n % 2]
                    n += 1
                    A("pe", lambda e, h=h, kb=kb, sbk=sbk: e.matmul(bank[sbk][:, 0:128], lhsT=KT[:, h, kb * 128:(kb + 1) * 128], rhs=qT[:, h, :],
                                                                    start=True, stop=True),
                      reads=[("KT", kb), "qT"], writes=[BK(sbk)])
                    A("act", lambda e, h=h, kb=kb, sbk=sbk, pt=pt: e.activation(out=pt[:, 0:128], in_=bank[sbk][:, 0:128], func=AF.Exp,
                                                                                bias=biasT[:, kb, h:h + 1], scale=SC_A),
                      reads=[BK(sbk), "biasT"], writes=[("PT", id(pt))])
                    if kb == ti:
                        A("dve", lambda e, pt=pt: e.tensor_tensor(out=pt[:, 0:128], in0=pt[:, 0:128], in1=caus_bf[:, :], op=ALU.mult),
                          reads=[("PT", id(pt)), "caus_bf"], writes=[("PT", id(pt))])

                    def fpv(e, h=h, kb=kb, pt=pt):
                        e.matmul(bank[6][:, 0:128], lhsT=Vt[:, kb, h * 128:(h + 1) * 128],
                                 rhs=pt[:, 0:128], start=(kb == 0), stop=(kb == nblk - 1))
                        return e.matmul(bank[7][:, 0:128], lhsT=ones_bf[:, :], rhs=pt[:, 0:128], start=(kb == 0), stop=(kb == nblk - 1))
                    A("pe", fpv, reads=[("Vt", kb), ("PT", id(pt)), "ones_bf"], writes=[BK(6), BK(7)])
                A("dve", lambda e: e.reciprocal(out=tmpf[2][:, 0:128], in_=bank[7][:, 0:128]), reads=[BK(7)], writes=[("tmpf", 2)])
                A("dve", lambda e, h=h: e.tensor_tensor(out=mixT[:, h, :], in0=bank[6][:, 0:128], in1=tmpf[2][:, 0:128], op=ALU.mult),
                  reads=[BK(6), ("tmpf", 2)], writes=["mixT"])

        sb_knew = sb("knew", [128, 8, 64], BF16)
        vnew = sb("vnew", [64, 1024], BF16)

        def gla_sample():
            R = 64
            kd_transpose(R)
            ob = [6, 7]
            for j in range(16):
                s = j % 2
                A("sp", lambda e, j=j, s=s: e.dma_start(out=S0f[s], in_=sgl[j].rearrange("h k v -> k h v")), reads=["fence"], writes=[("S0f", s), "e_sb"], dma=True)
                A("pool", lambda e, j=j, s=s: e.dma_start(out=S0b[s], in_=sgl[j].rearrange("h k v -> k h v")), reads=["fence"], writes=[("S0b", s)], dma=True)
                if j == 0:
                    for h in range(4):
                        ab = 4 + (h % 2)
                        A("pe", lambda e, h=h, ab=ab: e.matmul(bank[ab][0:64, 0:64], lhsT=keT[:, h, 0:64], rhs=qeT[:, h, 0:64], start=True, stop=True),
                          reads=["keT", "qeT"], writes=[BK(ab)])
                        A("dve", lambda e, h=h, ab=ab: e.tensor_tensor(out=ATs[0:64, h, :], in0=bank[ab][0:64, 0:64], in1=blk4_bf[:, :], op=ALU.mult),
                          reads=[BK(ab), "blk4_bf"], writes=["ATs"])

                    def fo(e):
                        ins = None
                        for c in range(2):
                            e.matmul(bank[ob[c]][:, :], lhsT=zeros_bf[:, 0:128], rhs=zeros_bf[:, :], start=True, stop=False)
                        for h in range(4):
                            for c in range(2):
                                ins = e.matmul(bank[ob[c]][:, h * 128:h * 128 + 64], lhsT=vb[0:64, h * 256 + c * 128:h * 256 + (c + 1) * 128],
                                               rhs=ATs[0:64, h, :], start=False, stop=False)
                        return ins
                    A("pe", fo, reads=["vb", "ATs", "zeros_bf"], writes=[BK(6), BK(7)])

                def fi(e, j=j, s=s):
                    ins = None
                    for h in range(4):
                        for c in range(2):
                            ins = e.matmul(bank[ob[c]][:, h * 128 + 4 * j:h * 128 + 4 * j + 4], lhsT=S0b[s][:, h, c * 128:(c + 1) * 128],
                                           rhs=qeT[:, h, 4 * j:4 * j + 4], start=False, stop=(j == 15 and h == 3))
                    return ins
                A("pe", fi, reads=[("S0b", s), "qeT"], writes=[BK(6), BK(7)])
                A("dve", lambda e, j=j, s=s: e.tensor_scalar(out=VMj[s], in0=vb[0:64, :], scalar1=selT[:, j:j + 1], scalar2=None, op0=ALU.mult),
                  reads=["vb", "selT", "fence"], writes=[("VMj", s)])
                for h in range(4):
                    b = mmbank()
                    A("pe", lambda e, h=h, b=b, s=s: e.matmul(bank[b][:, 0:256], lhsT=kdTok[0:64, h, :], rhs=VMj[s][:, h * 256:(h + 1) * 256],
                                                              start=True, stop=True),
                      reads=["kdTok", ("VMj", s)], writes=[BK(b)])
                    A("dve", lambda e, h=h, b=b, s=s, j=j: e.scalar_tensor_tensor(out=Sout[s][:, h, :], in0=S0f[s][:, h, :],
                                                                                  scalar=eb[:, h, 4 * j + 3:4 * j + 4], in1=bank[b][:, 0:256],
                                                                                  op0=ALU.mult, op1=ALU.add),
                      reads=[BK(b), ("S0f", s), ("eb", h), "fence"], writes=[("Sout", s), ("tmpf", 2 * s), ("tmpf", 2 * s + 1)])
                A("sp", lambda e, j=j, s=s: e.dma_start(out=sso[j].rearrange("h k v -> k h v"), in_=Sout[s]),
                  reads=[("Sout", s), ("tmpf", 2 * s), ("tmpf", 2 * s + 1)], writes=[("sso", j)], dma=True)
            gla_finish(R, ob)

        ATs = sb("ATs", [64, 4, 64], BF16)

        def fox_sample():
            R = 64
            A("sp", lambda e: e.dma_start(out=idx[:, :], in_=ptab.partition_broadcast(128)), writes=["idx"], dma=True)
            A("pool", lambda e: e.iota(iot[:, :], [[0, 1]], base=0, channel_multiplier=1, allow_small_or_imprecise_dtypes=True), writes=["iot"])
            A("dve", lambda e: e.tensor_copy(out=idxf[:, :], in_=idx[:, :]), reads=["idx"], writes=["idxf"])
            A("dve", lambda e: e.tensor_scalar(out=idxf[:, :], in0=idxf[:, :], scalar1=128.0, scalar2=iot[:, 0:1], op0=ALU.mult, op1=ALU.add),
              reads=["idxf", "iot"], writes=["idxf"])
            A("dve", lambda e: e.tensor_copy(out=idx[:, :], in_=idxf[:, :]), reads=["idxf"], writes=["idx"])
            A("pool", lambda e: e.affine_select(out=Emat[:, :], in_=ones_bf[0:16, 0:64], pattern=[[1, 64]], compare_op=ALU.is_ge, fill=0.0,
                                                base=0, channel_multiplier=-4), reads=["ones_bf"], writes=["Emat"])
            A("pool", lambda e: e.affine_select(out=Emat[:, :], in_=Emat[:, :], pattern=[[-1, 64]], compare_op=ALU.is_ge, fill=0.0,
                                                base=3, channel_multiplier=4), reads=["Emat"], writes=["Emat"])
            A("pool", lambda e: e.affine_select(out=selT[:, :], in_=ones_f[0:64, 0:16], pattern=[[-4, 16]], compare_op=ALU.is_ge, fill=0.0,
                                                base=0, channel_multiplier=1), reads=["ones_f"], writes=["selT"])
            A("pool", lambda e: e.affine_select(out=selT[:, :], in_=selT[:, :], pattern=[[4, 16]], compare_op=ALU.is_ge, fill=0.0,
                                                base=3, channel_multiplier=-1), reads=["selT"], writes=["selT"])
            b = mmbank()
            A("pe", lambda e, b=b: e.matmul(bank[b][0:64, 0:64], lhsT=Emat[:, :], rhs=Emat[:, :], start=True, stop=True), reads=["Emat"], writes=[BK(b)])
            A("dve", lambda e, b=b: e.tensor_tensor(out=blk4_bf[:, :], in0=bank[b][0:64, 0:64], in1=caus_bf[0:64, 0:64], op=ALU.mult),
              reads=[BK(b), "caus_bf"], writes=["blk4_bf"])
            A("dve", lambda e, b=b: e.tensor_tensor(out=blk4_f[:, :], in0=bank[b][0:64, 0:64], in1=caus_f[0:64, 0:64], op=ALU.mult),
              reads=[BK(b), "caus_f"], writes=["blk4_f"])

        fox_sample_masks = fox_sample

        def fox_sample_main():
            R = 64
            b = mmbank()
            A("pe", lambda e, b=b: e.matmul(bank[b][0:64, 0:8], lhsT=blk4_f[:, :], rhs=lf[0:64, :], start=True, stop=True),
              reads=["blk4_f", "lf"], writes=[BK(b)])
            A("dve", lambda e, b=b: e.tensor_scalar(out=cn[:, :], in0=bank[b][0:64, 0:8], scalar1=-1.0, scalar2=None, op0=ALU.mult),
              reads=[BK(b)], writes=["cn"])
            for h in range(8):
                sbk = 4 + (h % 2)
                pt = PTb[h % 2]
                A("pe", lambda e, h=h, sbk=sbk: e.matmul(bank[sbk][0:64, 0:64], lhsT=sb_knew[:, h, :], rhs=qT[:, h, 0:64], start=True, stop=True),
                  reads=["knew", "qT"], writes=[BK(sbk)])
                A("act", lambda e, h=h, sbk=sbk, pt=pt: e.activation(out=pt[0:64, 0:64], in_=bank[sbk][0:64, 0:64], func=AF.Exp,
                                                                     bias=cn[:, h:h + 1], scale=SC_A),
                  reads=[BK(sbk), "cn"], writes=[("PT", id(pt))])
                A("dve", lambda e, pt=pt: e.tensor_tensor(out=pt[0:64, 0:64], in0=pt[0:64, 0:64], in1=blk4_bf[:, :], op=ALU.mult),
                  reads=[("PT", id(pt)), "blk4_bf"], writes=[("PT", id(pt))])

                def fpv(e, h=h, pt=pt):
                    if h == 0:
                        for bb_ in (6, 7):
                            e.matmul(bank[bb_][:, :], lhsT=zeros_bf[:, 0:128], rhs=zeros_bf[:, :], start=True, stop=False)
                    e.matmul(bank[6][:, h * 64:(h + 1) * 64], lhsT=vnew[0:64, h * 128:(h + 1) * 128], rhs=pt[0:64, 0:64], start=False, stop=False)
                    return e.matmul(bank[7][:, h * 64:(h + 1) * 64], lhsT=ones_bf[0:64, :], rhs=pt[0:64, 0:64], start=False, stop=False)
                A("pe", fpv, reads=["vnew", ("PT", id(pt)), "ones_bf", "zeros_bf"], writes=[BK(6), BK(7)])
            for j in range(16):
                ls = j % 2
                for pg in range(16):
                    col = j * 16 + pg
                    A("pool", lambda e, ls=ls, pg=pg, col=col: e.indirect_dma_start(
                        out=LFp[ls][:, pg, :], out_offset=None, in_=clf,
                        in_offset=bass.IndirectOffsetOnAxis(ap=idx[:, col:col + 1], axis=0)),
                      reads=["idx"], writes=[("LFp", ls)], dma=True)
                b = mmbank()

                def fb(e, b=b, ls=ls):
                    lfv = LFp[ls][:, :, :].rearrange("p a b -> p (a b)")
                    ins = e.matmul(bank[b][:, 0:128], lhsT=lstr_f[:, :], rhs=lfv, start=True, stop=False)
                    for k in range(1, 16):
                        ins = e.matmul(bank[b][:, 0:128 - 8 * k], lhsT=ones_f[:, :], rhs=lfv[:, 8 * k:128], start=False, stop=(k == 15))
                    return ins
                A("pe", fb, reads=[("LFp", ls), "lstr_f", "ones_f"], writes=[BK(b)])
                A("act", lambda e, b=b: e.activation(out=bias_s[:, :, :].rearrange("p a b -> p (a b)"), in_=bank[b][:, 0:128], func=AF.Copy),
                  reads=[BK(b)], writes=["bias_s"])
                sbk = 4 + (j % 2)
                for pg in range(16):
                    col = j * 16 + pg
                    s = col % 2
                    A("pool", lambda e, s=s, col=col: e.indirect_dma_start(
                        out=Kp[s], out_offset=None, in_=ck, in_offset=bass.IndirectOffsetOnAxis(ap=idx[:, col:col + 1], axis=0)),
                      reads=["idx", "fence"], writes=[("Kp", s)], dma=True)
                    A("pool", lambda e, s=s, col=col: e.indirect_dma_start(
                        out=Vp[s], out_offset=None, in_=cv, in_offset=bass.IndirectOffsetOnAxis(ap=idx[:, col:col + 1], axis=0)),
                      reads=["idx", "fence"], writes=[("Vp", s)], dma=True)
                    tb = mmbank()

                    def ftr(e, s=s, tb=tb):
                        ins = None
                        for h in range(8):
                            ins = e.transpose(out=bbf(tb)[:, h * 128:(h + 1) * 128], in_=Kp[s][:, h * 128:(h + 1) * 128], identity=ident[:, :])
                        return ins
                    A("pe", ftr, reads=[("Kp", s), "ident"], writes=[BK(tb)])
                    if col % 2 == 0:
                        A("act", lambda e, s=s, tb=tb: e.activation(out=KpT[s].rearrange("p a b -> p (a b)"), in_=bbf(tb), func=AF.Copy),
                          reads=[BK(tb), "fence"], writes=[("KpT", s)])
                    else:
                        A("dve", lambda e, s=s, tb=tb: e.tensor_copy(out=KpT[s].rearrange("p a b -> p (a b)"), in_=bbf(tb)),
                          reads=[BK(tb), "fence"], writes=[("KpT", s)])

                    def fsc(e, s=s, pg=pg, j=j, sbk=sbk):
                        ins = None
                        for h in range(8):
                            o = bank[sbk][:, pg * 32 + h * 4:pg * 32 + h * 4 + 4]
                            ins = e.matmul(o, lhsT=KpT[s][:, h, :], rhs=qT[:, h, 4 * j:4 * j + 4], start=True, stop=True)
                        return ins
                    A("pe", fsc, reads=[("KpT", s), "qT"], writes=[BK(sbk)])
                    tq = tmpf[col % 2]
                    A("dve", lambda e, pg=pg, sbk=sbk, tq=tq: e.scalar_tensor_tensor(
                        out=tq[:, 0:32].rearrange("p (h q) -> p h q", q=4), in0=bank[sbk][:, pg * 32:(pg + 1) * 32].rearrange("p (h q) -> p h q", q=4),
                        scalar=SC_A, in1=bc_last(bias_s[:, pg, :], 4), op0=ALU.mult, op1=ALU.add),
                      reads=[BK(sbk), "bias_s"], writes=[("tmpf", col % 2)])
                    pt = PTb[col % 2]
                    A("act", lambda e, tq=tq, pt=pt: e.activation(out=pt[:, 0:32], in_=tq[:, 0:32], func=AF.Exp),
                      reads=[("tmpf", col % 2)], writes=[("PT", id(pt))])

                    def fpv(e, s=s, j=j, pt=pt, last=(pg == 15 and j == 15)):
                        for h in range(8):
                            e.matmul(bank[6][:, h * 64 + 4 * j:h * 64 + 4 * j + 4], lhsT=Vp[s][:, h * 128:(h + 1) * 128], rhs=pt[:, h * 4:h * 4 + 4],
                                     start=False, stop=(last and h == 7))
                        ins = None
                        for h in range(8):
                            ins = e.matmul(bank[7][:, h * 64 + 4 * j:h * 64 + 4 * j + 4], lhsT=ones_bf[:, :], rhs=pt[:, h * 4:h * 4 + 4],
                                           start=False, stop=(last and h == 7))
                        return ins
                    A("pe", fpv, reads=[("Vp", s), ("PT", id(pt)), "ones_bf"], writes=[BK(6), BK(7)])
            A("dve", lambda e: e.reciprocal(out=tmpf[2][:, :], in_=bank[7][:, :]), reads=[BK(7)], writes=[("tmpf", 2)])
            A("dve", lambda e: e.tensor_tensor(out=mixT[:, 0:8, 0:64], in0=bank[6][:, :].rearrange("p (h t) -> p h t", h=8),
                                               in1=tmpf[2][:, :].rearrange("p (h t) -> p h t", h=8), op=ALU.mult),
              reads=[BK(6), ("tmpf", 2)], writes=["mixT"])

        if with_sample:
            fox_sample_masks()
        fox_sample = fox_sample_main
        assert len(wq) == NGRP * len(tiles)
        w_prologue()
        for kind, ti in tiles:
            if kind == "s":
                A("pool", lambda e: e.memset(smallf[:, 7:8], 0.0),
                  writes=["fence"] + [("Vt", k) for k in range(16)] + [("KT", k) for k in range(16)])
            tile_body(kind, ti)
        P.emit()
    return nc


_CACHE = {}


def kernel(x_prompt, x_sample, cache_k, cache_v, cache_logf, state_gla, page_table, p_prompt, p_sample,
           g_mix, w_in, b_f, w_gk2, b_gk, g_gla_out, w_out, g_mlp, w_up, w_down, w_ple, g_ple, g_ple_gate,
           w_ple_gate, g_final):
    f32 = lambda a: np.ascontiguousarray(np.asarray(a, dtype=np.float32))
    n = 8
    if "nc" not in _CACHE:
        nc = bass.Bass("TRN2", target_bir_lowering=False)
        build(nc)
        _CACHE["nc"] = nc
    nc = _CACHE["nc"]
    ck = f32(cache_k).reshape(N_POOL_ROWS, 1024)
    cv = f32(cache_v).reshape(N_POOL_ROWS, 1024)
    clf = f32(cache_logf).reshape(N_POOL_ROWS, 8)
    shared = {
        "ck": ck, "cv": cv, "clf": clf,
        "g_mix": f32(g_mix).reshape(D), "w_in": f32(w_in).reshape(D, IN_COLS), "b_f": f32(b_f).reshape(1, 8),
        "w_gk2": f32(w_gk2).reshape(16, 512), "b_gk": f32(b_gk).reshape(512), "g_gla": f32(g_gla_out).reshape(256),
        "w_out": f32(w_out).reshape(D, D), "g_mlp": f32(g_mlp).reshape(D), "w_up": f32(w_up).reshape(D, DFF),
        "w_down": f32(w_down).reshape(DFF, D), "w_ple": f32(w_ple).reshape(256, D), "g_ple": f32(g_ple).reshape(1, D),
        "g_pg": f32(g_ple_gate).reshape(D), "w_pg": f32(w_ple_gate).reshape(D, D), "g_fin": f32(g_final).reshape(1, D),
    }
    xp = f32(x_prompt)
    xs = f32(x_sample)
    pp = f32(p_prompt)[0]
    ps_ = f32(p_sample)[0]
    sg = f32(state_gla)[0]
    pt = np.ascontiguousarray(np.asarray(page_table, dtype=np.int32))
    in_maps = []
    for c in range(n):
        m = dict(shared)
        m["xp"] = xp[c]
        m["xs"] = np.ascontiguousarray(xs[16 * c:16 * c + 16].reshape(64, D))
        m["pp"] = pp[c]
        m["ps"] = np.ascontiguousarray(ps_[16 * c:16 * c + 16].reshape(64, 256))
        m["sgl"] = np.ascontiguousarray(sg[16 * c:16 * c + 16])
        m["ptab"] = np.ascontiguousarray(pt[16 * c:16 * c + 16].reshape(1, 256))
        in_maps.append(m)
    res = run_bass_kernel_spmd(nc, in_maps, core_ids=list(range(n)))
    r = res.results
    g = lambda k: [np.asarray(r[c][k], dtype=np.float32) for c in range(n)]
    y_prompt = np.stack(g("yp"), 0)
    y_sample = np.concatenate([a.reshape(16, 4, D) for a in g("ys")], 0)
    k_prompt = np.stack([a.reshape(2048, 8, 128) for a in g("kpo")], 0)[None]
    v_prompt = np.stack([a.reshape(2048, 8, 128) for a in g("vpo")], 0)[None]
    f_prompt = np.stack(g("fpo"), 0)[None]
    s_prompt = np.stack(g("spo"), 0)[None]
    k_sample = np.concatenate([a.reshape(16, 4, 8, 128) for a in g("kso")], 0)[None]
    v_sample = np.concatenate([a.reshape(16, 4, 8, 128) for a in g("vso")], 0)[None]
    f_sample = np.concatenate([a.reshape(16, 4, 8) for a in g("fso")], 0)[None]
    s_sample = np.concatenate(g("sso"), 0)[None]
    return (y_prompt, y_sample, k_prompt, v_prompt, f_prompt, s_prompt, k_sample, v_sample, f_sample, s_sample)
```

```python
import contextlib
import numpy as np
import concourse.bass as bass
import concourse.mybir as mybir
from concourse.bass_utils import run_bass_kernel_spmd

F32 = mybir.dt.float32
BF16 = mybir.dt.bfloat16
I32 = mybir.dt.int32
AF = mybir.ActivationFunctionType
ALU = mybir.AluOpType

RAW, WAW, WAR = 0, 1, 2
COMPUTE = ("pe", "act", "dve", "pool")
NDMASEM = 8


class Op:
    __slots__ = ("id", "eng", "fn", "deps", "is_dma", "sig", "need_sig", "dsem", "dtarget", "prev_dma")

    def __init__(self, id, eng, fn, is_dma):
        self.id = id
        self.eng = eng
        self.fn = fn
        self.deps = {}
        self.is_dma = is_dma
        self.sig = None
        self.need_sig = False
        self.dsem = None
        self.dtarget = None
        self.prev_dma = None


class Prog:
    def __init__(self, nc):
        self.nc = nc
        self.ops = []
        self.last_w = {}
        self.readers = {}
        self.dma_count = {}
        self.dma_hist = {}

    def add(self, eng, fn, reads=(), writes=(), dma=False):
        op = Op(len(self.ops), eng, fn, dma)
        deps = op.deps
        for k in reads:
            w = self.last_w.get(k)
            if w is not None:
                deps[w] = RAW
        for k in writes:
            w = self.last_w.get(k)
            if w is not None and w not in deps:
                deps[w] = WAW
            for r in self.readers.get(k, ()):
                if r not in deps:
                    deps[r] = WAR
        for k in reads:
            self.readers.setdefault(k, []).append(op.id)
        for k in writes:
            self.last_w[k] = op.id
            self.readers[k] = []
        if dma:
            n = self.dma_count.get(eng, 0)
            self.dma_count[eng] = n + 1
            op.dsem = n % NDMASEM
            op.dtarget = 16 * (n // NDMASEM + 1)
            hist = self.dma_hist.setdefault(eng, [])
            if n >= NDMASEM:
                op.prev_dma = hist[n - NDMASEM]
            hist.append(op.id)
        self.ops.append(op)
        return op.id

    def emit(self):
        nc = self.nc
        ops = self.ops
        for y in ops:
            nd = {}
            for xid, kind in y.deps.items():
                x = ops[xid]
                if x.is_dma:
                    nd[xid] = kind
                    continue
                if x.eng == y.eng and not y.is_dma and (kind == WAR or (kind == WAW and x.eng == "pe")):
                    continue
                nd[xid] = kind
                x.need_sig = True
            y.deps = nd
        cnt = {e: 0 for e in COMPUTE}
        for x in ops:
            if x.need_sig:
                cnt[x.eng] += 1
                x.sig = cnt[x.eng]
        engs = ["pe", "act", "dve", "pool", "sp"]
        per = {e: [o for o in ops if o.eng == e] for e in engs}
        with contextlib.ExitStack() as ctx:
            esem = {e: ctx.enter_context(nc.semaphore("s_" + e)) for e in COMPUTE}
            dsem = {}
            for e in self.dma_count:
                dsem[e] = [ctx.enter_context(nc.semaphore("d_%s%d" % (e, i))) for i in range(NDMASEM)]
            block = ctx.enter_context(nc.Block())

            def run(e, eng):
                waited = {}

                def wait_tok(key, sem, val):
                    if waited.get(key, 0) >= val:
                        return
                    waited[key] = val
                    eng.wait_ge(sem, val)

                for y in per[e]:
                    for xid in sorted(y.deps):
                        x = ops[xid]
                        if x.is_dma:
                            wait_tok(("d", x.eng, x.dsem), dsem[x.eng][x.dsem], x.dtarget)
                        else:
                            wait_tok(("e", x.eng), esem[x.eng], x.sig)
                    if y.is_dma:
                        if y.prev_dma is not None:
                            p = ops[y.prev_dma]
                            wait_tok(("d", p.eng, p.dsem), dsem[p.eng][p.dsem], p.dtarget)
                        y.fn(eng).then_inc(dsem[e][y.dsem], 16)
                    else:
                        ins = y.fn(eng)
                        if y.need_sig:
                            ins.then_inc(esem[e], 1)
                if e == "sp":
                    for q, n in self.dma_count.items():
                        for i in range(NDMASEM):
                            k = (n - i + NDMASEM - 1) // NDMASEM
                            if k > 0:
                                eng.wait_ge(dsem[q][i], 16 * k)

            @block.tensor
            def _(eng):
                run("pe", eng)

            @block.scalar
            def _(eng):
                run("act", eng)

            @block.vector
            def _(eng):
                run("dve", eng)

            @block.gpsimd
            def _(eng):
                run("pool", eng)

            @block.sync
            def _(eng):
                run("sp", eng)


D = 2048
NKC = 16
H_A = 8
H_B = 4
DFF = 8192
EPS = 1e-6
SC_A = 128 ** -0.5
N_POOL_ROWS = 2560 * 128
C_QA, C_KA, C_VA, C_FA, C_QB, C_KB, C_VB, C_RB, C_GL = 0, 1024, 2048, 3072, 3080, 3592, 4104, 5128, 6152
IN_COLS = 6168
STOP = [99]


def bc_last(ap, n):
    return bass.AP(ap.tensor, ap.offset, [list(x) for x in ap.ap] + [[0, n]])


def bc_mid(ap, n):
    a = [list(x) for x in ap.ap]
    return bass.AP(ap.tensor, ap.offset, [a[0], [0, n]] + a[1:])


def build(nc, with_sample=True, n_ptiles=16):
    dt = nc.dram_tensor
    xp = dt("xp", [2048, D], F32, kind="ExternalInput").ap()
    xs = dt("xs", [64, D], F32, kind="ExternalInput").ap()
    ppi = dt("pp", [2048, 256], F32, kind="ExternalInput").ap()
    psi = dt("ps", [64, 256], F32, kind="ExternalInput").ap()
    ck = dt("ck", [N_POOL_ROWS, 1024], F32, kind="ExternalInput").ap()
    cv = dt("cv", [N_POOL_ROWS, 1024], F32, kind="ExternalInput").ap()
    clf = dt("clf", [N_POOL_ROWS, 8], F32, kind="ExternalInput").ap()
    sgl = dt("sgl", [16, 4, 128, 256], F32, kind="ExternalInput").ap()
    ptab = dt("ptab", [1, 256], I32, kind="ExternalInput").ap()
    g_mix = dt("g_mix", [D], F32, kind="ExternalInput").ap()
    w_in = dt("w_in", [D, IN_COLS], F32, kind="ExternalInput").ap()
    b_f = dt("b_f", [1, 8], F32, kind="ExternalInput").ap()
    w_gk2 = dt("w_gk2", [16, 512], F32, kind="ExternalInput").ap()
    b_gk = dt("b_gk", [512], F32, kind="ExternalInput").ap()
    g_gla = dt("g_gla", [256], F32, kind="ExternalInput").ap()
    w_out = dt("w_out", [D, D], F32, kind="ExternalInput").ap()
    g_mlp = dt("g_mlp", [D], F32, kind="ExternalInput").ap()
    w_up = dt("w_up", [D, DFF], F32, kind="ExternalInput").ap()
    w_down = dt("w_down", [DFF, D], F32, kind="ExternalInput").ap()
    w_ple = dt("w_ple", [256, D], F32, kind="ExternalInput").ap()
    g_ple = dt("g_ple", [1, D], F32, kind="ExternalInput").ap()
    g_pg = dt("g_pg", [D], F32, kind="ExternalInput").ap()
    w_pg = dt("w_pg", [D, D], F32, kind="ExternalInput").ap()
    g_fin = dt("g_fin", [1, D], F32, kind="ExternalInput").ap()

    yp = dt("yp", [2048, D], F32, kind="ExternalOutput").ap()
    ys = dt("ys", [64, D], F32, kind="ExternalOutput").ap()
    kpo = dt("kpo", [2048, 1024], F32, kind="ExternalOutput").ap()
    vpo = dt("vpo", [2048, 1024], F32, kind="ExternalOutput").ap()
    fpo = dt("fpo", [2048, 8], F32, kind="ExternalOutput").ap()
    spo = dt("spo", [4, 128, 256], F32, kind="ExternalOutput").ap()
    kso = dt("kso", [64, 1024], F32, kind="ExternalOutput").ap()
    vso = dt("vso", [64, 1024], F32, kind="ExternalOutput").ap()
    fso = dt("fso", [64, 8], F32, kind="ExternalOutput").ap()
    sso = dt("sso", [16, 4, 128, 256], F32, kind="ExternalOutput").ap()

    ctx = contextlib.ExitStack()
    with ctx:
        def sb(name, shape, dtype):
            return ctx.enter_context(nc.sbuf_tensor(name, shape, dtype))

        P = Prog(nc)
        A = P.add

        KT = sb("KT", [128, 8, 2048], BF16)
        Vt = sb("Vt", [128, 16, 1024], BF16)
        x_sb = sb("x_sb", [128, D], F32)
        hT = sb("hT", [128, NKC, 128], BF16)
        mixT = sb("mixT", [128, NKC, 128], BF16)
        hidT = sb("hidT", [128, 64, 128], BF16)
        wsl = [sb("wsl%d" % i, [128, NKC, 512], BF16) for i in range(3)]
        gbc = sb("gbc", [128, D], F32)
        e_sb = sb("e_sb", [128, D], F32)
        hn = sb("hn", [128, D], BF16)
        tmpfall = sb("tmpfall", [128, 4, 512], F32)
        tmpf = [tmpfall[:, i, :] for i in range(4)]
        PTb = [sb("PT%d" % i, [128, 512], BF16) for i in range(2)]
        Sst = sb("Sst", [128, 4, 256], F32)
        Sbf = sb("Sbf", [128, 4, 256], BF16)
        cK = sb("cK", [128, 16, 8], F32)
        ctot = sb("ctot", [128, 8], F32)
        biasT = sb("biasT", [128, 16, 8], F32)
        lf = sb("lf", [128, 8], F32)
        lfx = sb("lfx", [128, 8], F32)
        smallf = sb("smallf", [128, 8], F32)
        ssq = sb("ssq", [128, 8], F32)
        rstd = sb("rstd", [128, 8], F32)
        ident = sb("ident", [128, 128], BF16)
        ones_bf = sb("ones_bf", [128, 128], BF16)
        zeros_bf = sb("zeros_bf", [128, 512], BF16)
        caus_bf = sb("caus_bf", [128, 128], BF16)
        caus_f = sb("caus_f", [128, 128], F32)
        ones_f = sb("ones_f", [128, 128], F32)
        lstr_f = sb("lstr_f", [128, 128], F32)
        m128 = sb("m128", [128, 128], F32)
        m4 = sb("m4", [128, 16, 4], F32)
        gains = sb("gains", [128, 3, 16], F32)
        ggla = sb("ggla", [128, 2], F32)
        nbgk = sb("nbgk", [128, 4], F32)
        bfb = sb("bfb", [128, 8], F32)
        wsm = sb("wsm", [128, NKC, 24], BF16)
        wgk = sb("wgk", [16, 512], BF16)
        qT = sb("qT", [128, 8, 128], BF16)
        qeT = sb("qeT", [128, 4, 128], BF16)
        keT = sb("keT", [128, 4, 128], BF16)
        kdT = sb("kdT", [128, 4, 128], BF16)
        kdTok = sb("kdTok", [128, 4, 128], BF16)
        vb = sb("vb", [128, 1024], BF16)
        srT = sb("srT", [128, 8, 128], BF16)
        glT = sb("glT", [16, 128], BF16)
        pT = sb("pT", [128, 2, 128], BF16)
        p_tm = sb("p_tm", [128, 256], BF16)
        eb = sb("eb", [128, 4, 128], F32)
        enb = sb("enb", [128, 4, 128], F32)
        ATm = [sb("ATm%d" % i, [128, 128], BF16) for i in range(2)]
        idx = sb("idx", [128, 256], I32)
        idxf = sb("idxf", [128, 256], F32)
        iot = sb("iot", [128, 1], F32)
        Emat = sb("Emat", [16, 64], BF16)
        blk4_bf = sb("blk4_bf", [64, 64], BF16)
        blk4_f = sb("blk4_f", [64, 64], F32)
        selT = sb("selT", [64, 16], F32)
        LFp = [sb("LFp%d" % i, [128, 16, 8], F32) for i in range(2)]
        bias_s = sb("bias_s", [128, 16, 8], F32)
        Vflat = Vt[:, :, :].rearrange("p a b -> p (a b)")
        KTflat = KT[:, :, :].rearrange("p a b -> p (a b)")

        S0f = [e_sb[:, 0:1024].rearrange("p (h v) -> p h v", h=4), e_sb[:, 1024:2048].rearrange("p (h v) -> p h v", h=4)]
        Sout = [tmpfall[:, 0:2, :].rearrange("p a (b v) -> p (a b) v", b=2), tmpfall[:, 2:4, :].rearrange("p a (b v) -> p (a b) v", b=2)]
        S0b = [Vflat[:, 8192:9216].rearrange("p (h v) -> p h v", h=4), Vflat[:, 9216:10240].rearrange("p (h v) -> p h v", h=4)]
        VMj = [Vflat[0:64, 10240:11264], Vflat[0:64, 11264:12288]]
        Kp = [Vflat[:, 12288 + 1024 * i:13312 + 1024 * i] for i in range(4)]
        Vp = [Vflat[:, 1024 * i:1024 * (i + 1)] for i in range(4)]
        KpT = [KTflat[:, 1024 * i:1024 * (i + 1)].rearrange("p (a b) -> p a b", a=8) for i in range(4)]
        cn = sb("cn", [64, 8], F32)

        bank = [ctx.enter_context(nc.psum_tensor("bank%d" % i, [128, 512], F32)) for i in range(8)]

        def bbf(i):
            return bank[i][:, :].bitcast(BF16)

        rr = [0]

        def mmbank():
            b = rr[0] % 4
            rr[0] += 1
            return b

        def BK(b):
            return ("bank", b)

        wq = []
        wst = {"issued": 0, "used": 0}

        NGRP = 53
        wscr = nc.dram_tensor("wscr", [NGRP, 128, 8192], BF16).ap()

        def w_issue_upto(n):
            while wst["issued"] < min(n, len(wq)):
                i = wst["issued"]
                s = i % 3
                gi = i % NGRP
                kind = wq[i][1]
                if kind == "ple":
                    A("pool", lambda e, s=s, gi=gi: e.dma_start(out=wsl[s][:, 0:8, :].rearrange("p a b -> p (a b)"), in_=wscr[gi, :, 0:4096]),
                      reads=[("wscr", gi)], writes=[("w", s, 0), ("w", s, 1)], dma=True)
                else:
                    A("pool", lambda e, s=s, gi=gi: e.dma_start(out=wsl[s][:, :, :].rearrange("p a b -> p (a b)"), in_=wscr[gi, :, :]),
                      reads=[("wscr", gi)], writes=[("w", s, 0), ("w", s, 1)], dma=True)
                wst["issued"] += 1

        def w_prologue():
            for gi in range(NGRP):
                s = gi % 3
                src, kind = wq[gi]
                if kind == "ple":
                    A("pool", lambda e, s=s, src=src: e.dma_start(
                        out=wsl[s][:, 0:8, :].rearrange("p (k a) c -> p k (a c)", k=2), in_=src),
                      writes=[("w", s, 0), ("w", s, 1)], dma=True)
                    A("sp", lambda e, s=s, gi=gi: e.dma_start(out=wscr[gi, :, 0:4096], in_=wsl[s][:, 0:8, :].rearrange("p a b -> p (a b)")),
                      reads=[("w", s, 0), ("w", s, 1)], writes=[("wscr", gi)], dma=True)
                else:
                    for g in range(2):
                        A("pool", lambda e, s=s, src=src, g=g: e.dma_start(out=wsl[s][:, 8 * g:8 * g + 8, :], in_=src[:, 8 * g:8 * g + 8, :]),
                          writes=[("w", s, g)], dma=True)
                    A("sp", lambda e, s=s, gi=gi: e.dma_start(out=wscr[gi, :, :], in_=wsl[s][:, :, :].rearrange("p a b -> p (a b)")),
                      reads=[("w", s, 0), ("w", s, 1)], writes=[("wscr", gi)], dma=True)

        def w_next():
            i = wst["used"]
            wst["used"] += 1
            w_issue_upto(i + 3)
            s = i % 3
            return wsl[s], [("w", s, 0), ("w", s, 1)]

        def wcols(w, c0):
            return w[:, c0:c0 + 512].rearrange("(kc p) c -> p kc c", p=128)

        tiles = [("p", i) for i in range(n_ptiles)] + ([("s", 0)] if with_sample else [])
        for _ in tiles:
            for c0 in [C_QB, C_KB, C_VB, C_VB + 512, C_RB, C_RB + 512, C_QA, C_QA + 512, C_KA, C_KA + 512, C_VA, C_VA + 512]:
                wq.append((wcols(w_in, c0), "std"))
            for j in range(4):
                wq.append((wcols(w_out, 512 * j), "std"))
            for j in range(16):
                wq.append((wcols(w_up, 512 * j), "std"))
            for cg in range(4):
                for fg in range(4):
                    wq.append((w_down[fg * 2048:(fg + 1) * 2048, cg * 512:(cg + 1) * 512].rearrange("(kc p) c -> p kc c", p=128), "std"))
            wq.append((w_ple.rearrange("(k p) c -> p k c", p=128), "ple"))
            for j in range(4):
                wq.append((wcols(w_pg, 512 * j), "std"))

        A("pool", lambda e: e.memset(ones_bf[:, :], 1.0), writes=["ones_bf"])
        A("pool", lambda e: e.memset(zeros_bf[:, :], 0.0), writes=["zeros_bf"])
        A("pool", lambda e: e.memset(ones_f[:, :], 1.0), writes=["ones_f"])
        A("pool", lambda e: e.affine_select(out=ident[:, :], in_=ones_bf[:, :], pattern=[[1, 128]], compare_op=ALU.is_equal,
                                            fill=0.0, base=0, channel_multiplier=-1), reads=["ones_bf"], writes=["ident"])
        A("pool", lambda e: e.affine_select(out=caus_bf[:, :], in_=ones_bf[:, :], pattern=[[1, 128]], compare_op=ALU.is_ge,
                                            fill=0.0, base=0, channel_multiplier=-1), reads=["ones_bf"], writes=["caus_bf"])
        A("pool", lambda e: e.affine_select(out=caus_f[:, :], in_=ones_f[:, :], pattern=[[1, 128]], compare_op=ALU.is_ge,
                                            fill=0.0, base=0, channel_multiplier=-1), reads=["ones_f"], writes=["caus_f"])
        A("pool", lambda e: e.affine_select(out=lstr_f[:, :], in_=ones_f[:, :], pattern=[[-1, 128]], compare_op=ALU.is_gt,
                                            fill=0.0, base=0, channel_multiplier=1), reads=["ones_f"], writes=["lstr_f"])
        A("pool", lambda e: e.memset(m128[:, :], 1.0), writes=["m128"])
        A("pool", lambda e: e.memset(m128[:, 0:1], 0.0), writes=["m128"])
        A("pool", lambda e: e.memset(m4[:, :, :], 1.0), writes=["m4"])
        A("pool", lambda e: e.memset(m4[:, :, 0:1], 0.0), writes=["m4"])
        A("pool", lambda e: e.memset(ctot[:, :], 0.0), writes=["ctot"])
        for gi, g in enumerate((g_mix, g_mlp, g_pg)):
            A("sp", lambda e, gi=gi, g=g: e.dma_start(out=gains[:, gi, :], in_=g.rearrange("(kc p) -> p kc", p=128),
                                                      allow_slow_non_contiguous=True), writes=["gains"], dma=True)
        A("sp", lambda e: e.dma_start(out=ggla[:, :], in_=g_gla.rearrange("(c p) -> p c", p=128), allow_slow_non_contiguous=True),
          writes=["ggla"], dma=True)
        A("sp", lambda e: e.dma_start(out=nbgk[:, :], in_=b_gk.rearrange("(h p) -> p h", p=128), allow_slow_non_contiguous=True),
          writes=["nbgk"], dma=True)
        A("dve", lambda e: e.tensor_scalar(out=nbgk[:, :], in0=nbgk[:, :], scalar1=-1.0, scalar2=None, op0=ALU.mult),
          reads=["nbgk"], writes=["nbgk"])
        A("sp", lambda e: e.dma_start(out=bfb[:, :], in_=b_f.partition_broadcast(128)), writes=["bfb"], dma=True)
        A("pool", lambda e: e.dma_start(out=wsm[:, :, 0:8], in_=w_in[:, C_FA:C_FA + 8].rearrange("(kc p) c -> p kc c", p=128)),
          writes=["wsm"], dma=True)
        A("pool", lambda e: e.dma_start(out=wsm[:, :, 8:24], in_=w_in[:, C_GL:C_GL + 16].rearrange("(kc p) c -> p kc c", p=128)),
          writes=["wsm"], dma=True)
        A("pool", lambda e: e.dma_start(out=wgk[:, :], in_=w_gk2[:, :]), writes=["wgk"], dma=True)

        def rms_rstd(R, src_key, col):
            A("act", lambda e: e.activation(out=smallf[0:R, col:col + 1], in_=ssq[0:R, col:col + 1], func=AF.Ln, bias=EPS, scale=1.0 / D),
              reads=[("ssq", col)], writes=[("smallf", col)])
            A("act", lambda e: e.activation(out=rstd[0:R, col:col + 1], in_=smallf[0:R, col:col + 1], func=AF.Exp, scale=-0.5),
              reads=[("smallf", col)], writes=[("rstd", col)])

        def norm_to_hT(R, gi):
            A("dve", lambda e: e.memset(ssq[0:R, 0:1], 0.0), writes=[("ssq", 0)])
            A("act", lambda e: e.activation(out=hn[0:R, :], in_=x_sb[0:R, :], func=AF.Square, accum_out=ssq[0:R, 0:1]),
              reads=["x", ("ssq", 0)], writes=["hn", ("ssq", 0)])
            rms_rstd(R, "x", 0)
            A("dve", lambda e: e.tensor_scalar(out=hn[0:R, :], in0=x_sb[0:R, :], scalar1=rstd[0:R, 0:1], scalar2=None, op0=ALU.mult),
              reads=["x", ("rstd", 0)], writes=["hn"])
            for g in range(2):
                b = mmbank()

                def tr(e, g=g, b=b):
                    ins = None
                    for k in range(8):
                        kc = 8 * g + k
                        ins = e.transpose(out=bbf(b)[:, k * 128:k * 128 + R], in_=hn[0:R, kc * 128:(kc + 1) * 128], identity=ident[0:R, 0:R])
                    return ins
                A("pe", tr, reads=["hn", "ident"], writes=[BK(b)])
                A("dve", lambda e, g=g, b=b: e.tensor_tensor(
                    out=hT[:, 8 * g:8 * g + 8, 0:R], in0=bbf(b).rearrange("p (k t) -> p k t", k=8)[:, :, 0:R],
                    in1=bc_last(gains[:, gi, 8 * g:8 * g + 8], R), op=ALU.mult),
                  reads=[BK(b), "gains"], writes=["hT"])

        def fm_group(wt, wk, R, src, srck, ncc=4, kcs=NKC):
            b = mmbank()

            def f(e):
                ins = None
                for cc in range(ncc):
                    for kc in range(kcs):
                        ins = e.matmul(bank[b][:, cc * 128:cc * 128 + R], lhsT=wt[:, kc, cc * 128:(cc + 1) * 128], rhs=src[:, kc, 0:R],
                                       start=(kc == 0), stop=(kc == kcs - 1))
                return ins
            A("pe", f, reads=wk + [srck], writes=[BK(b)])
            return b

        def tm_group(wt, wk, R, src, srck, kcs=NKC, ncols=512):
            b = mmbank()

            def f(e):
                ins = None
                for kc in range(kcs):
                    ins = e.matmul(bank[b][0:R, 0:ncols], lhsT=src[:, kc, 0:R], rhs=wt[:, kc, 0:ncols], start=(kc == 0), stop=(kc == kcs - 1))
                return ins
            A("pe", f, reads=wk + [srck], writes=[BK(b)])
            return b

        def fmview(b, R, n=4):
            return bank[b][:, :].rearrange("p (c t) -> p c t", c=4)[:, 0:n, 0:R]

        def log_sigmoid_neg(R, src_ap, dst_ap, bias_ap, rk, wk_):
            pass

        def tile_body(kind, ti):
            R = 128 if kind == "p" else 64
            xsrc = xp[ti * 128:(ti + 1) * 128, :] if kind == "p" else xs[:, :]
            A("sp", lambda e: e.dma_start(out=x_sb[0:R, :], in_=xsrc), writes=["x"], dma=True)
            psrc = ppi[ti * 128:(ti + 1) * 128, :] if kind == "p" else psi[:, :]
            A("pool", lambda e: e.dma_start(out=p_tm[0:R, :], in_=psrc), writes=["p_tm"], dma=True)
            norm_to_hT(R, 0)

            if STOP[0] <= 1:
                return
            b = mmbank()

            def f_fa(e, b=b):
                ins = None
                for kc in range(NKC):
                    ins = e.matmul(bank[b][0:R, 0:8], lhsT=hT[:, kc, 0:R], rhs=wsm[:, kc, 0:8], start=(kc == 0), stop=(kc == NKC - 1))
                return ins
            A("pe", f_fa, reads=["hT", "wsm"], writes=[BK(b)])
            A("dve", lambda e, b=b: e.tensor_tensor(out=lfx[0:R, :], in0=bank[b][0:R, 0:8], in1=bfb[0:R, :], op=ALU.add),
              reads=[BK(b), "bfb"], writes=["lfx"])
            A("act", lambda e: e.activation(out=lfx[0:R, :], in_=lfx[0:R, :], func=AF.Exp, scale=-1.0), reads=["lfx"], writes=["lfx"])
            A("act", lambda e: e.activation(out=lfx[0:R, :], in_=lfx[0:R, :], func=AF.Ln, bias=1.0), reads=["lfx"], writes=["lfx"])
            A("dve", lambda e: e.tensor_scalar(out=lf[0:R, :], in0=lfx[0:R, :], scalar1=-1.0, scalar2=None, op0=ALU.mult),
              reads=["lfx"], writes=["lf"])
            fdst = fpo[ti * 128:(ti + 1) * 128, :] if kind == "p" else fso[:, :]
            A("sp", lambda e: e.dma_start(out=fdst, in_=lf[0:R, :]), reads=["lf"], writes=["fout"], dma=True)

            b2 = mmbank()

            def f_gl(e):
                ins = None
                for kc in range(NKC):
                    ins = e.matmul(bank[b2][0:16, 0:R], lhsT=wsm[:, kc, 8:24], rhs=hT[:, kc, 0:R], start=(kc == 0), stop=(kc == NKC - 1))
                return ins
            A("pe", f_gl, reads=["hT", "wsm"], writes=[BK(b2)])
            A("act", lambda e: e.activation(out=glT[:, 0:R], in_=bank[b2][0:16, 0:R], func=AF.Copy), reads=[BK(b2)], writes=["glT"])
            b3 = mmbank()

            def f_la(e):
                ins = None
                for h in range(4):
                    ins = e.matmul(bank[b3][:, h * 128:h * 128 + R], lhsT=wgk[:, h * 128:(h + 1) * 128], rhs=glT[:, 0:R], start=True, stop=True)
                return ins
            A("pe", f_la, reads=["glT", "wgk"], writes=[BK(b3)])
            la = tmpf[0][:, :].rearrange("p (h t) -> p h t", h=4)
            for h in range(4):
                A("act", lambda e, h=h: e.activation(out=la[:, h, 0:R], in_=bank[b3][:, h * 128:h * 128 + R], func=AF.Exp,
                                                     bias=nbgk[:, h:h + 1], scale=-1.0),
                  reads=[BK(b3), "nbgk"], writes=[("tmpf", 0)])
                A("act", lambda e, h=h: e.activation(out=la[:, h, 0:R], in_=la[:, h, 0:R], func=AF.Ln, bias=1.0),
                  reads=[("tmpf", 0)], writes=[("tmpf", 0)])
                msk = m128[:, 0:R] if kind == "p" else m4[:, :, :].rearrange("p a b -> p (a b)")
                A("dve", lambda e, h=h, msk=msk: e.tensor_tensor_scan(out=eb[:, h, 0:R], data0=msk, data1=la[:, h, 0:R], initial=0.0,
                                                                      op0=ALU.mult, op1=ALU.add),
                  reads=[("tmpf", 0), "m128", "m4"], writes=[("eb", h)])
                A("act", lambda e, h=h: e.activation(out=enb[:, h, 0:R], in_=eb[:, h, 0:R], func=AF.Exp, scale=1.0 / 16.0),
                  reads=[("eb", h)], writes=[("enb", h)])
                A("act", lambda e, h=h: e.activation(out=eb[:, h, 0:R], in_=eb[:, h, 0:R], func=AF.Exp, scale=-1.0 / 16.0),
                  reads=[("eb", h)], writes=[("eb", h)])

            if STOP[0] <= 2:
                return
            wt, wk = w_next()
            b = fm_group(wt, wk, R, hT, "hT")
            A("dve", lambda e, b=b: e.scalar_tensor_tensor(out=qeT[:, :, 0:R], in0=fmview(b, R), scalar=SC_A, in1=eb[:, :, 0:R],
                                                           op0=ALU.mult, op1=ALU.mult),
              reads=[BK(b)] + [("eb", h) for h in range(4)], writes=["qeT"])
            if STOP[0] == 21:
                return
            wt, wk = w_next()
            b = fm_group(wt, wk, R, hT, "hT")
            A("dve", lambda e, b=b: e.tensor_tensor(out=keT[:, :, 0:R], in0=fmview(b, R), in1=enb[:, :, 0:R], op=ALU.mult),
              reads=[BK(b)] + [("enb", h) for h in range(4)], writes=["keT"])
            if kind == "p":
                A("dve", lambda e: e.tensor_tensor(out=kdT[:, :, 0:R], in0=keT[:, :, 0:R], in1=bc_last(eb[:, :, R - 1], R), op=ALU.mult),
                  reads=["keT"] + [("eb", h) for h in range(4)], writes=["kdT"])
            else:
                for h in range(4):
                    A("dve", lambda e, h=h: e.tensor_tensor(
                        out=kdT[:, h, 0:64].rearrange("p (j i) -> p j i", i=4), in0=keT[:, h, 0:64].rearrange("p (j i) -> p j i", i=4),
                        in1=bc_last(eb[:, h, 0:64].rearrange("p (j i) -> p j i", i=4)[:, :, 3], 4), op=ALU.mult),
                      reads=["keT", ("eb", h)], writes=["kdT"])
            if STOP[0] == 22:
                return
            for j in range(2):
                wt, wk = w_next()
                b = tm_group(wt, wk, R, hT, "hT")
                A("act", lambda e, b=b, j=j: e.activation(out=vb[0:R, j * 512:(j + 1) * 512], in_=bank[b][0:R, :], func=AF.Copy),
                  reads=[BK(b)], writes=["vb"])
            if STOP[0] == 23:
                return
            for j in range(2):
                wt, wk = w_next()
                b = fm_group(wt, wk, R, hT, "hT")
                t1 = tmpf[1][:, :].rearrange("p (c t) -> p c t", c=4)[:, :, 0:R]
                A("act", lambda e, b=b, t1=t1: e.activation(out=t1, in_=fmview(b, R), func=AF.Exp, scale=-1.0), reads=[BK(b)], writes=[("tmpf", 1)])
                A("dve", lambda e, t1=t1: e.tensor_scalar(out=t1, in0=t1, scalar1=1.0, scalar2=None, op0=ALU.add), reads=[("tmpf", 1)], writes=[("tmpf", 1)])
                A("dve", lambda e, t1=t1: e.reciprocal(out=t1, in_=t1), reads=[("tmpf", 1)], writes=[("tmpf", 1)])
                A("dve", lambda e, b=b, j=j, t1=t1: e.tensor_tensor(out=srT[:, 4 * j:4 * j + 4, 0:R], in0=fmview(b, R), in1=t1, op=ALU.mult),
                  reads=[BK(b), ("tmpf", 1)], writes=["srT"])
            if STOP[0] == 24:
                return
            for j in range(2):
                wt, wk = w_next()
                b = fm_group(wt, wk, R, hT, "hT")
                A("act", lambda e, b=b, j=j: e.activation(out=qT[:, 4 * j:4 * j + 4, 0:R], in_=fmview(b, R), func=AF.Copy),
                  reads=[BK(b)], writes=["qT"])
            if STOP[0] == 25:
                return
            knew = sb_knew
            for j in range(2):
                wt, wk = w_next()
                b = fm_group(wt, wk, R, hT, "hT")
                if kind == "p":
                    A("act", lambda e, b=b, j=j: e.activation(out=KT[:, 4 * j:4 * j + 4, ti * 128:(ti + 1) * 128], in_=fmview(b, R), func=AF.Copy),
                      reads=[BK(b)], writes=[("KT", ti)])
                else:
                    A("act", lambda e, b=b, j=j: e.activation(out=knew[:, 4 * j:4 * j + 4, 0:R], in_=fmview(b, R), func=AF.Copy),
                      reads=[BK(b), "fence"], writes=["knew"])
                b = tm_group(wt, wk, R, hT, "hT")
                A("dve", lambda e, b=b, j=j: e.tensor_copy(out=e_sb[0:R, j * 512:(j + 1) * 512], in_=bank[b][0:R, :]),
                  reads=[BK(b)], writes=["e_sb"])
            kdst = kpo[ti * 128:(ti + 1) * 128, :] if kind == "p" else kso[:, :]
            A("sp", lambda e: e.dma_start(out=kdst, in_=e_sb[0:R, 0:1024]), reads=["e_sb"], writes=["kout"], dma=True)
            if STOP[0] == 26:
                return
            if STOP[0] == 27:
                w_issue_upto(14)
                return
            for j in range(2):
                wt, wk = w_next()
                b = tm_group(wt, wk, R, hT, "hT")
                A("dve", lambda e, b=b, j=j: e.tensor_copy(out=e_sb[0:R, 1024 + j * 512:1024 + (j + 1) * 512], in_=bank[b][0:R, :]),
                  reads=[BK(b)], writes=["e_sb"])
                vdst_sb = Vt[0:R, ti, j * 512:(j + 1) * 512] if kind == "p" else vnew[0:R, j * 512:(j + 1) * 512]
                if STOP[0] != 28:
                    A("act", lambda e, j=j, v=vdst_sb: e.activation(out=v, in_=e_sb[0:R, 1024 + j * 512:1024 + (j + 1) * 512], func=AF.Copy),
                      reads=["e_sb"] + ([] if kind == "p" else ["fence"]), writes=[("Vt", ti) if kind == "p" else "vnew"])
            vdst = vpo[ti * 128:(ti + 1) * 128, :] if kind == "p" else vso[:, :]
            if STOP[0] != 29:
                A("sp", lambda e: e.dma_start(out=vdst, in_=e_sb[0:R, 1024:2048]), reads=["e_sb"], writes=["vout"], dma=True)
            if STOP[0] in (28, 29):
                return

            if STOP[0] <= 3:
                return
            if kind == "p":
                gla_prompt(ti)
            else:
                gla_sample()
            if STOP[0] <= 4:
                return
            if kind == "p":
                fox_prompt(ti)
            else:
                fox_sample()

            if STOP[0] <= 5:
                return
            for j in range(4):
                wt, wk = w_next()
                b = tm_group(wt, wk, R, mixT, "mixT")
                A("dve", lambda e, b=b, j=j: e.tensor_tensor(out=x_sb[0:R, j * 512:(j + 1) * 512], in0=x_sb[0:R, j * 512:(j + 1) * 512],
                                                             in1=bank[b][0:R, :], op=ALU.add),
                  reads=[BK(b), "x"], writes=["x"])
            if STOP[0] <= 6:
                return
            norm_to_hT(R, 1)
            for j in range(16):
                wt, wk = w_next()
                b = fm_group(wt, wk, R, hT, "hT")
                tr_ = tmpf[j % 2][:, :].rearrange("p (c t) -> p c t", c=4)[:, :, 0:R]
                A("act", lambda e, b=b, tr_=tr_: e.activation(out=tr_, in_=fmview(b, R), func=AF.Relu), reads=[BK(b)], writes=[("tmpf", j % 2)])
                A("dve", lambda e, j=j, tr_=tr_: e.tensor_tensor(out=hidT[:, 4 * j:4 * j + 4, 0:R], in0=tr_, in1=tr_, op=ALU.mult),
                  reads=[("tmpf", j % 2)], writes=["hidT"])
            for cg in range(4):
                b = 4 + (cg % 2)
                for fg in range(4):
                    wt, wk = w_next()

                    def f(e, wt=wt, fg=fg, b=b):
                        ins = None
                        for kc in range(NKC):
                            ins = e.matmul(bank[b][0:R, :], lhsT=hidT[:, fg * 16 + kc, 0:R], rhs=wt[:, kc, :],
                                           start=(fg == 0 and kc == 0), stop=(fg == 3 and kc == NKC - 1))
                        return ins
                    A("pe", f, reads=wk + ["hidT"], writes=[BK(b)])
                A("dve", lambda e, b=b, cg=cg: e.tensor_tensor(out=x_sb[0:R, cg * 512:(cg + 1) * 512], in0=x_sb[0:R, cg * 512:(cg + 1) * 512],
                                                               in1=bank[b][0:R, :], op=ALU.add),
                  reads=[BK(b), "x"], writes=["x"])
            if STOP[0] <= 7:
                return
            b = mmbank()

            def trp(e, b=b):
                ins = None
                for k in range(2):
                    ins = e.transpose(out=bbf(b)[:, k * 128:k * 128 + R], in_=p_tm[0:R, k * 128:(k + 1) * 128], identity=ident[0:R, 0:R])
                return ins
            A("pe", trp, reads=["p_tm", "ident"], writes=[BK(b)])
            A("act", lambda e, b=b: e.activation(out=pT[:, :, 0:R], in_=bbf(b).rearrange("p (k t) -> p k t", k=8)[:, 0:2, 0:R], func=AF.Copy),
              reads=[BK(b)], writes=["pT"])
            wt, wk = w_next()
            wple = wt[:, 0:8, :].rearrange("p (k a) c -> p k (a c)", k=2)
            for cg in range(4):
                b = mmbank()

                def f(e, b=b, cg=cg, wple=wple):
                    ins = None
                    for k in range(2):
                        ins = e.matmul(bank[b][0:R, :], lhsT=pT[:, k, 0:R], rhs=wple[:, k, cg * 512:(cg + 1) * 512], start=(k == 0), stop=(k == 1))
                    return ins
                A("pe", f, reads=wk + ["pT"], writes=[BK(b)])
                A("dve", lambda e, b=b, cg=cg: e.tensor_copy(out=e_sb[0:R, cg * 512:(cg + 1) * 512], in_=bank[b][0:R, :]),
                  reads=[BK(b)], writes=["e_sb"])
            A("dve", lambda e: e.memset(ssq[0:R, 1:2], 0.0), writes=[("ssq", 1)])
            A("act", lambda e: e.activation(out=hn[0:R, :], in_=e_sb[0:R, :], func=AF.Square, accum_out=ssq[0:R, 1:2]),
              reads=["e_sb", ("ssq", 1)], writes=["hn", ("ssq", 1)])
            rms_rstd(R, "e", 1)
            A("sp", lambda e: e.dma_start(out=gbc[:, :], in_=g_ple.partition_broadcast(128)), writes=["gbc"], dma=True)
            A("dve", lambda e: e.scalar_tensor_tensor(out=e_sb[0:R, :], in0=e_sb[0:R, :], scalar=rstd[0:R, 1:2], in1=gbc[0:R, :],
                                                      op0=ALU.mult, op1=ALU.mult),
              reads=["e_sb", ("rstd", 1), "gbc"], writes=["e_sb"])
            A("sp", lambda e: e.dma_start(out=gbc[:, :], in_=g_fin.partition_broadcast(128)), writes=["gbc"], dma=True)
            norm_to_hT(R, 2)
            for j in range(4):
                wt, wk = w_next()
                b = tm_group(wt, wk, R, hT, "hT")
                t2 = tmpf[2][0:R, :]
                A("act", lambda e, b=b, t2=t2: e.activation(out=t2, in_=bank[b][0:R, :], func=AF.Exp, scale=-1.0), reads=[BK(b)], writes=[("tmpf", 2)])
                A("dve", lambda e, t2=t2: e.tensor_scalar(out=t2, in0=t2, scalar1=1.0, scalar2=None, op0=ALU.add), reads=[("tmpf", 2)], writes=[("tmpf", 2)])
                A("dve", lambda e, t2=t2: e.reciprocal(out=t2, in_=t2), reads=[("tmpf", 2)], writes=[("tmpf", 2)])
                A("dve", lambda e, t2=t2, j=j: e.tensor_tensor(out=t2, in0=t2, in1=e_sb[0:R, j * 512:(j + 1) * 512], op=ALU.mult),
                  reads=[("tmpf", 2), "e_sb"], writes=[("tmpf", 2)])
                A("dve", lambda e, t2=t2, j=j: e.tensor_tensor(out=x_sb[0:R, j * 512:(j + 1) * 512], in0=x_sb[0:R, j * 512:(j + 1) * 512], in1=t2,
                                                               op=ALU.add),
                  reads=[("tmpf", 2), "x"], writes=["x"])
            if STOP[0] <= 8:
                return
            A("dve", lambda e: e.memset(ssq[0:R, 2:3], 0.0), writes=[("ssq", 2)])
            A("act", lambda e: e.activation(out=hn[0:R, :], in_=x_sb[0:R, :], func=AF.Square, accum_out=ssq[0:R, 2:3]),
              reads=["x", ("ssq", 2)], writes=["hn", ("ssq", 2)])
            rms_rstd(R, "x", 2)
            A("dve", lambda e: e.scalar_tensor_tensor(out=x_sb[0:R, :], in0=x_sb[0:R, :], scalar=rstd[0:R, 2:3], in1=gbc[0:R, :],
                                                      op0=ALU.mult, op1=ALU.mult),
              reads=["x", ("rstd", 2), "gbc"], writes=["x"])
            ydst = yp[ti * 128:(ti + 1) * 128, :] if kind == "p" else ys[:, :]
            A("sp", lambda e: e.dma_start(out=ydst, in_=x_sb[0:R, :]), reads=["x"], writes=["yout"], dma=True)

        def gla_out_norm(R):
            pass

        def gla_finish(R, ob):
            for c in range(2):
                A("act", lambda e, c=c: e.activation(out=hn[:, c * 512:(c + 1) * 512].rearrange("p (h t) -> p h t", h=4)[:, :, 0:R],
                                                     in_=fmview(ob[c], R), func=AF.Square),
                  reads=[BK(ob[c])], writes=["hn"])
            b = mmbank()

            def f(e, b=b):
                ins = None
                for h in range(4):
                    for c in range(2):
                        ins = e.matmul(bank[b][:, h * 128:h * 128 + R], lhsT=ones_bf[:, :], rhs=hn[:, c * 512 + h * 128:c * 512 + h * 128 + R],
                                       start=(c == 0), stop=(c == 1))
                return ins
            A("pe", f, reads=["hn", "ones_bf"], writes=[BK(b)])
            t3 = tmpf[3][:, :].rearrange("p (h t) -> p h t", h=4)[:, :, 0:R]
            A("act", lambda e, b=b: e.activation(out=t3, in_=fmview(b, R), func=AF.Ln, bias=EPS, scale=1.0 / 256.0), reads=[BK(b)], writes=[("tmpf", 3)])
            A("act", lambda e: e.activation(out=t3, in_=t3, func=AF.Exp, scale=-0.5), reads=[("tmpf", 3)], writes=[("tmpf", 3)])
            for c in range(2):
                t1 = tmpf[1][:, :].rearrange("p (h t) -> p h t", h=4)[:, :, 0:R]
                A("dve", lambda e, c=c, t1=t1: e.scalar_tensor_tensor(out=t1, in0=fmview(ob[c], R), scalar=ggla[:, c:c + 1], in1=t3,
                                                                      op0=ALU.mult, op1=ALU.mult),
                  reads=[BK(ob[c]), ("tmpf", 3), "ggla"], writes=[("tmpf", 1)])
                A("dve", lambda e, c=c, t1=t1: e.tensor_tensor(
                    out=mixT[:, 8:16, 0:R].rearrange("p (h c) t -> p h c t", c=2)[:, :, c, :], in0=t1,
                    in1=srT[:, :, 0:R].rearrange("p (h c) t -> p h c t", c=2)[:, :, c, :], op=ALU.mult),
                  reads=[("tmpf", 1), "srT"], writes=["mixT"])

        def kd_transpose(R):
            b = mmbank()

            def f(e, b=b):
                ins = None
                for h in range(4):
                    ins = e.transpose(out=bbf(b)[0:R, h * 128:(h + 1) * 128], in_=kdT[:, h, 0:R], identity=ident[:, :])
                return ins
            A("pe", f, reads=["kdT", "ident"], writes=[BK(b)])
            A("act", lambda e, b=b: e.activation(out=kdTok[0:R, :, :], in_=bbf(b)[0:R, 0:512].rearrange("p (h k) -> p h k", h=4), func=AF.Copy),
              reads=[BK(b)], writes=["kdTok"])

        def gla_prompt(ti):
            R = 128
            kd_transpose(R)
            ob = [6, 7]
            for h in range(4):
                ab = 4 + (h % 2)
                A("pe", lambda e, h=h, ab=ab: e.matmul(bank[ab][:, 0:128], lhsT=keT[:, h, :], rhs=qeT[:, h, :], start=True, stop=True),
                  reads=["keT", "qeT"], writes=[BK(ab)])
                A("dve", lambda e, h=h, ab=ab: e.tensor_tensor(out=ATm[h % 2][:, :], in0=bank[ab][:, 0:128], in1=caus_bf[:, :], op=ALU.mult),
                  reads=[BK(ab), "caus_bf"], writes=[("ATm", h % 2)])

                def fo(e, h=h):
                    ins = None
                    for c in range(2):
                        o = bank[ob[c]][:, h * 128:(h + 1) * 128]
                        if ti > 0:
                            e.matmul(o, lhsT=Sbf[:, h, c * 128:(c + 1) * 128], rhs=qeT[:, h, :], start=True, stop=False)
                        ins = e.matmul(o, lhsT=vb[:, h * 256 + c * 128:h * 256 + (c + 1) * 128], rhs=ATm[h % 2][:, :], start=(ti == 0), stop=True)
                    return ins
                A("pe", fo, reads=[("Sbf", h), "qeT", "vb", ("ATm", h % 2)], writes=[BK(6), BK(7)])
                b = mmbank()
                A("pe", lambda e, h=h, b=b: e.matmul(bank[b][:, 0:256], lhsT=kdTok[:, h, :], rhs=vb[:, h * 256:(h + 1) * 256], start=True, stop=True),
                  reads=["kdTok", "vb"], writes=[BK(b)])
                if ti == 0:
                    A("dve", lambda e, h=h, b=b: e.tensor_copy(out=Sst[:, h, :], in_=bank[b][:, 0:256]), reads=[BK(b)], writes=[("Sst", h)])
                else:
                    A("dve", lambda e, h=h, b=b: e.scalar_tensor_tensor(out=Sst[:, h, :], in0=Sst[:, h, :], scalar=eb[:, h, 127:128],
                                                                        in1=bank[b][:, 0:256], op0=ALU.mult, op1=ALU.add),
                      reads=[BK(b), ("Sst", h), ("eb", h)], writes=[("Sst", h)])
                A("act", lambda e, h=h: e.activation(out=Sbf[:, h, :], in_=Sst[:, h, :], func=AF.Copy), reads=[("Sst", h)], writes=[("Sbf", h)])
            gla_finish(R, ob)
            if ti == n_ptiles - 1:
                A("sp", lambda e: e.dma_start(out=spo.rearrange("h k v -> k h v"), in_=Sst[:, :, :]),
                  reads=[("Sst", h) for h in range(4)], writes=["spo"], dma=True)

        def fox_prompt(ti):
            R = 128
            nblk = ti + 1
            b = mmbank()
            A("pe", lambda e, b=b: e.matmul(bank[b][:, 0:8], lhsT=caus_f[:, :], rhs=lf[:, :], start=True, stop=True),
              reads=["caus_f", "lf"], writes=[BK(b)])
            A("dve", lambda e, b=b: e.tensor_tensor(out=cK[:, ti, :], in0=bank[b][:, 0:8], in1=ctot[:, :], op=ALU.add),
              reads=[BK(b), "ctot"], writes=["cK"])
            A("dve", lambda e: e.tensor_tensor(out=biasT[:, 0:nblk, :], in0=bc_mid(ctot[:, :], nblk), in1=cK[:, 0:nblk, :], op=ALU.subtract),
              reads=["cK", "ctot"], writes=["biasT"])
            b2 = mmbank()
            A("pe", lambda e, b2=b2: e.matmul(bank[b2][:, 0:8], lhsT=ones_f[:, :], rhs=lf[:, :], start=True, stop=True),
              reads=["ones_f", "lf"], writes=[BK(b2)])
            A("dve", lambda e, b2=b2: e.tensor_tensor(out=ctot[:, :], in0=ctot[:, :], in1=bank[b2][:, 0:8], op=ALU.add),
              reads=[BK(b2), "ctot"], writes=["ctot"])
            n = 0
            for h in range(8):
                for kb in range(nblk):
                    sbk = 4 + (n % 2)
                    pt = PTb[n % 2]
                    n += 1
                    A("pe", lambda e, h=h, kb=kb, sbk=sbk: e.matmul(bank[sbk][:, 0:128], lhsT=KT[:, h, kb * 128:(kb + 1) * 128], rhs=qT[:, h, :],
                                                                    start=True, stop=True),
                      reads=[("KT", kb), "qT"], writes=[BK(sbk)])
                    A("act", lambda e, h=h, kb=kb, sbk=sbk, pt=pt: e.activation(out=pt[:, 0:128], in_=bank[sbk][:, 0:128], func=AF.Exp,
                                                                                bias=biasT[:, kb, h:h + 1], scale=SC_A),
                      reads=[BK(sbk), "biasT"], writes=[("PT", id(pt))])
                    if kb == ti:
                        A("dve", lambda e, pt=pt: e.tensor_tensor(out=pt[:, 0:128], in0=pt[:, 0:128], in1=caus_bf[:, :], op=ALU.mult),
                          reads=[("PT", id(pt)), "caus_bf"], writes=[("PT", id(pt))])

                    def fpv(e, h=h, kb=kb, pt=pt):
                        e.matmul(bank[6][:, 0:128], lhsT=Vt[:, kb, h * 128:(h + 1) * 128],
                                 rhs=pt[:, 0:128], start=(kb == 0), stop=(kb == nblk - 1))
                        return e.matmul(bank[7][:, 0:128], lhsT=ones_bf[:, :], rhs=pt[:, 0:128], start=(kb == 0), stop=(kb == nblk - 1))
                    A("pe", fpv, reads=[("Vt", kb), ("PT", id(pt)), "ones_bf"], writes=[BK(6), BK(7)])
                A("dve", lambda e: e.reciprocal(out=tmpf[2][:, 0:128], in_=bank[7][:, 0:128]), reads=[BK(7)], writes=[("tmpf", 2)])
                A("dve", lambda e, h=h: e.tensor_tensor(out=mixT[:, h, :], in0=bank[6][:, 0:128], in1=tmpf[2][:, 0:128], op=ALU.mult),
                  reads=[BK(6), ("tmpf", 2)], writes=["mixT"])

        sb_knew = Vflat[:, 5120:5632].rearrange("p (a b) -> p a b", a=8)
        vnew = Vflat[0:64, 4096:5120]

        def gla_sample():
            R = 64
            kd_transpose(R)
            ob = [6, 7]
            for j in range(16):
                s = j % 2
                A("sp", lambda e, j=j, s=s: e.dma_start(out=S0f[s], in_=sgl[j].rearrange("h k v -> k h v")), reads=["fence"], writes=[("S0f", s), "e_sb"], dma=True)
                A("pool", lambda e, j=j, s=s: e.dma_start(out=S0b[s], in_=sgl[j].rearrange("h k v -> k h v")), reads=["fence"], writes=[("S0b", s)], dma=True)
                if j == 0:
                    for h in range(4):
                        ab = 4 + (h % 2)
                        A("pe", lambda e, h=h, ab=ab: e.matmul(bank[ab][0:64, 0:64], lhsT=keT[:, h, 0:64], rhs=qeT[:, h, 0:64], start=True, stop=True),
                          reads=["keT", "qeT"], writes=[BK(ab)])
                        A("dve", lambda e, h=h, ab=ab: e.tensor_tensor(out=ATs[0:64, h, :], in0=bank[ab][0:64, 0:64], in1=blk4_bf[:, :], op=ALU.mult),
                          reads=[BK(ab), "blk4_bf"], writes=["ATs"])

                    def fo(e):
                        ins = None
                        for c in range(2):
                            e.matmul(bank[ob[c]][:, :], lhsT=zeros_bf[:, 0:128], rhs=zeros_bf[:, :], start=True, stop=False)
                        for h in range(4):
                            for c in range(2):
                                ins = e.matmul(bank[ob[c]][:, h * 128:h * 128 + 64], lhsT=vb[0:64, h * 256 + c * 128:h * 256 + (c + 1) * 128],
                                               rhs=ATs[0:64, h, :], start=False, stop=False)
                        return ins
                    A("pe", fo, reads=["vb", "ATs", "zeros_bf"], writes=[BK(6), BK(7)])

                def fi(e, j=j, s=s):
                    ins = None
                    for h in range(4):
                        for c in range(2):
                            ins = e.matmul(bank[ob[c]][:, h * 128 + 4 * j:h * 128 + 4 * j + 4], lhsT=S0b[s][:, h, c * 128:(c + 1) * 128],
                                           rhs=qeT[:, h, 4 * j:4 * j + 4], start=False, stop=(j == 15 and h == 3))
                    return ins
                A("pe", fi, reads=[("S0b", s), "qeT"], writes=[BK(6), BK(7)])
                A("dve", lambda e, j=j, s=s: e.tensor_scalar(out=VMj[s], in0=vb[0:64, :], scalar1=selT[:, j:j + 1], scalar2=None, op0=ALU.mult),
                  reads=["vb", "selT", "fence"], writes=[("VMj", s)])
                for h in range(4):
                    b = mmbank()
                    A("pe", lambda e, h=h, b=b, s=s: e.matmul(bank[b][:, 0:256], lhsT=kdTok[0:64, h, :], rhs=VMj[s][:, h * 256:(h + 1) * 256],
                                                              start=True, stop=True),
                      reads=["kdTok", ("VMj", s)], writes=[BK(b)])
                    A("dve", lambda e, h=h, b=b, s=s, j=j: e.scalar_tensor_tensor(out=Sout[s][:, h, :], in0=S0f[s][:, h, :],
                                                                                  scalar=eb[:, h, 4 * j + 3:4 * j + 4], in1=bank[b][:, 0:256],
                                                                                  op0=ALU.mult, op1=ALU.add),
                      reads=[BK(b), ("S0f", s), ("eb", h), "fence"], writes=[("Sout", s), ("tmpf", 2 * s), ("tmpf", 2 * s + 1)])
                A("sp", lambda e, j=j, s=s: e.dma_start(out=sso[j].rearrange("h k v -> k h v"), in_=Sout[s]),
                  reads=[("Sout", s), ("tmpf", 2 * s), ("tmpf", 2 * s + 1)], writes=[("sso", j)], dma=True)
            gla_finish(R, ob)

        ATs = sb("ATs", [64, 4, 64], BF16)

        def fox_sample():
            R = 64
            A("sp", lambda e: e.dma_start(out=idx[:, :], in_=ptab.partition_broadcast(128)), writes=["idx"], dma=True)
            A("pool", lambda e: e.iota(iot[:, :], [[0, 1]], base=0, channel_multiplier=1, allow_small_or_imprecise_dtypes=True), writes=["iot"])
            A("dve", lambda e: e.tensor_copy(out=idxf[:, :], in_=idx[:, :]), reads=["idx"], writes=["idxf"])
            A("dve", lambda e: e.tensor_scalar(out=idxf[:, :], in0=idxf[:, :], scalar1=128.0, scalar2=iot[:, 0:1], op0=ALU.mult, op1=ALU.add),
              reads=["idxf", "iot"], writes=["idxf"])
            A("dve", lambda e: e.tensor_copy(out=idx[:, :], in_=idxf[:, :]), reads=["idxf"], writes=["idx"])
            A("pool", lambda e: e.affine_select(out=Emat[:, :], in_=ones_bf[0:16, 0:64], pattern=[[1, 64]], compare_op=ALU.is_ge, fill=0.0,
                                                base=0, channel_multiplier=-4), reads=["ones_bf"], writes=["Emat"])
            A("pool", lambda e: e.affine_select(out=Emat[:, :], in_=Emat[:, :], pattern=[[-1, 64]], compare_op=ALU.is_ge, fill=0.0,
                                                base=3, channel_multiplier=4), reads=["Emat"], writes=["Emat"])
            A("pool", lambda e: e.affine_select(out=selT[:, :], in_=ones_f[0:64, 0:16], pattern=[[-4, 16]], compare_op=ALU.is_ge, fill=0.0,
                                                base=0, channel_multiplier=1), reads=["ones_f"], writes=["selT"])
            A("pool", lambda e: e.affine_select(out=selT[:, :], in_=selT[:, :], pattern=[[4, 16]], compare_op=ALU.is_ge, fill=0.0,
                                                base=3, channel_multiplier=-1), reads=["selT"], writes=["selT"])
            b = mmbank()
            A("pe", lambda e, b=b: e.matmul(bank[b][0:64, 0:64], lhsT=Emat[:, :], rhs=Emat[:, :], start=True, stop=True), reads=["Emat"], writes=[BK(b)])
            A("dve", lambda e, b=b: e.tensor_tensor(out=blk4_bf[:, :], in0=bank[b][0:64, 0:64], in1=caus_bf[0:64, 0:64], op=ALU.mult),
              reads=[BK(b), "caus_bf"], writes=["blk4_bf"])
            A("dve", lambda e, b=b: e.tensor_tensor(out=blk4_f[:, :], in0=bank[b][0:64, 0:64], in1=caus_f[0:64, 0:64], op=ALU.mult),
              reads=[BK(b), "caus_f"], writes=["blk4_f"])

        fox_sample_masks = fox_sample

        def fox_sample_main():
            R = 64
            b = mmbank()
            A("pe", lambda e, b=b: e.matmul(bank[b][0:64, 0:8], lhsT=blk4_f[:, :], rhs=lf[0:64, :], start=True, stop=True),
              reads=["blk4_f", "lf"], writes=[BK(b)])
            A("dve", lambda e, b=b: e.tensor_scalar(out=cn[:, :], in0=bank[b][0:64, 0:8], scalar1=-1.0, scalar2=None, op0=ALU.mult),
              reads=[BK(b)], writes=["cn"])
            for h in range(8):
                sbk = 4 + (h % 2)
                pt = PTb[h % 2]
                A("pe", lambda e, h=h, sbk=sbk: e.matmul(bank[sbk][0:64, 0:64], lhsT=sb_knew[:, h, :], rhs=qT[:, h, 0:64], start=True, stop=True),
                  reads=["knew", "qT"], writes=[BK(sbk)])
                A("act", lambda e, h=h, sbk=sbk, pt=pt: e.activation(out=pt[0:64, 0:64], in_=bank[sbk][0:64, 0:64], func=AF.Exp,
                                                                     bias=cn[:, h:h + 1], scale=SC_A),
                  reads=[BK(sbk), "cn"], writes=[("PT", id(pt))])
                A("dve", lambda e, pt=pt: e.tensor_tensor(out=pt[0:64, 0:64], in0=pt[0:64, 0:64], in1=blk4_bf[:, :], op=ALU.mult),
                  reads=[("PT", id(pt)), "blk4_bf"], writes=[("PT", id(pt))])

                def fpv(e, h=h, pt=pt):
                    if h == 0:
                        for bb_ in (6, 7):
                            e.matmul(bank[bb_][:, :], lhsT=zeros_bf[:, 0:128], rhs=zeros_bf[:, :], start=True, stop=False)
                    e.matmul(bank[6][:, h * 64:(h + 1) * 64], lhsT=vnew[0:64, h * 128:(h + 1) * 128], rhs=pt[0:64, 0:64], start=False, stop=False)
                    return e.matmul(bank[7][:, h * 64:(h + 1) * 64], lhsT=ones_bf[0:64, :], rhs=pt[0:64, 0:64], start=False, stop=False)
                A("pe", fpv, reads=["vnew", ("PT", id(pt)), "ones_bf", "zeros_bf"], writes=[BK(6), BK(7)])
            for j in range(16):
                ls = j % 2
                for pg in range(16):
                    col = j * 16 + pg
                    A("pool", lambda e, ls=ls, pg=pg, col=col: e.indirect_dma_start(
                        out=LFp[ls][:, pg, :], out_offset=None, in_=clf,
                        in_offset=bass.IndirectOffsetOnAxis(ap=idx[:, col:col + 1], axis=0)),
                      reads=["idx"], writes=[("LFp", ls)], dma=True)
                b = mmbank()

                def fb(e, b=b, ls=ls):
                    lfv = LFp[ls][:, :, :].rearrange("p a b -> p (a b)")
                    ins = e.matmul(bank[b][:, 0:128], lhsT=lstr_f[:, :], rhs=lfv, start=True, stop=False)
                    for k in range(1, 16):
                        ins = e.matmul(bank[b][:, 0:128 - 8 * k], lhsT=ones_f[:, :], rhs=lfv[:, 8 * k:128], start=False, stop=(k == 15))
                    return ins
                A("pe", fb, reads=[("LFp", ls), "lstr_f", "ones_f"], writes=[BK(b)])
                A("act", lambda e, b=b: e.activation(out=bias_s[:, :, :].rearrange("p a b -> p (a b)"), in_=bank[b][:, 0:128], func=AF.Copy),
                  reads=[BK(b)], writes=["bias_s"])
                for pg in range(16):
                    col = j * 16 + pg
                    s = col % 4
                    sbk = 4 + (col % 2)
                    A("pool", lambda e, s=s, col=col: e.indirect_dma_start(
                        out=Kp[s], out_offset=None, in_=ck, in_offset=bass.IndirectOffsetOnAxis(ap=idx[:, col:col + 1], axis=0)),
                      reads=["idx", "fence"], writes=[("Kp", s)], dma=True)
                    A("pool", lambda e, s=s, col=col: e.indirect_dma_start(
                        out=Vp[s], out_offset=None, in_=cv, in_offset=bass.IndirectOffsetOnAxis(ap=idx[:, col:col + 1], axis=0)),
                      reads=["idx", "fence"], writes=[("Vp", s)], dma=True)
                    tb = mmbank()

                    def ftr(e, s=s, tb=tb):
                        ins = None
                        for h in range(8):
                            ins = e.transpose(out=bbf(tb)[:, h * 128:(h + 1) * 128], in_=Kp[s][:, h * 128:(h + 1) * 128], identity=ident[:, :])
                        return ins
                    A("pe", ftr, reads=[("Kp", s), "ident"], writes=[BK(tb)])
                    if col % 2 == 0:
                        A("act", lambda e, s=s, tb=tb: e.activation(out=KpT[s].rearrange("p a b -> p (a b)"), in_=bbf(tb), func=AF.Copy),
                          reads=[BK(tb), "fence"], writes=[("KpT", s)])
                    else:
                        A("dve", lambda e, s=s, tb=tb: e.tensor_copy(out=KpT[s].rearrange("p a b -> p (a b)"), in_=bbf(tb)),
                          reads=[BK(tb), "fence"], writes=[("KpT", s)])

                    def fsc(e, s=s, pg=pg, j=j, sbk=sbk):
                        ins = None
                        for h in range(8):
                            o = bank[sbk][:, pg * 32 + h * 4:pg * 32 + h * 4 + 4]
                            ins = e.matmul(o, lhsT=KpT[s][:, h, :], rhs=qT[:, h, 4 * j:4 * j + 4], start=True, stop=True)
                        return ins
                    A("pe", fsc, reads=[("KpT", s), "qT"], writes=[BK(sbk)])
                    tq = tmpf[col % 2]
                    A("dve", lambda e, pg=pg, sbk=sbk, tq=tq: e.scalar_tensor_tensor(
                        out=tq[:, 0:32].rearrange("p (h q) -> p h q", q=4), in0=bank[sbk][:, pg * 32:(pg + 1) * 32].rearrange("p (h q) -> p h q", q=4),
                        scalar=SC_A, in1=bc_last(bias_s[:, pg, :], 4), op0=ALU.mult, op1=ALU.add),
                      reads=[BK(sbk), "bias_s"], writes=[("tmpf", col % 2)])
                    pt = PTb[col % 2]
                    A("act", lambda e, tq=tq, pt=pt: e.activation(out=pt[:, 0:32], in_=tq[:, 0:32], func=AF.Exp),
                      reads=[("tmpf", col % 2)], writes=[("PT", id(pt))])

                    def fpv(e, s=s, j=j, pt=pt, last=(pg == 15 and j == 15)):
                        for h in range(8):
                            e.matmul(bank[6][:, h * 64 + 4 * j:h * 64 + 4 * j + 4], lhsT=Vp[s][:, h * 128:(h + 1) * 128], rhs=pt[:, h * 4:h * 4 + 4],
                                     start=False, stop=(last and h == 7))
                        ins = None
                        for h in range(8):
                            ins = e.matmul(bank[7][:, h * 64 + 4 * j:h * 64 + 4 * j + 4], lhsT=ones_bf[:, :], rhs=pt[:, h * 4:h * 4 + 4],
                                           start=False, stop=(last and h == 7))
                        return ins
                    A("pe", fpv, reads=[("Vp", s), ("PT", id(pt)), "ones_bf"], writes=[BK(6), BK(7)])
            A("dve", lambda e: e.reciprocal(out=tmpf[2][:, :], in_=bank[7][:, :]), reads=[BK(7)], writes=[("tmpf", 2)])
            A("dve", lambda e: e.tensor_tensor(out=mixT[:, 0:8, 0:64], in0=bank[6][:, :].rearrange("p (h t) -> p h t", h=8),
                                               in1=tmpf[2][:, :].rearrange("p (h t) -> p h t", h=8), op=ALU.mult),
              reads=[BK(6), ("tmpf", 2)], writes=["mixT"])

        if with_sample:
            fox_sample_masks()
        fox_sample = fox_sample_main
        assert len(wq) == NGRP * len(tiles)
        w_prologue()
        for kind, ti in tiles:
            if kind == "s":
                A("pool", lambda e: e.memset(smallf[:, 7:8], 0.0),
                  writes=["fence"] + [("Vt", k) for k in range(16)] + [("KT", k) for k in range(16)])
            tile_body(kind, ti)
        print('SBUF bytes remaining', nc.sbuf_bytes_remaining)
        P.emit()
    return nc


_CACHE = {}


def kernel(x_prompt, x_sample, cache_k, cache_v, cache_logf, state_gla, page_table, p_prompt, p_sample,
           g_mix, w_in, b_f, w_gk2, b_gk, g_gla_out, w_out, g_mlp, w_up, w_down, w_ple, g_ple, g_ple_gate,
           w_ple_gate, g_final):
    f32 = lambda a: np.ascontiguousarray(np.asarray(a, dtype=np.float32))
    n = 8
    if "nc" not in _CACHE:
        nc = bass.Bass("TRN2", target_bir_lowering=False)
        build(nc)
        _CACHE["nc"] = nc
    nc = _CACHE["nc"]
    ck = f32(cache_k).reshape(N_POOL_ROWS, 1024)
    cv = f32(cache_v).reshape(N_POOL_ROWS, 1024)
    clf = f32(cache_logf).reshape(N_POOL_ROWS, 8)
    shared = {
        "ck": ck, "cv": cv, "clf": clf,
        "g_mix": f32(g_mix).reshape(D), "w_in": f32(w_in).reshape(D, IN_COLS), "b_f": f32(b_f).reshape(1, 8),
        "w_gk2": f32(w_gk2).reshape(16, 512), "b_gk": f32(b_gk).reshape(512), "g_gla": f32(g_gla_out).reshape(256),
        "w_out": f32(w_out).reshape(D, D), "g_mlp": f32(g_mlp).reshape(D), "w_up": f32(w_up).reshape(D, DFF),
        "w_down": f32(w_down).reshape(DFF, D), "w_ple": f32(w_ple).reshape(256, D), "g_ple": f32(g_ple).reshape(1, D),
        "g_pg": f32(g_ple_gate).reshape(D), "w_pg": f32(w_ple_gate).reshape(D, D), "g_fin": f32(g_final).reshape(1, D),
    }
    xp = f32(x_prompt)
    xs = f32(x_sample)
    pp = f32(p_prompt)[0]
    ps_ = f32(p_sample)[0]
    sg = f32(state_gla)[0]
    pt = np.ascontiguousarray(np.asarray(page_table, dtype=np.int32))
    in_maps = []
    for c in range(n):
        m = dict(shared)
        m["xp"] = xp[c]
        m["xs"] = np.ascontiguousarray(xs[16 * c:16 * c + 16].reshape(64, D))
        m["pp"] = pp[c]
        m["ps"] = np.ascontiguousarray(ps_[16 * c:16 * c + 16].reshape(64, 256))
        m["sgl"] = np.ascontiguousarray(sg[16 * c:16 * c + 16])
        m["ptab"] = np.ascontiguousarray(pt[16 * c:16 * c + 16].reshape(1, 256))
        in_maps.append(m)
    res = run_bass_kernel_spmd(nc, in_maps, core_ids=list(range(n)))
    r = res.results
    g = lambda k: [np.asarray(r[c][k], dtype=np.float32) for c in range(n)]
    y_prompt = np.stack(g("yp"), 0)
    y_sample = np.concatenate([a.reshape(16, 4, D) for a in g("ys")], 0)
    k_prompt = np.stack([a.reshape(2048, 8, 128) for a in g("kpo")], 0)[None]
    v_prompt = np.stack([a.reshape(2048, 8, 128) for a in g("vpo")], 0)[None]
    f_prompt = np.stack(g("fpo"), 0)[None]
    s_prompt = np.stack(g("spo"), 0)[None]
    k_sample = np.concatenate([a.reshape(16, 4, 8, 128) for a in g("kso")], 0)[None]
    v_sample = np.concatenate([a.reshape(16, 4, 8, 128) for a in g("vso")], 0)[None]
    f_sample = np.concatenate([a.reshape(16, 4, 8) for a in g("fso")], 0)[None]
    s_sample = np.concatenate(g("sso"), 0)[None]
    return (y_prompt, y_sample, k_prompt, v_prompt, f_prompt, s_prompt, k_sample, v_sample, f_sample, s_sample)
```

```python
import contextlib
import numpy as np
import concourse.bass as bass
import concourse.mybir as mybir
from concourse.bass_utils import run_bass_kernel_spmd

F32 = mybir.dt.float32
BF16 = mybir.dt.bfloat16
I32 = mybir.dt.int32
AF = mybir.ActivationFunctionType
ALU = mybir.AluOpType

RAW, WAW, WAR = 0, 1, 2
COMPUTE = ("pe", "act", "dve", "pool")
NDMASEM = 8


class Op:
    __slots__ = ("id", "eng", "fn", "deps", "is_dma", "sig", "need_sig", "dsem", "dtarget", "prev_dma")

    def __init__(self, id, eng, fn, is_dma):
        self.id = id
        self.eng = eng
        self.fn = fn
        self.deps = {}
        self.is_dma = is_dma
        self.sig = None
        self.need_sig = False
        self.dsem = None
        self.dtarget = None
        self.prev_dma = None


class Prog:
    def __init__(self, nc):
        self.nc = nc
        self.ops = []
        self.last_w = {}
        self.readers = {}
        self.dma_count = {}
        self.dma_hist = {}

    def add(self, eng, fn, reads=(), writes=(), dma=False):
        op = Op(len(self.ops), eng, fn, dma)
        deps = op.deps
        for k in reads:
            w = self.last_w.get(k)
            if w is not None:
                deps[w] = RAW
        for k in writes:
            w = self.last_w.get(k)
            if w is not None and w not in deps:
                deps[w] = WAW
            for r in self.readers.get(k, ()):
                if r not in deps:
                    deps[r] = WAR
        for k in reads:
            self.readers.setdefault(k, []).append(op.id)
        for k in writes:
            self.last_w[k] = op.id
            self.readers[k] = []
        if dma:
            n = self.dma_count.get(eng, 0)
            self.dma_count[eng] = n + 1
            op.dsem = n % NDMASEM
            op.dtarget = 16 * (n // NDMASEM + 1)
            hist = self.dma_hist.setdefault(eng, [])
            if n >= NDMASEM:
                op.prev_dma = hist[n - NDMASEM]
            hist.append(op.id)
        self.ops.append(op)
        return op.id

    def emit(self):
        nc = self.nc
        ops = self.ops
        for y in ops:
            nd = {}
            for xid, kind in y.deps.items():
                x = ops[xid]
                if x.is_dma:
                    nd[xid] = kind
                    continue
                if x.eng == y.eng and not y.is_dma and (kind == WAR or (kind == WAW and x.eng == "pe")):
                    continue
                nd[xid] = kind
                x.need_sig = True
            y.deps = nd
        cnt = {e: 0 for e in COMPUTE}
        for x in ops:
            if x.need_sig:
                cnt[x.eng] += 1
                x.sig = cnt[x.eng]
        engs = ["pe", "act", "dve", "pool", "sp"]
        per = {e: [o for o in ops if o.eng == e] for e in engs}
        with contextlib.ExitStack() as ctx:
            esem = {e: ctx.enter_context(nc.semaphore("s_" + e)) for e in COMPUTE}
            dsem = {}
            for e in self.dma_count:
                dsem[e] = [ctx.enter_context(nc.semaphore("d_%s%d" % (e, i))) for i in range(NDMASEM)]
            block = ctx.enter_context(nc.Block())

            def run(e, eng):
                waited = {}

                def wait_tok(key, sem, val):
                    if waited.get(key, 0) >= val:
                        return
                    waited[key] = val
                    eng.wait_ge(sem, val)

                for y in per[e]:
                    for xid in sorted(y.deps):
                        x = ops[xid]
                        if x.is_dma:
                            wait_tok(("d", x.eng, x.dsem), dsem[x.eng][x.dsem], x.dtarget)
                        else:
                            wait_tok(("e", x.eng), esem[x.eng], x.sig)
                    if y.is_dma:
                        if y.prev_dma is not None:
                            p = ops[y.prev_dma]
                            wait_tok(("d", p.eng, p.dsem), dsem[p.eng][p.dsem], p.dtarget)
                        y.fn(eng).then_inc(dsem[e][y.dsem], 16)
                    else:
                        ins = y.fn(eng)
                        if y.need_sig:
                            ins.then_inc(esem[e], 1)
                if e == "sp":
                    for q, n in self.dma_count.items():
                        for i in range(NDMASEM):
                            k = (n - i + NDMASEM - 1) // NDMASEM
                            if k > 0:
                                eng.wait_ge(dsem[q][i], 16 * k)

            @block.tensor
            def _(eng):
                run("pe", eng)

            @block.scalar
            def _(eng):
                run("act", eng)

            @block.vector
            def _(eng):
                run("dve", eng)

            @block.gpsimd
            def _(eng):
                run("pool", eng)

            @block.sync
            def _(eng):
                run("sp", eng)


D = 2048
NKC = 16
H_A = 8
H_B = 4
DFF = 8192
EPS = 1e-6
SC_A = 128 ** -0.5
N_POOL_ROWS = 2560 * 128
C_QA, C_KA, C_VA, C_FA, C_QB, C_KB, C_VB, C_RB, C_GL = 0, 1024, 2048, 3072, 3080, 3592, 4104, 5128, 6152
IN_COLS = 6168
STOP = [99]


def bc_last(ap, n):
    return bass.AP(ap.tensor, ap.offset, [list(x) for x in ap.ap] + [[0, n]])


def bc_mid(ap, n):
    a = [list(x) for x in ap.ap]
    return bass.AP(ap.tensor, ap.offset, [a[0], [0, n]] + a[1:])


def build(nc, with_sample=True, n_ptiles=16):
    dt = nc.dram_tensor
    xp = dt("xp", [2048, D], F32, kind="ExternalInput").ap()
    xs = dt("xs", [64, D], F32, kind="ExternalInput").ap()
    ppi = dt("pp", [2048, 256], F32, kind="ExternalInput").ap()
    psi = dt("ps", [64, 256], F32, kind="ExternalInput").ap()
    ck = dt("ck", [N_POOL_ROWS, 1024], F32, kind="ExternalInput").ap()
    cv = dt("cv", [N_POOL_ROWS, 1024], F32, kind="ExternalInput").ap()
    clf = dt("clf", [N_POOL_ROWS, 8], F32, kind="ExternalInput").ap()
    sgl = dt("sgl", [16, 4, 128, 256], F32, kind="ExternalInput").ap()
    ptab = dt("ptab", [1, 256], I32, kind="ExternalInput").ap()
    g_mix = dt("g_mix", [D], F32, kind="ExternalInput").ap()
    w_in = dt("w_in", [D, IN_COLS], F32, kind="ExternalInput").ap()
    b_f = dt("b_f", [1, 8], F32, kind="ExternalInput").ap()
    w_gk2 = dt("w_gk2", [16, 512], F32, kind="ExternalInput").ap()
    b_gk = dt("b_gk", [512], F32, kind="ExternalInput").ap()
    g_gla = dt("g_gla", [256], F32, kind="ExternalInput").ap()
    w_out = dt("w_out", [D, D], F32, kind="ExternalInput").ap()
    g_mlp = dt("g_mlp", [D], F32, kind="ExternalInput").ap()
    w_up = dt("w_up", [D, DFF], F32, kind="ExternalInput").ap()
    w_down = dt("w_down", [DFF, D], F32, kind="ExternalInput").ap()
    w_ple = dt("w_ple", [256, D], F32, kind="ExternalInput").ap()
    g_ple = dt("g_ple", [1, D], F32, kind="ExternalInput").ap()
    g_pg = dt("g_pg", [D], F32, kind="ExternalInput").ap()
    w_pg = dt("w_pg", [D, D], F32, kind="ExternalInput").ap()
    g_fin = dt("g_fin", [1, D], F32, kind="ExternalInput").ap()

    yp = dt("yp", [2048, D], F32, kind="ExternalOutput").ap()
    ys = dt("ys", [64, D], F32, kind="ExternalOutput").ap()
    kpo = dt("kpo", [2048, 1024], F32, kind="ExternalOutput").ap()
    vpo = dt("vpo", [2048, 1024], F32, kind="ExternalOutput").ap()
    fpo = dt("fpo", [2048, 8], F32, kind="ExternalOutput").ap()
    spo = dt("spo", [4, 128, 256], F32, kind="ExternalOutput").ap()
    kso = dt("kso", [64, 1024], F32, kind="ExternalOutput").ap()
    vso = dt("vso", [64, 1024], F32, kind="ExternalOutput").ap()
    fso = dt("fso", [64, 8], F32, kind="ExternalOutput").ap()
    sso = dt("sso", [16, 4, 128, 256], F32, kind="ExternalOutput").ap()

    ctx = contextlib.ExitStack()
    with ctx:
        def sb(name, shape, dtype):
            return ctx.enter_context(nc.sbuf_tensor(name, shape, dtype))

        P = Prog(nc)
        A = P.add

        KT = sb("KT", [128, 8, 2048], BF16)
        Vt = sb("Vt", [128, 16, 1024], BF16)
        x_sb = sb("x_sb", [128, D], F32)
        hT = sb("hT", [128, NKC, 128], BF16)
        mixT = sb("mixT", [128, NKC, 128], BF16)
        hidT = sb("hidT", [128, 64, 128], BF16)
        wsl = [sb("wsl%d" % i, [128, NKC, 512], BF16) for i in range(3)]
        gbc = sb("gbc", [128, D], F32)
        e_sb = sb("e_sb", [128, D], F32)
        hn = sb("hn", [128, D], BF16)
        tmpfall = sb("tmpfall", [128, 4, 512], F32)
        tmpf = [tmpfall[:, i, :] for i in range(4)]
        PTb = [sb("PT%d" % i, [128, 512], BF16) for i in range(2)]
        Sst = sb("Sst", [128, 4, 256], F32)
        Sbf = sb("Sbf", [128, 4, 256], BF16)
        cK = sb("cK", [128, 16, 8], F32)
        ctot = sb("ctot", [128, 8], F32)
        biasT = sb("biasT", [128, 16, 8], F32)
        lf = sb("lf", [128, 8], F32)
        lfx = sb("lfx", [128, 8], F32)
        smallf = sb("smallf", [128, 8], F32)
        ssq = sb("ssq", [128, 8], F32)
        rstd = sb("rstd", [128, 8], F32)
        ident = sb("ident", [128, 128], BF16)
        ones_bf = sb("ones_bf", [128, 128], BF16)
        zeros_bf = sb("zeros_bf", [128, 512], BF16)
        caus_bf = sb("caus_bf", [128, 128], BF16)
        caus_f = sb("caus_f", [128, 128], F32)
        ones_f = sb("ones_f", [128, 128], F32)
        lstr_f = sb("lstr_f", [128, 128], F32)
        m128 = sb("m128", [128, 128], F32)
        m4 = sb("m4", [128, 16, 4], F32)
        gains = sb("gains", [128, 3, 16], F32)
        ggla = sb("ggla", [128, 2], F32)
        nbgk = sb("nbgk", [128, 4], F32)
        bfb = sb("bfb", [128, 8], F32)
        wsm = sb("wsm", [128, NKC, 24], BF16)
        wgk = sb("wgk", [16, 512], BF16)
        qT = sb("qT", [128, 8, 128], BF16)
        qeT = sb("qeT", [128, 4, 128], BF16)
        keT = sb("keT", [128, 4, 128], BF16)
        kdT = sb("kdT", [128, 4, 128], BF16)
        kdTok = sb("kdTok", [128, 4, 128], BF16)
        vb = sb("vb", [128, 1024], BF16)
        srT = sb("srT", [128, 8, 128], BF16)
        glT = sb("glT", [16, 128], BF16)
        pT = sb("pT", [128, 2, 128], BF16)
        p_tm = sb("p_tm", [128, 256], BF16)
        eb = sb("eb", [128, 4, 128], F32)
        enb = sb("enb", [128, 4, 128], F32)
        ATm = [sb("ATm%d" % i, [128, 128], BF16) for i in range(2)]
        idx = sb("idx", [128, 256], I32)
        idxf = sb("idxf", [128, 256], F32)
        iot = sb("iot", [128, 1], F32)
        Emat = sb("Emat", [16, 64], BF16)
        blk4_bf = sb("blk4_bf", [64, 64], BF16)
        blk4_f = sb("blk4_f", [64, 64], F32)
        selT = sb("selT", [64, 16], F32)
        LFp = [sb("LFp%d" % i, [128, 16, 8], F32) for i in range(2)]
        bias_s = sb("bias_s", [128, 16, 8], F32)
        Vflat = Vt[:, :, :].rearrange("p a b -> p (a b)")
        KTflat = KT[:, :, :].rearrange("p a b -> p (a b)")

        S0f = [e_sb[:, 0:1024].rearrange("p (h v) -> p h v", h=4), e_sb[:, 1024:2048].rearrange("p (h v) -> p h v", h=4)]
        Sout = [tmpfall[:, 0:2, :].rearrange("p a (b v) -> p (a b) v", b=2), tmpfall[:, 2:4, :].rearrange("p a (b v) -> p (a b) v", b=2)]
        S0b = [Vflat[:, 8192:9216].rearrange("p (h v) -> p h v", h=4), Vflat[:, 9216:10240].rearrange("p (h v) -> p h v", h=4)]
        VMj = [Vflat[0:64, 10240:11264], Vflat[0:64, 11264:12288]]
        Kp = [Vflat[:, 12288 + 1024 * i:13312 + 1024 * i] for i in range(4)]
        Vp = [Vflat[:, 1024 * i:1024 * (i + 1)] for i in range(4)]
        KpT = [KTflat[:, 1024 * i:1024 * (i + 1)].rearrange("p (a b) -> p a b", a=8) for i in range(4)]
        cn = sb("cn", [64, 8], F32)

        bank = [ctx.enter_context(nc.psum_tensor("bank%d" % i, [128, 512], F32)) for i in range(8)]

        def bbf(i):
            return bank[i][:, :].bitcast(BF16)

        rr = [0]

        def mmbank():
            b = rr[0] % 4
            rr[0] += 1
            return b

        def BK(b):
            return ("bank", b)

        wq = []
        wst = {"issued": 0, "used": 0}

        NGRP = 53
        wscr = nc.dram_tensor("wscr", [NGRP, 128, 8192], BF16).ap()

        def w_issue_upto(n):
            while wst["issued"] < min(n, len(wq)):
                i = wst["issued"]
                s = i % 3
                gi = i % NGRP
                src, kind = wq[i]
                if i < NGRP:
                    if kind == "ple":
                        A("pool", lambda e, s=s, src=src: e.dma_start(
                            out=wsl[s][:, 0:8, :].rearrange("p (k a) c -> p k (a c)", k=2), in_=src),
                          writes=[("w", s, 0), ("w", s, 1)], dma=True)
                        A("sp", lambda e, s=s, gi=gi: e.dma_start(out=wscr[gi, :, 0:4096], in_=wsl[s][:, 0:8, :].rearrange("p a b -> p (a b)")),
                          reads=[("w", s, 0), ("w", s, 1)], writes=[("wscr", gi)], dma=True)
                    else:
                        for g in range(2):
                            A("pool", lambda e, s=s, src=src, g=g: e.dma_start(out=wsl[s][:, 8 * g:8 * g + 8, :], in_=src[:, 8 * g:8 * g + 8, :]),
                              writes=[("w", s, g)], dma=True)
                        A("sp", lambda e, s=s, gi=gi: e.dma_start(out=wscr[gi, :, :], in_=wsl[s][:, :, :].rearrange("p a b -> p (a b)")),
                          reads=[("w", s, 0), ("w", s, 1)], writes=[("wscr", gi)], dma=True)
                elif kind == "ple":
                    A("pool", lambda e, s=s, gi=gi: e.dma_start(out=wsl[s][:, 0:8, :].rearrange("p a b -> p (a b)"), in_=wscr[gi, :, 0:4096]),
                      reads=[("wscr", gi)], writes=[("w", s, 0), ("w", s, 1)], dma=True)
                else:
                    A("pool", lambda e, s=s, gi=gi: e.dma_start(out=wsl[s][:, :, :].rearrange("p a b -> p (a b)"), in_=wscr[gi, :, :]),
                      reads=[("wscr", gi)], writes=[("w", s, 0), ("w", s, 1)], dma=True)
                wst["issued"] += 1

        def w_prologue():
            for gi in range(NGRP):
                s = gi % 3
                src, kind = wq[gi]
                if kind == "ple":
                    A("pool", lambda e, s=s, src=src: e.dma_start(
                        out=wsl[s][:, 0:8, :].rearrange("p (k a) c -> p k (a c)", k=2), in_=src),
                      writes=[("w", s, 0), ("w", s, 1)], dma=True)
                    A("sp", lambda e, s=s, gi=gi: e.dma_start(out=wscr[gi, :, 0:4096], in_=wsl[s][:, 0:8, :].rearrange("p a b -> p (a b)")),
                      reads=[("w", s, 0), ("w", s, 1)], writes=[("wscr", gi)], dma=True)
                else:
                    for g in range(2):
                        A("pool", lambda e, s=s, src=src, g=g: e.dma_start(out=wsl[s][:, 8 * g:8 * g + 8, :], in_=src[:, 8 * g:8 * g + 8, :]),
                          writes=[("w", s, g)], dma=True)
                    A("sp", lambda e, s=s, gi=gi: e.dma_start(out=wscr[gi, :, :], in_=wsl[s][:, :, :].rearrange("p a b -> p (a b)")),
                      reads=[("w", s, 0), ("w", s, 1)], writes=[("wscr", gi)], dma=True)

        def w_next():
            i = wst["used"]
            wst["used"] += 1
            w_issue_upto(i + 3)
            s = i % 3
            return wsl[s], [("w", s, 0), ("w", s, 1)]

        def wcols(w, c0):
            return w[:, c0:c0 + 512].rearrange("(kc p) c -> p kc c", p=128)

        tiles = [("p", i) for i in range(n_ptiles)] + ([("s", 0)] if with_sample else [])
        for _ in tiles:
            for c0 in [C_QB, C_KB, C_VB, C_VB + 512, C_RB, C_RB + 512, C_QA, C_QA + 512, C_KA, C_KA + 512, C_VA, C_VA + 512]:
                wq.append((wcols(w_in, c0), "std"))
            for j in range(4):
                wq.append((wcols(w_out, 512 * j), "std"))
            for j in range(16):
                wq.append((wcols(w_up, 512 * j), "std"))
            for cg in range(4):
                for fg in range(4):
                    wq.append((w_down[fg * 2048:(fg + 1) * 2048, cg * 512:(cg + 1) * 512].rearrange("(kc p) c -> p kc c", p=128), "std"))
            wq.append((w_ple.rearrange("(k p) c -> p k c", p=128), "ple"))
            for j in range(4):
                wq.append((wcols(w_pg, 512 * j), "std"))

        A("pool", lambda e: e.memset(ones_bf[:, :], 1.0), writes=["ones_bf"])
        A("pool", lambda e: e.memset(zeros_bf[:, :], 0.0), writes=["zeros_bf"])
        A("pool", lambda e: e.memset(ones_f[:, :], 1.0), writes=["ones_f"])
        A("pool", lambda e: e.affine_select(out=ident[:, :], in_=ones_bf[:, :], pattern=[[1, 128]], compare_op=ALU.is_equal,
                                            fill=0.0, base=0, channel_multiplier=-1), reads=["ones_bf"], writes=["ident"])
        A("pool", lambda e: e.affine_select(out=caus_bf[:, :], in_=ones_bf[:, :], pattern=[[1, 128]], compare_op=ALU.is_ge,
                                            fill=0.0, base=0, channel_multiplier=-1), reads=["ones_bf"], writes=["caus_bf"])
        A("pool", lambda e: e.affine_select(out=caus_f[:, :], in_=ones_f[:, :], pattern=[[1, 128]], compare_op=ALU.is_ge,
                                            fill=0.0, base=0, channel_multiplier=-1), reads=["ones_f"], writes=["caus_f"])
        A("pool", lambda e: e.affine_select(out=lstr_f[:, :], in_=ones_f[:, :], pattern=[[-1, 128]], compare_op=ALU.is_gt,
                                            fill=0.0, base=0, channel_multiplier=1), reads=["ones_f"], writes=["lstr_f"])
        A("pool", lambda e: e.memset(m128[:, :], 1.0), writes=["m128"])
        A("pool", lambda e: e.memset(m128[:, 0:1], 0.0), writes=["m128"])
        A("pool", lambda e: e.memset(m4[:, :, :], 1.0), writes=["m4"])
        A("pool", lambda e: e.memset(m4[:, :, 0:1], 0.0), writes=["m4"])
        A("pool", lambda e: e.memset(ctot[:, :], 0.0), writes=["ctot"])
        for gi, g in enumerate((g_mix, g_mlp, g_pg)):
            A("sp", lambda e, gi=gi, g=g: e.dma_start(out=gains[:, gi, :], in_=g.rearrange("(kc p) -> p kc", p=128),
                                                      allow_slow_non_contiguous=True), writes=["gains"], dma=True)
        A("sp", lambda e: e.dma_start(out=ggla[:, :], in_=g_gla.rearrange("(c p) -> p c", p=128), allow_slow_non_contiguous=True),
          writes=["ggla"], dma=True)
        A("sp", lambda e: e.dma_start(out=nbgk[:, :], in_=b_gk.rearrange("(h p) -> p h", p=128), allow_slow_non_contiguous=True),
          writes=["nbgk"], dma=True)
        A("dve", lambda e: e.tensor_scalar(out=nbgk[:, :], in0=nbgk[:, :], scalar1=-1.0, scalar2=None, op0=ALU.mult),
          reads=["nbgk"], writes=["nbgk"])
        A("sp", lambda e: e.dma_start(out=bfb[:, :], in_=b_f.partition_broadcast(128)), writes=["bfb"], dma=True)
        A("pool", lambda e: e.dma_start(out=wsm[:, :, 0:8], in_=w_in[:, C_FA:C_FA + 8].rearrange("(kc p) c -> p kc c", p=128)),
          writes=["wsm"], dma=True)
        A("pool", lambda e: e.dma_start(out=wsm[:, :, 8:24], in_=w_in[:, C_GL:C_GL + 16].rearrange("(kc p) c -> p kc c", p=128)),
          writes=["wsm"], dma=True)
        A("pool", lambda e: e.dma_start(out=wgk[:, :], in_=w_gk2[:, :]), writes=["wgk"], dma=True)

        def rms_rstd(R, src_key, col):
            A("act", lambda e: e.activation(out=smallf[0:R, col:col + 1], in_=ssq[0:R, col:col + 1], func=AF.Ln, bias=EPS, scale=1.0 / D),
              reads=[("ssq", col)], writes=[("smallf", col)])
            A("act", lambda e: e.activation(out=rstd[0:R, col:col + 1], in_=smallf[0:R, col:col + 1], func=AF.Exp, scale=-0.5),
              reads=[("smallf", col)], writes=[("rstd", col)])

        def norm_to_hT(R, gi):
            A("dve", lambda e: e.memset(ssq[0:R, 0:1], 0.0), writes=[("ssq", 0)])
            A("act", lambda e: e.activation(out=hn[0:R, :], in_=x_sb[0:R, :], func=AF.Square, accum_out=ssq[0:R, 0:1]),
              reads=["x", ("ssq", 0)], writes=["hn", ("ssq", 0)])
            rms_rstd(R, "x", 0)
            A("dve", lambda e: e.tensor_scalar(out=hn[0:R, :], in0=x_sb[0:R, :], scalar1=rstd[0:R, 0:1], scalar2=None, op0=ALU.mult),
              reads=["x", ("rstd", 0)], writes=["hn"])
            for g in range(2):
                b = mmbank()

                def tr(e, g=g, b=b):
                    ins = None
                    for k in range(8):
                        kc = 8 * g + k
                        ins = e.transpose(out=bbf(b)[:, k * 128:k * 128 + R], in_=hn[0:R, kc * 128:(kc + 1) * 128], identity=ident[0:R, 0:R])
                    return ins
                A("pe", tr, reads=["hn", "ident"], writes=[BK(b)])
                A("dve", lambda e, g=g, b=b: e.tensor_tensor(
                    out=hT[:, 8 * g:8 * g + 8, 0:R], in0=bbf(b).rearrange("p (k t) -> p k t", k=8)[:, :, 0:R],
                    in1=bc_last(gains[:, gi, 8 * g:8 * g + 8], R), op=ALU.mult),
                  reads=[BK(b), "gains"], writes=["hT"])

        def fm_group(wt, wk, R, src, srck, ncc=4, kcs=NKC):
            b = mmbank()

            def f(e):
                ins = None
                for cc in range(ncc):
                    for kc in range(kcs):
                        ins = e.matmul(bank[b][:, cc * 128:cc * 128 + R], lhsT=wt[:, kc, cc * 128:(cc + 1) * 128], rhs=src[:, kc, 0:R],
                                       start=(kc == 0), stop=(kc == kcs - 1))
                return ins
            A("pe", f, reads=wk + [srck], writes=[BK(b)])
            return b

        def tm_group(wt, wk, R, src, srck, kcs=NKC, ncols=512):
            b = mmbank()

            def f(e):
                ins = None
                for kc in range(kcs):
                    ins = e.matmul(bank[b][0:R, 0:ncols], lhsT=src[:, kc, 0:R], rhs=wt[:, kc, 0:ncols], start=(kc == 0), stop=(kc == kcs - 1))
                return ins
            A("pe", f, reads=wk + [srck], writes=[BK(b)])
            return b

        def fmview(b, R, n=4):
            return bank[b][:, :].rearrange("p (c t) -> p c t", c=4)[:, 0:n, 0:R]

        def log_sigmoid_neg(R, src_ap, dst_ap, bias_ap, rk, wk_):
            pass

        def tile_body(kind, ti):
            R = 128 if kind == "p" else 64
            xsrc = xp[ti * 128:(ti + 1) * 128, :] if kind == "p" else xs[:, :]
            A("sp", lambda e: e.dma_start(out=x_sb[0:R, :], in_=xsrc), writes=["x"], dma=True)
            psrc = ppi[ti * 128:(ti + 1) * 128, :] if kind == "p" else psi[:, :]
            A("pool", lambda e: e.dma_start(out=p_tm[0:R, :], in_=psrc), writes=["p_tm"], dma=True)
            norm_to_hT(R, 0)

            if STOP[0] <= 1:
                return
            b = mmbank()

            def f_fa(e, b=b):
                ins = None
                for kc in range(NKC):
                    ins = e.matmul(bank[b][0:R, 0:8], lhsT=hT[:, kc, 0:R], rhs=wsm[:, kc, 0:8], start=(kc == 0), stop=(kc == NKC - 1))
                return ins
            A("pe", f_fa, reads=["hT", "wsm"], writes=[BK(b)])
            A("dve", lambda e, b=b: e.tensor_tensor(out=lfx[0:R, :], in0=bank[b][0:R, 0:8], in1=bfb[0:R, :], op=ALU.add),
              reads=[BK(b), "bfb"], writes=["lfx"])
            A("act", lambda e: e.activation(out=lfx[0:R, :], in_=lfx[0:R, :], func=AF.Exp, scale=-1.0), reads=["lfx"], writes=["lfx"])
            A("act", lambda e: e.activation(out=lfx[0:R, :], in_=lfx[0:R, :], func=AF.Ln, bias=1.0), reads=["lfx"], writes=["lfx"])
            A("dve", lambda e: e.tensor_scalar(out=lf[0:R, :], in0=lfx[0:R, :], scalar1=-1.0, scalar2=None, op0=ALU.mult),
              reads=["lfx"], writes=["lf"])
            fdst = fpo[ti * 128:(ti + 1) * 128, :] if kind == "p" else fso[:, :]
            A("sp", lambda e: e.dma_start(out=fdst, in_=lf[0:R, :]), reads=["lf"], writes=["fout"], dma=True)

            b2 = mmbank()

            def f_gl(e):
                ins = None
                for kc in range(NKC):
                    ins = e.matmul(bank[b2][0:16, 0:R], lhsT=wsm[:, kc, 8:24], rhs=hT[:, kc, 0:R], start=(kc == 0), stop=(kc == NKC - 1))
                return ins
            A("pe", f_gl, reads=["hT", "wsm"], writes=[BK(b2)])
            A("act", lambda e: e.activation(out=glT[:, 0:R], in_=bank[b2][0:16, 0:R], func=AF.Copy), reads=[BK(b2)], writes=["glT"])
            b3 = mmbank()

            def f_la(e):
                ins = None
                for h in range(4):
                    ins = e.matmul(bank[b3][:, h * 128:h * 128 + R], lhsT=wgk[:, h * 128:(h + 1) * 128], rhs=glT[:, 0:R], start=True, stop=True)
                return ins
            A("pe", f_la, reads=["glT", "wgk"], writes=[BK(b3)])
            la = tmpf[0][:, :].rearrange("p (h t) -> p h t", h=4)
            for h in range(4):
                A("act", lambda e, h=h: e.activation(out=la[:, h, 0:R], in_=bank[b3][:, h * 128:h * 128 + R], func=AF.Exp,
                                                     bias=nbgk[:, h:h + 1], scale=-1.0),
                  reads=[BK(b3), "nbgk"], writes=[("tmpf", 0)])
                A("act", lambda e, h=h: e.activation(out=la[:, h, 0:R], in_=la[:, h, 0:R], func=AF.Ln, bias=1.0),
                  reads=[("tmpf", 0)], writes=[("tmpf", 0)])
                msk = m128[:, 0:R] if kind == "p" else m4[:, :, :].rearrange("p a b -> p (a b)")
                A("dve", lambda e, h=h, msk=msk: e.tensor_tensor_scan(out=eb[:, h, 0:R], data0=msk, data1=la[:, h, 0:R], initial=0.0,
                                                                      op0=ALU.mult, op1=ALU.add),
                  reads=[("tmpf", 0), "m128", "m4"], writes=[("eb", h)])
                A("act", lambda e, h=h: e.activation(out=enb[:, h, 0:R], in_=eb[:, h, 0:R], func=AF.Exp, scale=1.0 / 16.0),
                  reads=[("eb", h)], writes=[("enb", h)])
                A("act", lambda e, h=h: e.activation(out=eb[:, h, 0:R], in_=eb[:, h, 0:R], func=AF.Exp, scale=-1.0 / 16.0),
                  reads=[("eb", h)], writes=[("eb", h)])

            if STOP[0] <= 2:
                return
            wt, wk = w_next()
            b = fm_group(wt, wk, R, hT, "hT")
            A("dve", lambda e, b=b: e.scalar_tensor_tensor(out=qeT[:, :, 0:R], in0=fmview(b, R), scalar=SC_A, in1=eb[:, :, 0:R],
                                                           op0=ALU.mult, op1=ALU.mult),
              reads=[BK(b)] + [("eb", h) for h in range(4)], writes=["qeT"])
            if STOP[0] == 21:
                return
            wt, wk = w_next()
            b = fm_group(wt, wk, R, hT, "hT")
            A("dve", lambda e, b=b: e.tensor_tensor(out=keT[:, :, 0:R], in0=fmview(b, R), in1=enb[:, :, 0:R], op=ALU.mult),
              reads=[BK(b)] + [("enb", h) for h in range(4)], writes=["keT"])
            if kind == "p":
                A("dve", lambda e: e.tensor_tensor(out=kdT[:, :, 0:R], in0=keT[:, :, 0:R], in1=bc_last(eb[:, :, R - 1], R), op=ALU.mult),
                  reads=["keT"] + [("eb", h) for h in range(4)], writes=["kdT"])
            else:
                for h in range(4):
                    A("dve", lambda e, h=h: e.tensor_tensor(
                        out=kdT[:, h, 0:64].rearrange("p (j i) -> p j i", i=4), in0=keT[:, h, 0:64].rearrange("p (j i) -> p j i", i=4),
                        in1=bc_last(eb[:, h, 0:64].rearrange("p (j i) -> p j i", i=4)[:, :, 3], 4), op=ALU.mult),
                      reads=["keT", ("eb", h)], writes=["kdT"])
            if STOP[0] == 22:
                return
            for j in range(2):
                wt, wk = w_next()
                b = tm_group(wt, wk, R, hT, "hT")
                A("act", lambda e, b=b, j=j: e.activation(out=vb[0:R, j * 512:(j + 1) * 512], in_=bank[b][0:R, :], func=AF.Copy),
                  reads=[BK(b)], writes=["vb"])
            if STOP[0] == 23:
                return
            for j in range(2):
                wt, wk = w_next()
                b = fm_group(wt, wk, R, hT, "hT")
                t1 = tmpf[1][:, :].rearrange("p (c t) -> p c t", c=4)[:, :, 0:R]
                A("act", lambda e, b=b, t1=t1: e.activation(out=t1, in_=fmview(b, R), func=AF.Exp, scale=-1.0), reads=[BK(b)], writes=[("tmpf", 1)])
                A("dve", lambda e, t1=t1: e.tensor_scalar(out=t1, in0=t1, scalar1=1.0, scalar2=None, op0=ALU.add), reads=[("tmpf", 1)], writes=[("tmpf", 1)])
                A("dve", lambda e, t1=t1: e.reciprocal(out=t1, in_=t1), reads=[("tmpf", 1)], writes=[("tmpf", 1)])
                A("dve", lambda e, b=b, j=j, t1=t1: e.tensor_tensor(out=srT[:, 4 * j:4 * j + 4, 0:R], in0=fmview(b, R), in1=t1, op=ALU.mult),
                  reads=[BK(b), ("tmpf", 1)], writes=["srT"])
            if STOP[0] == 24:
                return
            for j in range(2):
                wt, wk = w_next()
                b = fm_group(wt, wk, R, hT, "hT")
                A("act", lambda e, b=b, j=j: e.activation(out=qT[:, 4 * j:4 * j + 4, 0:R], in_=fmview(b, R), func=AF.Copy),
                  reads=[BK(b)], writes=["qT"])
            if STOP[0] == 25:
                return
            knew = sb_knew
            for j in range(2):
                wt, wk = w_next()
                b = fm_group(wt, wk, R, hT, "hT")
                if kind == "p":
                    A("act", lambda e, b=b, j=j: e.activation(out=KT[:, 4 * j:4 * j + 4, ti * 128:(ti + 1) * 128], in_=fmview(b, R), func=AF.Copy),
                      reads=[BK(b)], writes=[("KT", ti)])
                else:
                    A("act", lambda e, b=b, j=j: e.activation(out=knew[:, 4 * j:4 * j + 4, 0:R], in_=fmview(b, R), func=AF.Copy),
                      reads=[BK(b), "fence"], writes=["knew"])
                b = tm_group(wt, wk, R, hT, "hT")
                A("dve", lambda e, b=b, j=j: e.tensor_copy(out=e_sb[0:R, j * 512:(j + 1) * 512], in_=bank[b][0:R, :]),
                  reads=[BK(b)], writes=["e_sb"])
            kdst = kpo[ti * 128:(ti + 1) * 128, :] if kind == "p" else kso[:, :]
            A("sp", lambda e: e.dma_start(out=kdst, in_=e_sb[0:R, 0:1024]), reads=["e_sb"], writes=["kout"], dma=True)
            if STOP[0] == 26:
                return
            if STOP[0] == 27:
                w_issue_upto(14)
                return
            for j in range(2):
                wt, wk = w_next()
                b = tm_group(wt, wk, R, hT, "hT")
                A("dve", lambda e, b=b, j=j: e.tensor_copy(out=e_sb[0:R, 1024 + j * 512:1024 + (j + 1) * 512], in_=bank[b][0:R, :]),
                  reads=[BK(b)], writes=["e_sb"])
                vdst_sb = Vt[0:R, ti, j * 512:(j + 1) * 512] if kind == "p" else vnew[0:R, j * 512:(j + 1) * 512]
                if STOP[0] != 28:
                    A("act", lambda e, j=j, v=vdst_sb: e.activation(out=v, in_=e_sb[0:R, 1024 + j * 512:1024 + (j + 1) * 512], func=AF.Copy),
                      reads=["e_sb"] + ([] if kind == "p" else ["fence"]), writes=[("Vt", ti) if kind == "p" else "vnew"])
            vdst = vpo[ti * 128:(ti + 1) * 128, :] if kind == "p" else vso[:, :]
            if STOP[0] != 29:
                A("sp", lambda e: e.dma_start(out=vdst, in_=e_sb[0:R, 1024:2048]), reads=["e_sb"], writes=["vout"], dma=True)
            if STOP[0] in (28, 29):
                return

            if STOP[0] <= 3:
                return
            if kind == "p":
                gla_prompt(ti)
            else:
                gla_sample()
            if STOP[0] <= 4:
                return
            if kind == "p":
                fox_prompt(ti)
            else:
                fox_sample()

            if STOP[0] <= 5:
                return
            for j in range(4):
                wt, wk = w_next()
                b = tm_group(wt, wk, R, mixT, "mixT")
                A("dve", lambda e, b=b, j=j: e.tensor_tensor(out=x_sb[0:R, j * 512:(j + 1) * 512], in0=x_sb[0:R, j * 512:(j + 1) * 512],
                                                             in1=bank[b][0:R, :], op=ALU.add),
                  reads=[BK(b), "x"], writes=["x"])
            if STOP[0] <= 6:
                return
            norm_to_hT(R, 1)
            for j in range(16):
                wt, wk = w_next()
                b = fm_group(wt, wk, R, hT, "hT")
                tr_ = tmpf[j % 2][:, :].rearrange("p (c t) -> p c t", c=4)[:, :, 0:R]
                A("act", lambda e, b=b, tr_=tr_: e.activation(out=tr_, in_=fmview(b, R), func=AF.Relu), reads=[BK(b)], writes=[("tmpf", j % 2)])
                A("dve", lambda e, j=j, tr_=tr_: e.tensor_tensor(out=hidT[:, 4 * j:4 * j + 4, 0:R], in0=tr_, in1=tr_, op=ALU.mult),
                  reads=[("tmpf", j % 2)], writes=["hidT"])
            for cg in range(4):
                b = 4 + (cg % 2)
                for fg in range(4):
                    wt, wk = w_next()

                    def f(e, wt=wt, fg=fg, b=b):
                        ins = None
                        for kc in range(NKC):
                            ins = e.matmul(bank[b][0:R, :], lhsT=hidT[:, fg * 16 + kc, 0:R], rhs=wt[:, kc, :],
                                           start=(fg == 0 and kc == 0), stop=(fg == 3 and kc == NKC - 1))
                        return ins
                    A("pe", f, reads=wk + ["hidT"], writes=[BK(b)])
                A("dve", lambda e, b=b, cg=cg: e.tensor_tensor(out=x_sb[0:R, cg * 512:(cg + 1) * 512], in0=x_sb[0:R, cg * 512:(cg + 1) * 512],
                                                               in1=bank[b][0:R, :], op=ALU.add),
                  reads=[BK(b), "x"], writes=["x"])
            if STOP[0] <= 7:
                return
            b = mmbank()

            def trp(e, b=b):
                ins = None
                for k in range(2):
                    ins = e.transpose(out=bbf(b)[:, k * 128:k * 128 + R], in_=p_tm[0:R, k * 128:(k + 1) * 128], identity=ident[0:R, 0:R])
                return ins
            A("pe", trp, reads=["p_tm", "ident"], writes=[BK(b)])
            A("act", lambda e, b=b: e.activation(out=pT[:, :, 0:R], in_=bbf(b).rearrange("p (k t) -> p k t", k=8)[:, 0:2, 0:R], func=AF.Copy),
              reads=[BK(b)], writes=["pT"])
            wt, wk = w_next()
            wple = wt[:, 0:8, :].rearrange("p (k a) c -> p k (a c)", k=2)
            for cg in range(4):
                b = mmbank()

                def f(e, b=b, cg=cg, wple=wple):
                    ins = None
                    for k in range(2):
                        ins = e.matmul(bank[b][0:R, :], lhsT=pT[:, k, 0:R], rhs=wple[:, k, cg * 512:(cg + 1) * 512], start=(k == 0), stop=(k == 1))
                    return ins
                A("pe", f, reads=wk + ["pT"], writes=[BK(b)])
                A("dve", lambda e, b=b, cg=cg: e.tensor_copy(out=e_sb[0:R, cg * 512:(cg + 1) * 512], in_=bank[b][0:R, :]),
                  reads=[BK(b)], writes=["e_sb"])
            A("dve", lambda e: e.memset(ssq[0:R, 1:2], 0.0), writes=[("ssq", 1)])
            A("act", lambda e: e.activation(out=hn[0:R, :], in_=e_sb[0:R, :], func=AF.Square, accum_out=ssq[0:R, 1:2]),
              reads=["e_sb", ("ssq", 1)], writes=["hn", ("ssq", 1)])
            rms_rstd(R, "e", 1)
            A("sp", lambda e: e.dma_start(out=gbc[:, :], in_=g_ple.partition_broadcast(128)), writes=["gbc"], dma=True)
            A("dve", lambda e: e.scalar_tensor_tensor(out=e_sb[0:R, :], in0=e_sb[0:R, :], scalar=rstd[0:R, 1:2], in1=gbc[0:R, :],
                                                      op0=ALU.mult, op1=ALU.mult),
              reads=["e_sb", ("rstd", 1), "gbc"], writes=["e_sb"])
            A("sp", lambda e: e.dma_start(out=gbc[:, :], in_=g_fin.partition_broadcast(128)), writes=["gbc"], dma=True)
            norm_to_hT(R, 2)
            for j in range(4):
                wt, wk = w_next()
                b = tm_group(wt, wk, R, hT, "hT")
                t2 = tmpf[2][0:R, :]
                A("act", lambda e, b=b, t2=t2: e.activation(out=t2, in_=bank[b][0:R, :], func=AF.Exp, scale=-1.0), reads=[BK(b)], writes=[("tmpf", 2)])
                A("dve", lambda e, t2=t2: e.tensor_scalar(out=t2, in0=t2, scalar1=1.0, scalar2=None, op0=ALU.add), reads=[("tmpf", 2)], writes=[("tmpf", 2)])
                A("dve", lambda e, t2=t2: e.reciprocal(out=t2, in_=t2), reads=[("tmpf", 2)], writes=[("tmpf", 2)])
                A("dve", lambda e, t2=t2, j=j: e.tensor_tensor(out=t2, in0=t2, in1=e_sb[0:R, j * 512:(j + 1) * 512], op=ALU.mult),
                  reads=[("tmpf", 2), "e_sb"], writes=[("tmpf", 2)])
                A("dve", lambda e, t2=t2, j=j: e.tensor_tensor(out=x_sb[0:R, j * 512:(j + 1) * 512], in0=x_sb[0:R, j * 512:(j + 1) * 512], in1=t2,
                                                               op=ALU.add),
                  reads=[("tmpf", 2), "x"], writes=["x"])
            if STOP[0] <= 8:
                return
            A("dve", lambda e: e.memset(ssq[0:R, 2:3], 0.0), writes=[("ssq", 2)])
            A("act", lambda e: e.activation(out=hn[0:R, :], in_=x_sb[0:R, :], func=AF.Square, accum_out=ssq[0:R, 2:3]),
              reads=["x", ("ssq", 2)], writes=["hn", ("ssq", 2)])
            rms_rstd(R, "x", 2)
            A("dve", lambda e: e.scalar_tensor_tensor(out=x_sb[0:R, :], in0=x_sb[0:R, :], scalar=rstd[0:R, 2:3], in1=gbc[0:R, :],
                                                      op0=ALU.mult, op1=ALU.mult),
              reads=["x", ("rstd", 2), "gbc"], writes=["x"])
            ydst = yp[ti * 128:(ti + 1) * 128, :] if kind == "p" else ys[:, :]
            A("sp", lambda e: e.dma_start(out=ydst, in_=x_sb[0:R, :]), reads=["x"], writes=["yout"], dma=True)

        def gla_out_norm(R):
            pass

        def gla_finish(R, ob):
            for c in range(2):
                A("act", lambda e, c=c: e.activation(out=hn[:, c * 512:(c + 1) * 512].rearrange("p (h t) -> p h t", h=4)[:, :, 0:R],
                                                     in_=fmview(ob[c], R), func=AF.Square),
                  reads=[BK(ob[c])], writes=["hn"])
            b = mmbank()

            def f(e, b=b):
                ins = None
                for h in range(4):
                    for c in range(2):
                        ins = e.matmul(bank[b][:, h * 128:h * 128 + R], lhsT=ones_bf[:, :], rhs=hn[:, c * 512 + h * 128:c * 512 + h * 128 + R],
                                       start=(c == 0), stop=(c == 1))
                return ins
            A("pe", f, reads=["hn", "ones_bf"], writes=[BK(b)])
            t3 = tmpf[3][:, :].rearrange("p (h t) -> p h t", h=4)[:, :, 0:R]
            A("act", lambda e, b=b: e.activation(out=t3, in_=fmview(b, R), func=AF.Ln, bias=EPS, scale=1.0 / 256.0), reads=[BK(b)], writes=[("tmpf", 3)])
            A("act", lambda e: e.activation(out=t3, in_=t3, func=AF.Exp, scale=-0.5), reads=[("tmpf", 3)], writes=[("tmpf", 3)])
            for c in range(2):
                t1 = tmpf[1][:, :].rearrange("p (h t) -> p h t", h=4)[:, :, 0:R]
                A("dve", lambda e, c=c, t1=t1: e.scalar_tensor_tensor(out=t1, in0=fmview(ob[c], R), scalar=ggla[:, c:c + 1], in1=t3,
                                                                      op0=ALU.mult, op1=ALU.mult),
                  reads=[BK(ob[c]), ("tmpf", 3), "ggla"], writes=[("tmpf", 1)])
                A("dve", lambda e, c=c, t1=t1: e.tensor_tensor(
                    out=mixT[:, 8:16, 0:R].rearrange("p (h c) t -> p h c t", c=2)[:, :, c, :], in0=t1,
                    in1=srT[:, :, 0:R].rearrange("p (h c) t -> p h c t", c=2)[:, :, c, :], op=ALU.mult),
                  reads=[("tmpf", 1), "srT"], writes=["mixT"])

        def kd_transpose(R):
            b = mmbank()

            def f(e, b=b):
                ins = None
                for h in range(4):
                    ins = e.transpose(out=bbf(b)[0:R, h * 128:(h + 1) * 128], in_=kdT[:, h, 0:R], identity=ident[:, :])
                return ins
            A("pe", f, reads=["kdT", "ident"], writes=[BK(b)])
            A("act", lambda e, b=b: e.activation(out=kdTok[0:R, :, :], in_=bbf(b)[0:R, 0:512].rearrange("p (h k) -> p h k", h=4), func=AF.Copy),
              reads=[BK(b)], writes=["kdTok"])

        def gla_prompt(ti):
            R = 128
            kd_transpose(R)
            ob = [6, 7]
            for h in range(4):
                ab = 4 + (h % 2)
                A("pe", lambda e, h=h, ab=ab: e.matmul(bank[ab][:, 0:128], lhsT=keT[:, h, :], rhs=qeT[:, h, :], start=True, stop=True),
                  reads=["keT", "qeT"], writes=[BK(ab)])
                A("dve", lambda e, h=h, ab=ab: e.tensor_tensor(out=ATm[h % 2][:, :], in0=bank[ab][:, 0:128], in1=caus_bf[:, :], op=ALU.mult),
                  reads=[BK(ab), "caus_bf"], writes=[("ATm", h % 2)])

                def fo(e, h=h):
                    ins = None
                    for c in range(2):
                        o = bank[ob[c]][:, h * 128:(h + 1) * 128]
                        if ti > 0:
                            e.matmul(o, lhsT=Sbf[:, h, c * 128:(c + 1) * 128], rhs=qeT[:, h, :], start=True, stop=False)
                        ins = e.matmul(o, lhsT=vb[:, h * 256 + c * 128:h * 256 + (c + 1) * 128], rhs=ATm[h % 2][:, :], start=(ti == 0), stop=True)
                    return ins
                A("pe", fo, reads=[("Sbf", h), "qeT", "vb", ("ATm", h % 2)], writes=[BK(6), BK(7)])
                b = mmbank()
                A("pe", lambda e, h=h, b=b: e.matmul(bank[b][:, 0:256], lhsT=kdTok[:, h, :], rhs=vb[:, h * 256:(h + 1) * 256], start=True, stop=True),
                  reads=["kdTok", "vb"], writes=[BK(b)])
                if ti == 0:
                    A("dve", lambda e, h=h, b=b: e.tensor_copy(out=Sst[:, h, :], in_=bank[b][:, 0:256]), reads=[BK(b)], writes=[("Sst", h)])
                else:
                    A("dve", lambda e, h=h, b=b: e.scalar_tensor_tensor(out=Sst[:, h, :], in0=Sst[:, h, :], scalar=eb[:, h, 127:128],
                                                                        in1=bank[b][:, 0:256], op0=ALU.mult, op1=ALU.add),
                      reads=[BK(b), ("Sst", h), ("eb", h)], writes=[("Sst", h)])
                A("act", lambda e, h=h: e.activation(out=Sbf[:, h, :], in_=Sst[:, h, :], func=AF.Copy), reads=[("Sst", h)], writes=[("Sbf", h)])
            gla_finish(R, ob)
            if ti == n_ptiles - 1:
                A("sp", lambda e: e.dma_start(out=spo.rearrange("h k v -> k h v"), in_=Sst[:, :, :]),
                  reads=[("Sst", h) for h in range(4)], writes=["spo"], dma=True)

        def fox_prompt(ti):
            R = 128
            nblk = ti + 1
            b = mmbank()
            A("pe", lambda e, b=b: e.matmul(bank[b][:, 0:8], lhsT=caus_f[:, :], rhs=lf[:, :], start=True, stop=True),
              reads=["caus_f", "lf"], writes=[BK(b)])
            A("dve", lambda e, b=b: e.tensor_tensor(out=cK[:, ti, :], in0=bank[b][:, 0:8], in1=ctot[:, :], op=ALU.add),
              reads=[BK(b), "ctot"], writes=["cK"])
            A("dve", lambda e: e.tensor_tensor(out=biasT[:, 0:nblk, :], in0=bc_mid(ctot[:, :], nblk), in1=cK[:, 0:nblk, :], op=ALU.subtract),
              reads=["cK", "ctot"], writes=["biasT"])
            b2 = mmbank()
            A("pe", lambda e, b2=b2: e.matmul(bank[b2][:, 0:8], lhsT=ones_f[:, :], rhs=lf[:, :], start=True, stop=True),
              reads=["ones_f", "lf"], writes=[BK(b2)])
            A("dve", lambda e, b2=b2: e.tensor_tensor(out=ctot[:, :], in0=ctot[:, :], in1=bank[b2][:, 0:8], op=ALU.add),
              reads=[BK(b2), "ctot"], writes=["ctot"])
            n = 0
            for h in range(8):
                for kb in range(nblk):
                    sbk = 4 + (n % 2)
                    pt = PTb[n % 2]
                    n += 1
                    A("pe", lambda e, h=h, kb=kb, sbk=sbk: e.matmul(bank[sbk][:, 0:128], lhsT=KT[:, h, kb * 128:(kb + 1) * 128], rhs=qT[:, h, :],
                                                                    start=True, stop=True),
                      reads=[("KT", kb), "qT"], writes=[BK(sbk)])
                    A("act", lambda e, h=h, kb=kb, sbk=sbk, pt=pt: e.activation(out=pt[:, 0:128], in_=bank[sbk][:, 0:128], func=AF.Exp,
                                                                                bias=biasT[:, kb, h:h + 1], scale=SC_A),
                      reads=[BK(sbk), "biasT"], writes=[("PT", id(pt))])
                    if kb == ti:
                        A("dve", lambda e, pt=pt: e.tensor_tensor(out=pt[:, 0:128], in0=pt[:, 0:128], in1=caus_bf[:, :], op=ALU.mult),
                          reads=[("PT", id(pt)), "caus_bf"], writes=[("PT", id(pt))])

                    def fpv(e, h=h, kb=kb, pt=pt):
                        e.matmul(bank[6][:, 0:128], lhsT=Vt[:, kb, h * 128:(h + 1) * 128],
                                 rhs=pt[:, 0:128], start=(kb == 0), stop=(kb == nblk - 1))
                        return e.matmul(bank[7][:, 0:128], lhsT=ones_bf[:, :], rhs=pt[:, 0:128], start=(kb == 0), stop=(kb == nblk - 1))
                    A("pe", fpv, reads=[("Vt", kb), ("PT", id(pt)), "ones_bf"], writes=[BK(6), BK(7)])
                A("dve", lambda e: e.reciprocal(out=tmpf[2][:, 0:128], in_=bank[7][:, 0:128]), reads=[BK(7)], writes=[("tmpf", 2)])
                A("dve", lambda e, h=h: e.tensor_tensor(out=mixT[:, h, :], in0=bank[6][:, 0:128], in1=tmpf[2][:, 0:128], op=ALU.mult),
                  reads=[BK(6), ("tmpf", 2)], writes=["mixT"])

        sb_knew = Vflat[:, 5120:5632].rearrange("p (a b) -> p a b", a=8)
        vnew = Vflat[0:64, 4096:5120]

        def gla_sample():
            R = 64
            kd_transpose(R)
            ob = [6, 7]
            for j in range(16):
                s = j % 2
                A("sp", lambda e, j=j, s=s: e.dma_start(out=S0f[s], in_=sgl[j].rearrange("h k v -> k h v")), reads=["fence"], writes=[("S0f", s), "e_sb"], dma=True)
                A("pool", lambda e, j=j, s=s: e.dma_start(out=S0b[s], in_=sgl[j].rearrange("h k v -> k h v")), reads=["fence"], writes=[("S0b", s)], dma=True)
                if j == 0:
                    for h in range(4):
                        ab = 4 + (h % 2)
                        A("pe", lambda e, h=h, ab=ab: e.matmul(bank[ab][0:64, 0:64], lhsT=keT[:, h, 0:64], rhs=qeT[:, h, 0:64], start=True, stop=True),
                          reads=["keT", "qeT"], writes=[BK(ab)])
                        A("dve", lambda e, h=h, ab=ab: e.tensor_tensor(out=ATs[0:64, h, :], in0=bank[ab][0:64, 0:64], in1=blk4_bf[:, :], op=ALU.mult),
                          reads=[BK(ab), "blk4_bf"], writes=["ATs"])

                    def fo(e):
                        ins = None
                        for c in range(2):
                            e.matmul(bank[ob[c]][:, :], lhsT=zeros_bf[:, 0:128], rhs=zeros_bf[:, :], start=True, stop=False)
                        for h in range(4):
                            for c in range(2):
                                ins = e.matmul(bank[ob[c]][:, h * 128:h * 128 + 64], lhsT=vb[0:64, h * 256 + c * 128:h * 256 + (c + 1) * 128],
                                               rhs=ATs[0:64, h, :], start=False, stop=False)
                        return ins
                    A("pe", fo, reads=["vb", "ATs", "zeros_bf"], writes=[BK(6), BK(7)])

                def fi(e, j=j, s=s):
                    ins = None
                    for h in range(4):
                        for c in range(2):
                            ins = e.matmul(bank[ob[c]][:, h * 128 + 4 * j:h * 128 + 4 * j + 4], lhsT=S0b[s][:, h, c * 128:(c + 1) * 128],
                                           rhs=qeT[:, h, 4 * j:4 * j + 4], start=False, stop=(j == 15 and h == 3))
                    return ins
                A("pe", fi, reads=[("S0b", s), "qeT"], writes=[BK(6), BK(7)])
                A("dve", lambda e, j=j, s=s: e.tensor_scalar(out=VMj[s], in0=vb[0:64, :], scalar1=selT[:, j:j + 1], scalar2=None, op0=ALU.mult),
                  reads=["vb", "selT", "fence"], writes=[("VMj", s)])
                for h in range(4):
                    b = mmbank()
                    A("pe", lambda e, h=h, b=b, s=s: e.matmul(bank[b][:, 0:256], lhsT=kdTok[0:64, h, :], rhs=VMj[s][:, h * 256:(h + 1) * 256],
                                                              start=True, stop=True),
                      reads=["kdTok", ("VMj", s)], writes=[BK(b)])
                    A("dve", lambda e, h=h, b=b, s=s, j=j: e.scalar_tensor_tensor(out=Sout[s][:, h, :], in0=S0f[s][:, h, :],
                                                                                  scalar=eb[:, h, 4 * j + 3:4 * j + 4], in1=bank[b][:, 0:256],
                                                                                  op0=ALU.mult, op1=ALU.add),
                      reads=[BK(b), ("S0f", s), ("eb", h), "fence"], writes=[("Sout", s), ("tmpf", 2 * s), ("tmpf", 2 * s + 1)])
                A("sp", lambda e, j=j, s=s: e.dma_start(out=sso[j].rearrange("h k v -> k h v"), in_=Sout[s]),
                  reads=[("Sout", s), ("tmpf", 2 * s), ("tmpf", 2 * s + 1)], writes=[("sso", j)], dma=True)
            gla_finish(R, ob)

        ATs = sb("ATs", [64, 4, 64], BF16)

        def fox_sample():
            R = 64
            A("sp", lambda e: e.dma_start(out=idx[:, :], in_=ptab.partition_broadcast(128)), writes=["idx"], dma=True)
            A("pool", lambda e: e.iota(iot[:, :], [[0, 1]], base=0, channel_multiplier=1, allow_small_or_imprecise_dtypes=True), writes=["iot"])
            A("dve", lambda e: e.tensor_copy(out=idxf[:, :], in_=idx[:, :]), reads=["idx"], writes=["idxf"])
            A("dve", lambda e: e.tensor_scalar(out=idxf[:, :], in0=idxf[:, :], scalar1=128.0, scalar2=iot[:, 0:1], op0=ALU.mult, op1=ALU.add),
              reads=["idxf", "iot"], writes=["idxf"])
            A("dve", lambda e: e.tensor_copy(out=idx[:, :], in_=idxf[:, :]), reads=["idxf"], writes=["idx"])
            A("pool", lambda e: e.affine_select(out=Emat[:, :], in_=ones_bf[0:16, 0:64], pattern=[[1, 64]], compare_op=ALU.is_ge, fill=0.0,
                                                base=0, channel_multiplier=-4), reads=["ones_bf"], writes=["Emat"])
            A("pool", lambda e: e.affine_select(out=Emat[:, :], in_=Emat[:, :], pattern=[[-1, 64]], compare_op=ALU.is_ge, fill=0.0,
                                                base=3, channel_multiplier=4), reads=["Emat"], writes=["Emat"])
            A("pool", lambda e: e.affine_select(out=selT[:, :], in_=ones_f[0:64, 0:16], pattern=[[-4, 16]], compare_op=ALU.is_ge, fill=0.0,
                                                base=0, channel_multiplier=1), reads=["ones_f"], writes=["selT"])
            A("pool", lambda e: e.affine_select(out=selT[:, :], in_=selT[:, :], pattern=[[4, 16]], compare_op=ALU.is_ge, fill=0.0,
                                                base=3, channel_multiplier=-1), reads=["selT"], writes=["selT"])
            b = mmbank()
            A("pe", lambda e, b=b: e.matmul(bank[b][0:64, 0:64], lhsT=Emat[:, :], rhs=Emat[:, :], start=True, stop=True), reads=["Emat"], writes=[BK(b)])
            A("dve", lambda e, b=b: e.tensor_tensor(out=blk4_bf[:, :], in0=bank[b][0:64, 0:64], in1=caus_bf[0:64, 0:64], op=ALU.mult),
              reads=[BK(b), "caus_bf"], writes=["blk4_bf"])
            A("dve", lambda e, b=b: e.tensor_tensor(out=blk4_f[:, :], in0=bank[b][0:64, 0:64], in1=caus_f[0:64, 0:64], op=ALU.mult),
              reads=[BK(b), "caus_f"], writes=["blk4_f"])

        fox_sample_masks = fox_sample

        def fox_sample_main():
            R = 64
            b = mmbank()
            A("pe", lambda e, b=b: e.matmul(bank[b][0:64, 0:8], lhsT=blk4_f[:, :], rhs=lf[0:64, :], start=True, stop=True),
              reads=["blk4_f", "lf"], writes=[BK(b)])
            A("dve", lambda e, b=b: e.tensor_scalar(out=cn[:, :], in0=bank[b][0:64, 0:8], scalar1=-1.0, scalar2=None, op0=ALU.mult),
              reads=[BK(b)], writes=["cn"])
            for h in range(8):
                sbk = 4 + (h % 2)
                pt = PTb[h % 2]
                A("pe", lambda e, h=h, sbk=sbk: e.matmul(bank[sbk][0:64, 0:64], lhsT=sb_knew[:, h, :], rhs=qT[:, h, 0:64], start=True, stop=True),
                  reads=["knew", "qT"], writes=[BK(sbk)])
                A("act", lambda e, h=h, sbk=sbk, pt=pt: e.activation(out=pt[0:64, 0:64], in_=bank[sbk][0:64, 0:64], func=AF.Exp,
                                                                     bias=cn[:, h:h + 1], scale=SC_A),
                  reads=[BK(sbk), "cn"], writes=[("PT", id(pt))])
                A("dve", lambda e, pt=pt: e.tensor_tensor(out=pt[0:64, 0:64], in0=pt[0:64, 0:64], in1=blk4_bf[:, :], op=ALU.mult),
                  reads=[("PT", id(pt)), "blk4_bf"], writes=[("PT", id(pt))])

                def fpv(e, h=h, pt=pt):
                    if h == 0:
                        for bb_ in (6, 7):
                            e.matmul(bank[bb_][:, :], lhsT=zeros_bf[:, 0:128], rhs=zeros_bf[:, :], start=True, stop=False)
                    e.matmul(bank[6][:, h * 64:(h + 1) * 64], lhsT=vnew[0:64, h * 128:(h + 1) * 128], rhs=pt[0:64, 0:64], start=False, stop=False)
                    return e.matmul(bank[7][:, h * 64:(h + 1) * 64], lhsT=ones_bf[0:64, :], rhs=pt[0:64, 0:64], start=False, stop=False)
                A("pe", fpv, reads=["vnew", ("PT", id(pt)), "ones_bf", "zeros_bf"], writes=[BK(6), BK(7)])
            for j in range(16):
                ls = j % 2
                for pg in range(16):
                    col = j * 16 + pg
                    A("pool", lambda e, ls=ls, pg=pg, col=col: e.indirect_dma_start(
                        out=LFp[ls][:, pg, :], out_offset=None, in_=clf,
                        in_offset=bass.IndirectOffsetOnAxis(ap=idx[:, col:col + 1], axis=0)),
                      reads=["idx"], writes=[("LFp", ls)], dma=True)
                b = mmbank()

                def fb(e, b=b, ls=ls):
                    lfv = LFp[ls][:, :, :].rearrange("p a b -> p (a b)")
                    ins = e.matmul(bank[b][:, 0:128], lhsT=lstr_f[:, :], rhs=lfv, start=True, stop=False)
                    for k in range(1, 16):
                        ins = e.matmul(bank[b][:, 0:128 - 8 * k], lhsT=ones_f[:, :], rhs=lfv[:, 8 * k:128], start=False, stop=(k == 15))
                    return ins
                A("pe", fb, reads=[("LFp", ls), "lstr_f", "ones_f"], writes=[BK(b)])
                A("act", lambda e, b=b: e.activation(out=bias_s[:, :, :].rearrange("p a b -> p (a b)"), in_=bank[b][:, 0:128], func=AF.Copy),
                  reads=[BK(b)], writes=["bias_s"])
                for pg in range(16):
                    col = j * 16 + pg
                    s = col % 4
                    sbk = 4 + (col % 2)
                    A("pool", lambda e, s=s, col=col: e.indirect_dma_start(
                        out=Kp[s], out_offset=None, in_=ck, in_offset=bass.IndirectOffsetOnAxis(ap=idx[:, col:col + 1], axis=0)),
                      reads=["idx", "fence"], writes=[("Kp", s)], dma=True)
                    A("pool", lambda e, s=s, col=col: e.indirect_dma_start(
                        out=Vp[s], out_offset=None, in_=cv, in_offset=bass.IndirectOffsetOnAxis(ap=idx[:, col:col + 1], axis=0)),
                      reads=["idx", "fence"], writes=[("Vp", s)], dma=True)
                    tb = mmbank()

                    def ftr(e, s=s, tb=tb):
                        ins = None
                        for h in range(8):
                            ins = e.transpose(out=bbf(tb)[:, h * 128:(h + 1) * 128], in_=Kp[s][:, h * 128:(h + 1) * 128], identity=ident[:, :])
                        return ins
                    A("pe", ftr, reads=[("Kp", s), "ident"], writes=[BK(tb)])
                    if col % 2 == 0:
                        A("act", lambda e, s=s, tb=tb: e.activation(out=KpT[s].rearrange("p a b -> p (a b)"), in_=bbf(tb), func=AF.Copy),
                          reads=[BK(tb), "fence"], writes=[("KpT", s)])
                    else:
                        A("dve", lambda e, s=s, tb=tb: e.tensor_copy(out=KpT[s].rearrange("p a b -> p (a b)"), in_=bbf(tb)),
                          reads=[BK(tb), "fence"], writes=[("KpT", s)])

                    def fsc(e, s=s, pg=pg, j=j, sbk=sbk):
                        ins = None
                        for h in range(8):
                            o = bank[sbk][:, pg * 32 + h * 4:pg * 32 + h * 4 + 4]
                            ins = e.matmul(o, lhsT=KpT[s][:, h, :], rhs=qT[:, h, 4 * j:4 * j + 4], start=True, stop=True)
                        return ins
                    A("pe", fsc, reads=[("KpT", s), "qT"], writes=[BK(sbk)])
                    tq = tmpf[col % 2]
                    A("dve", lambda e, pg=pg, sbk=sbk, tq=tq: e.scalar_tensor_tensor(
                        out=tq[:, 0:32].rearrange("p (h q) -> p h q", q=4), in0=bank[sbk][:, pg * 32:(pg + 1) * 32].rearrange("p (h q) -> p h q", q=4),
                        scalar=SC_A, in1=bc_last(bias_s[:, pg, :], 4), op0=ALU.mult, op1=ALU.add),
                      reads=[BK(sbk), "bias_s"], writes=[("tmpf", col % 2)])
                    pt = PTb[col % 2]
                    A("act", lambda e, tq=tq, pt=pt: e.activation(out=pt[:, 0:32], in_=tq[:, 0:32], func=AF.Exp),
                      reads=[("tmpf", col % 2)], writes=[("PT", id(pt))])

                    def fpv(e, s=s, j=j, pt=pt, last=(pg == 15 and j == 15)):
                        for h in range(8):
                            e.matmul(bank[6][:, h * 64 + 4 * j:h * 64 + 4 * j + 4], lhsT=Vp[s][:, h * 128:(h + 1) * 128], rhs=pt[:, h * 4:h * 4 + 4],
                                     start=False, stop=(last and h == 7))
                        ins = None
                        for h in range(8):
                            ins = e.matmul(bank[7][:, h * 64 + 4 * j:h * 64 + 4 * j + 4], lhsT=ones_bf[:, :], rhs=pt[:, h * 4:h * 4 + 4],
                                           start=False, stop=(last and h == 7))
                        return ins
                    A("pe", fpv, reads=[("Vp", s), ("PT", id(pt)), "ones_bf"], writes=[BK(6), BK(7)])
            A("dve", lambda e: e.reciprocal(out=tmpf[2][:, :], in_=bank[7][:, :]), reads=[BK(7)], writes=[("tmpf", 2)])
            A("dve", lambda e: e.tensor_tensor(out=mixT[:, 0:8, 0:64], in0=bank[6][:, :].rearrange("p (h t) -> p h t", h=8),
                                               in1=tmpf[2][:, :].rearrange("p (h t) -> p h t", h=8), op=ALU.mult),
              reads=[BK(6), ("tmpf", 2)], writes=["mixT"])

        if with_sample:
            fox_sample_masks()
        fox_sample = fox_sample_main
        assert len(wq) == NGRP * len(tiles)
        for kind, ti in tiles:
            if kind == "s":
                A("pool", lambda e: e.memset(smallf[:, 7:8], 0.0),
                  writes=["fence"] + [("Vt", k) for k in range(16)] + [("KT", k) for k in range(16)])
            tile_body(kind, ti)
        print('SBUF bytes remaining', nc.sbuf_bytes_remaining)
        P.emit()
    return nc


_CACHE = {}


def kernel(x_prompt, x_sample, cache_k, cache_v, cache_logf, state_gla, page_table, p_prompt, p_sample,
           g_mix, w_in, b_f, w_gk2, b_gk, g_gla_out, w_out, g_mlp, w_up, w_down, w_ple, g_ple, g_ple_gate,
           w_ple_gate, g_final):
    f32 = lambda a: np.ascontiguousarray(np.asarray(a, dtype=np.float32))
    n = 8
    if "nc" not in _CACHE:
        nc = bass.Bass("TRN2", target_bir_lowering=False)
        build(nc)
        _CACHE["nc"] = nc
    nc = _CACHE["nc"]
    ck = f32(cache_k).reshape(N_POOL_ROWS, 1024)
    cv = f32(cache_v).reshape(N_POOL_ROWS, 1024)
    clf = f32(cache_logf).reshape(N_POOL_ROWS, 8)
    shared = {
        "ck": ck, "cv": cv, "clf": clf,
        "g_mix": f32(g_mix).reshape(D), "w_in": f32(w_in).reshape(D, IN_COLS), "b_f": f32(b_f).reshape(1, 8),
        "w_gk2": f32(w_gk2).reshape(16, 512), "b_gk": f32(b_gk).reshape(512), "g_gla": f32(g_gla_out).reshape(256),
        "w_out": f32(w_out).reshape(D, D), "g_mlp": f32(g_mlp).reshape(D), "w_up": f32(w_up).reshape(D, DFF),
        "w_down": f32(w_down).reshape(DFF, D), "w_ple": f32(w_ple).reshape(256, D), "g_ple": f32(g_ple).reshape(1, D),
        "g_pg": f32(g_ple_gate).reshape(D), "w_pg": f32(w_ple_gate).reshape(D, D), "g_fin": f32(g_final).reshape(1, D),
    }
    xp = f32(x_prompt)
    xs = f32(x_sample)
    pp = f32(p_prompt)[0]
    ps_ = f32(p_sample)[0]
    sg = f32(state_gla)[0]
    pt = np.ascontiguousarray(np.asarray(page_table, dtype=np.int32))
    in_maps = []
    for c in range(n):
        m = dict(shared)
        m["xp"] = xp[c]
        m["xs"] = np.ascontiguousarray(xs[16 * c:16 * c + 16].reshape(64, D))
        m["pp"] = pp[c]
        m["ps"] = np.ascontiguousarray(ps_[16 * c:16 * c + 16].reshape(64, 256))
        m["sgl"] = np.ascontiguousarray(sg[16 * c:16 * c + 16])
        m["ptab"] = np.ascontiguousarray(pt[16 * c:16 * c + 16].reshape(1, 256))
        in_maps.append(m)
    res = run_bass_kernel_spmd(nc, in_maps, core_ids=list(range(n)))
    r = res.results
    g = lambda k: [np.asarray(r[c][k], dtype=np.float32) for c in range(n)]
    y_prompt = np.stack(g("yp"), 0)
    y_sample = np.concatenate([a.reshape(16, 4, D) for a in g("ys")], 0)
    k_prompt = np.stack([a.reshape(2048, 8, 128) for a in g("kpo")], 0)[None]
    v_prompt = np.stack([a.reshape(2048, 8, 128) for a in g("vpo")], 0)[None]
    f_prompt = np.stack(g("fpo"), 0)[None]
    s_prompt = np.stack(g("spo"), 0)[None]
    k_sample = np.concatenate([a.reshape(16, 4, 8, 128) for a in g("kso")], 0)[None]
    v_sample = np.concatenate([a.reshape(16, 4, 8, 128) for a in g("vso")], 0)[None]
    f_sample = np.concatenate([a.reshape(16, 4, 8) for a in g("fso")], 0)[None]
    s_sample = np.concatenate(g("sso"), 0)[None]
    return (y_prompt, y_sample, k_prompt, v_prompt, f_prompt, s_prompt, k_sample, v_sample, f_sample, s_sample)
```
